# Optimizing a Trainium2 kernel written in Bass

```python
import jax, jax.numpy as jnp
from jax import lax
import numpy as np

D_MODEL = 1024
BATCH = 8
SEQ = 2048
DEPTH = 4

CHUNK = 64
N_A = DEPTH // 2
N_B = DEPTH - N_A
GM_BLOCK = 128
A_WIDTH = 2 * D_MODEL
A_GROUPS = 8
A_GROUP_CH = A_WIDTH // A_GROUPS
B_HEADS = 8
B_HEAD_DIM = D_MODEL // B_HEADS
B_WIDTH = B_HEADS * B_HEAD_DIM
Q_BLOCK = 128
PLE_DIM = 256
EPS = 1e-6

kernel_name = "yoco_gmlp_stickbreaking_trunk"


def rmsnorm(x, g):
    xf = x.astype(jnp.float32)
    y = xf * lax.rsqrt(jnp.mean(xf * xf, axis=-1, keepdims=True) + EPS)
    return (y * g.astype(jnp.float32)).astype(x.dtype)


def layernorm(x, g, b):
    xf = x.astype(jnp.float32)
    mu = jnp.mean(xf, axis=-1, keepdims=True)
    var = jnp.mean(jnp.square(xf - mu), axis=-1, keepdims=True)
    y = (xf - mu) * lax.rsqrt(var + EPS)
    return (y * g.astype(jnp.float32) + b.astype(jnp.float32)).astype(x.dtype)


def gmlp_mixer(hn, w_in, ln_g, ln_b, w_s, b_s, w_out):
    B, S, _ = hn.shape
    z = hn @ w_in
    u, v, gate = jnp.split(z, 3, axis=-1)
    u = jax.nn.gelu(u, approximate=False)
    v = layernorm(jax.nn.gelu(v, approximate=False), ln_g, ln_b)
    cid = jnp.arange(GM_BLOCK) // CHUNK
    mask = cid[None, :] <= cid[:, None]
    w = jnp.where(mask[None], w_s, jnp.zeros_like(w_s))
    vb = v.reshape(B, S // GM_BLOCK, GM_BLOCK, A_GROUPS, A_GROUP_CH)
    sv = jnp.einsum('gts,bnsgc->bntgc', w, vb) + b_s.T[None, None, :, :, None]
    sv = sv.reshape(B, S, A_WIDTH)
    return (u * sv * jax.nn.silu(gate)) @ w_out


def stick_breaking(q, k, v):
    S = q.shape[1]
    scale = 1.0 / np.sqrt(B_HEAD_DIM)
    outs = []
    for i in range(S // Q_BLOCK):
        q0 = i * Q_BLOCK
        k_end = q0 + Q_BLOCK
        qb = q[:, q0:k_end]
        kb = k[:, :k_end]
        vb = v[:, :k_end]
        z = jnp.einsum('bthd,bshd->bhts', qb, kb).astype(jnp.float32) * scale
        t_pos = q0 + jnp.arange(Q_BLOCK)
        s_pos = jnp.arange(k_end)
        causal = s_pos[None, :] < t_pos[:, None]
        log_beta = jnp.where(causal, jax.nn.log_sigmoid(z), -jnp.inf)
        log_1m = jnp.where(causal, jax.nn.log_sigmoid(-z), 0.0)
        suffix = lax.cumsum(log_1m, axis=3, reverse=True) - log_1m
        a = jnp.exp(log_beta + suffix).astype(vb.dtype)
        outs.append(jnp.einsum('bhts,bshd->bthd', a, vb))
    return jnp.concatenate(outs, axis=1)


def setup_inputs(seed: int = 0) -> dict:
    key = jax.random.key(seed)
    ks = jax.random.split(key, 20)
    f32 = jnp.float32
    nrm = lambda k, shape, s: jax.random.normal(k, shape, f32) * s
    D = D_MODEL
    return {
        "x": nrm(ks[0], (BATCH, SEQ, D), 1.0),
        "p": nrm(ks[1], (DEPTH, BATCH, SEQ, PLE_DIM), 1.0),
        "norm_g": 1.0 + nrm(ks[2], (DEPTH, D), 0.02),
        "a_w_in": nrm(ks[3], (N_A, D, 3 * A_WIDTH), D ** -0.5),
        "a_ln_g": 1.0 + nrm(ks[4], (N_A, A_WIDTH), 0.02),
        "a_ln_b": nrm(ks[5], (N_A, A_WIDTH), 0.02),
        "a_w_s": nrm(ks[6], (N_A, A_GROUPS, GM_BLOCK, GM_BLOCK), 0.5 * GM_BLOCK ** -0.5),
        "a_b_s": 1.0 + nrm(ks[7], (N_A, A_GROUPS, GM_BLOCK), 0.02),
        "a_w_out": nrm(ks[8], (N_A, A_WIDTH, D), A_WIDTH ** -0.5),
        "kv_norm_g": 1.0 + nrm(ks[9], (D,), 0.02),
        "w_kv": nrm(ks[10], (D, 2 * B_WIDTH), D ** -0.5),
        "b_w_in": nrm(ks[11], (N_B, D, 2 * B_WIDTH), D ** -0.5),
        "b_w_out": nrm(ks[12], (N_B, B_WIDTH, D), B_WIDTH ** -0.5),
        "ple_w": nrm(ks[13], (DEPTH, PLE_DIM, D), 0.5 * PLE_DIM ** -0.5),
        "ple_gate_w": nrm(ks[14], (DEPTH, D, D), D ** -0.5),
        "final_g": 1.0 + nrm(ks[15], (D,), 0.02),
    }


def reference(x, p, norm_g, a_w_in, a_ln_g, a_ln_b, a_w_s, a_b_s, a_w_out,
              kv_norm_g, w_kv, b_w_in, b_w_out, ple_w, ple_gate_w, final_g):
    B, S, _ = x.shape
    h = x
    k_sh = None
    v_sh = None
    for i in range(DEPTH):
        hn = rmsnorm(h, norm_g[i])
        if i < N_A:
            mix = gmlp_mixer(hn, a_w_in[i], a_ln_g[i], a_ln_b[i],
                             a_w_s[i], a_b_s[i], a_w_out[i])
        else:
            if k_sh is None:
                kv = rmsnorm(h, kv_norm_g) @ w_kv
                k_sh, v_sh = jnp.split(kv, 2, axis=-1)
                k_sh = k_sh.reshape(B, S, B_HEADS, B_HEAD_DIM)
                v_sh = v_sh.reshape(B, S, B_HEADS, B_HEAD_DIM)
            j = i - N_A
            qg = hn @ b_w_in[j]
            q, gate = jnp.split(qg, 2, axis=-1)
            q = q.reshape(B, S, B_HEADS, B_HEAD_DIM)
            o = stick_breaking(q, k_sh, v_sh).reshape(B, S, B_WIDTH)
            mix = (o * jax.nn.silu(gate)) @ b_w_out[j]
        h = h + mix
        h = h + jax.nn.sigmoid(h @ ple_gate_w[i]) * (p[i] @ ple_w[i])
    return rmsnorm(h, final_g)
```

```python
import numpy as np
OPT = "sp"
from contextlib import ExitStack
import concourse.bass as bass
import concourse.mybir as mybir
from concourse.bass_utils import run_bass_kernel_spmd

F32 = mybir.dt.float32
BF16 = mybir.dt.bfloat16
AF = mybir.ActivationFunctionType
ALU = mybir.AluOpType

S = 2048
D = 1024
HALF = 1024
EPS = 1e-6
NSLOT = 4
N_WARM = 1
FAST_RECIP = True
SEM_CHUNK = 30000


class Tok:
    __slots__ = ("eng", "sem", "val")

    def __init__(self, eng):
        self.eng = eng
        self.sem = None
        self.val = None


class Reg:
    __slots__ = ("w", "rs", "name")

    def __init__(self, name=""):
        self.w = None
        self.rs = {}
        self.name = name


class Sched:
    def __init__(self, nc, es):
        self.nc = nc
        self.es = es
        self.E = {"pe": nc.tensor, "act": nc.scalar, "dve": nc.vector, "pool": nc.gpsimd, "sp": nc.sync}
        self.sems = {}
        self.cnt = {}
        self.pending = {k: [] for k in self.E}
        self.seen = {k: {} for k in self.E}
        self.nsem = 0
        for k in self.E:
            self._newsem(k)
        self.dsem = {}

    def _newsem(self, k):
        self.nsem += 1
        self.sems[k] = self.nc.alloc_semaphore(name=f"s_{k}_{self.nsem}")
        self.cnt[k] = 0

    def _wait(self, eng, toks):
        best = {}
        for t in toks:
            if t is None:
                continue
            assert t.sem is not None, f"unresolved token from {t.eng} needed by {eng}"
            key = t.sem
            if key not in best or best[key].val < t.val:
                best[key] = t
        for key, t in best.items():
            if self.seen[eng].get(key, 0) >= t.val:
                continue
            self.E[eng].wait_ge(t.sem, t.val)
            self.seen[eng][key] = t.val

    def _deps(self, eng, reads, writes):
        deps = []
        for r in reads:
            if r.w is not None:
                if r.w.eng == eng and eng == "pe":
                    continue
                deps.append(r.w)
        for r in writes:
            if r.w is not None and not (r.w.eng == eng and eng == "pe"):
                deps.append(r.w)
            for e2, t in r.rs.items():
                if not (e2 == eng and eng == "pe"):
                    deps.append(t)
        return deps

    def _mark(self, tok, reads, writes):
        for r in reads:
            r.rs[tok.eng] = tok
        for r in writes:
            r.w = tok
            r.rs = {}

    def op(self, eng, fn, reads=(), writes=(), signal=True):
        self._wait(eng, self._deps(eng, reads, writes))
        inst = fn(self.E[eng])
        tok = Tok(eng)
        if signal:
            if self.cnt[eng] >= SEM_CHUNK:
                self._newsem(eng)
            self.cnt[eng] += 1
            inst.then_inc(self.sems[eng], 1)
            tok.sem = self.sems[eng]
            tok.val = self.cnt[eng]
            for p in self.pending[eng]:
                p.sem = tok.sem
                p.val = tok.val
            self.pending[eng] = []
        else:
            self.pending[eng].append(tok)
        self._mark(tok, reads, writes)
        return tok

    def dma(self, q, out, in_, semname, reads=(), writes=()):
        self._wait(q, self._deps("dma:" + semname, reads, writes))
        if semname not in self.dsem:
            self.dsem[semname] = [self.nc.alloc_semaphore(name="d_" + semname), 0]
        ent = self.dsem[semname]
        ent[1] += 16
        self.E[q].dma_start(out=out, in_=in_).then_inc(ent[0], 16)
        tok = Tok("dma:" + semname)
        tok.sem = ent[0]
        tok.val = ent[1]
        self._mark(tok, reads, writes)
        return tok

    def wait_all(self, eng, regs):
        toks = []
        for r in regs:
            if r.w is not None:
                toks.append(r.w)
        self._wait(eng, toks)


def _kc(w, k0, nk, c0, ncol):
    return w[k0 * 128:(k0 + nk) * 128, c0:c0 + ncol].reshape(nk, 128, ncol).transpose(1, 0, 2)


def pack_weights(a_w_in, a_w_out, w_kv, b_w_in, b_w_out, ple_w, ple_gate_w):
    units = []
    idx = {}

    def add(a):
        units.append(np.ascontiguousarray(a, dtype=np.float32).reshape(128, 2048))

    def ple(l):
        idx["PW", l] = len(units)
        for kh in range(2):
            add(_kc(ple_gate_w[l], kh * 4, 4, 0, 512))
        add(np.stack([_kc(ple_w[l], 0, 2, dc * 512, 512) for dc in range(2)], axis=1))
        for kh in range(2):
            add(_kc(ple_gate_w[l], kh * 4, 4, 512, 512))

    for l in range(2):
        w = a_w_in[l]
        idx["WV", l] = len(units)
        for c in range(4):
            for kh in range(2):
                add(_kc(w, kh * 4, 4, 2048 + c * 512, 512))
        idx["WUG", l] = len(units)
        for j in range(16):
            add(np.concatenate([_kc(w, 0, 8, j * 128, 128), _kc(w, 0, 8, 4096 + j * 128, 128)], axis=2))
        idx["WO", l] = len(units)
        for dc in range(2):
            for kq in range(4):
                add(_kc(a_w_out[l], kq * 4, 4, dc * 512, 512))
        ple(l)
    idx["WK"] = len(units)
    for hp in range(4):
        add(np.stack([_kc(w_kv, 0, 8, (hp * 2 + e) * 128, 128) for e in range(2)], axis=1))
    idx["WVV"] = len(units)
    for vc in range(2):
        for kh in range(2):
            add(_kc(w_kv, kh * 4, 4, 1024 + vc * 512, 512))
    for j in range(2):
        l = 2 + j
        idx["QG", l] = len(units)
        for hd in range(8):
            add(np.concatenate([_kc(b_w_in[j], 0, 8, hd * 128, 128), _kc(b_w_in[j], 0, 8, 1024 + hd * 128, 128)], axis=2))
        idx["BO", l] = len(units)
        for dc in range(2):
            for kh in range(2):
                add(_kc(b_w_out[j], kh * 4, 4, dc * 512, 512))
        ple(l)
    return np.stack(units, axis=0).reshape(len(units) * 128, 2048), idx


VOFF = {}
_o = 0
for _n, _sz in [("ng0", 1024), ("ng1", 1024), ("ng2", 1024), ("ng3", 1024), ("kvg", 1024), ("fing", 1024),
                ("lng0", 2048), ("lng1", 2048), ("lnb0", 2048), ("lnb1", 2048), ("bs0", 1024), ("bs1", 1024)]:
    VOFF[_n] = _o
    _o += _sz
NV = _o


def pack_vecs(norm_g, kv_norm_g, final_g, a_ln_g, a_ln_b, a_b_s):
    v = np.concatenate([norm_g[0], norm_g[1], norm_g[2], norm_g[3], kv_norm_g, final_g,
                        a_ln_g[0], a_ln_g[1], a_ln_b[0], a_ln_b[1],
                        a_b_s[0].reshape(-1), a_b_s[1].reshape(-1)]).astype(np.float32)
    return np.ascontiguousarray(np.broadcast_to(v[None, :], (128, NV)))


def make_consts():
    c = np.zeros((128, 512), np.float32)
    i = np.arange(128)
    c[:, 0:128] = np.eye(128)
    c[:, 128:256] = -1.0 * (i[:, None] >= i[None, :])
    c[:, 256:384] = -1.0
    c[:, 384:512] = -30000.0 * (i[:, None] >= i[None, :])
    return c


NUNITS = 2 * (8 + 16 + 8 + 5) + 8 + 2 * (8 + 4 + 5)


def build(layers=(0, 1, 2, 3), final=True, dbg=False):
    nc = bass.Bass("TRN2", target_bir_lowering=False)
    es = ExitStack()
    sc = Sched(nc, es)
    _, widx = pack_weights(*[np.zeros(s, np.float32) for s in
                             [(2, 1024, 6144), (2, 2048, 1024), (1024, 2048), (2, 1024, 2048),
                              (2, 1024, 1024), (4, 256, 1024), (4, 1024, 1024)]])

    x_d = nc.dram_tensor("x", [S, D], F32, kind="ExternalInput").ap()
    pT_d = nc.dram_tensor("pT", [4 * 128, 2 * S], F32, kind="ExternalInput").ap()
    wts_d = nc.dram_tensor("wts", [NUNITS * 128, 2048], F32, kind="ExternalInput").ap()
    vecs_d = nc.dram_tensor("vecs", [128, NV], F32, kind="ExternalInput").ap()
    wst_d = nc.dram_tensor("wst", [2 * 128, 1024], F32, kind="ExternalInput").ap()
    cst_d = nc.dram_tensor("cst", [128, 512], F32, kind="ExternalInput").ap()
    y_d = nc.dram_tensor("y", [S, D], F32, kind="ExternalOutput").ap()
    dbg_d = {}
    rDBG = Reg("dbg")

    def dump(name, ap, regs):
        if not dbg:
            return
        shp = list(ap.shape)
        dbg_d[name] = nc.dram_tensor("dbg_" + name, shp, F32, kind="ExternalOutput").ap()
        sc.dma("pool", dbg_d[name], ap, "dbg", reads=regs, writes=[rDBG])

    def sb(name, shape, dt):
        return es.enter_context(nc.sbuf_tensor(name, shape, dt))

    H = sb("H", [128, 16 * 1024], F32)
    HT = sb("HT", [128, 8 * HALF], BF16)
    OGT = sb("OGT", [128, 8 * HALF], BF16)
    X = sb("X", [128, 32768], BF16)
    RING = sb("RING", [128, NSLOT * 2048], BF16)
    TAA = sb("TAA", [128, 1024], F32)
    GB = TAA
    TA = [TAA[:, 0:512], TAA[:, 512:1024]]
    HN = [sb(f"HN{i}", [128, 1024], BF16) for i in range(2)]
    CST = sb("CST", [128, 512], BF16)
    PT = sb("PT", [128, 2 * HALF], BF16)
    SS = sb("SS", [128, 16], F32)
    RS = sb("RS", [128, 16], F32)
    TBB = sb("TBB", [128, 2048], BF16)
    TB = [TBB[:, i * 512:(i + 1) * 512] for i in range(4)]
    PS = [es.enter_context(nc.psum_tensor(f"ps{i}", [128, 512], F32)) for i in range(8)]
    NQ = 512
    LSUM = sb("LSUM", [128, NQ], F32)
    LSUMB = [sb(f"LSUMB{i}", [128, NQ], BF16) for i in range(2)]
    EE = sb("EE", [128, NQ], F32)
    LP = [sb(f"LP{i}", [128, NQ], BF16) for i in range(3)]
    AT = [sb(f"AT{i}", [128, NQ], BF16) for i in range(2)]
    QT = [PT[:, 0:HALF], PT[:, HALF:2 * HALF]]
    SG = [TBB[:, 0:HALF], TBB[:, HALF:2 * HALF]]
    OGF = OGT[:, :].bitcast(F32)
    LNG = OGF[:, 0:2048]
    LNB = OGF[:, 2048:4096]
    XB = X[:, 16384:32768]
    VHAT2 = [XB[:, 0:2048], XB[:, 8192:10240]]
    WST = XB[:, 2048:3072]
    BSH = XB[:, 3072:4096]
    BSL = XB[:, 4096:5120]
    BSHL = XB[0:64, 5120:6144]
    ONES = XB[0:64, 6144:6272]
    BSR = XB[0:64, 3072:5120]
    BSF = XB[:, 8192:10240].bitcast(F32)
    SML = sb("SML", [128, 64], F32)
    ST8 = sb("ST8", [128, 192], F32)
    ST = SML[:, 0:24]
    MV8 = SML[:, 24:40]
    RSV8 = SML[:, 40:48]
    NMR8 = SML[:, 48:56]

    H3 = H[:, :].rearrange("p (i d) -> p i d", d=1024)
    HT3 = HT[:, :].rearrange("p (k t) -> p k t", t=HALF)
    OGT3 = OGT[:, :].rearrange("p (k t) -> p k t", t=HALF)
    PT3 = PT[:, :].rearrange("p (k t) -> p k t", t=HALF)
    IDENT = CST[:, 0:128]
    NEGTRI = CST[:, 128:256]
    NEGONES = CST[:, 256:384]
    NEGMASK = CST[:, 384:512]

    rH = [Reg(f"H{i}") for i in range(16)]
    rHT = [Reg(f"HT{i}") for i in range(8)]
    rOGT = [Reg(f"OGT{h}") for h in range(8)]
    rHN = [Reg(), Reg()]
    rCST = Reg("CST")
    rSS = Reg("SS")
    rRS = Reg("RS")
    rTA = [Reg(), Reg()]
    rTB = [Reg() for _ in range(4)]
    rGBL = rTA
    rQT = [Reg(), Reg()]
    rPTL = rQT
    rSG = [[rTB[0], rTB[1]], [rTB[2], rTB[3]]]
    rLS = Reg(); rLSB = [Reg(), Reg()]; rEE = Reg()
    rLP = [Reg(), Reg(), Reg()]; rAT = [Reg(), Reg()]
    rLNG = Reg(); rLNB = Reg(); rWST = Reg(); rBS = Reg(); rVH2 = [Reg(), rBS]; rBSHL = Reg(); rBSR = Reg()
    rST = Reg(); rMV = Reg(); rSV = Reg()
    rST8 = [Reg() for _ in range(8)]
    rX = [Reg(f"X{i}") for i in range(8)]
    rPS = [Reg(f"ps{i}") for i in range(8)]
    rY = [Reg(f"Y{i}") for i in range(16)]

    cnt = {"ta": 0, "tb": 0, "hn": 0, "bank": 0, "lp": 0, "at": 0, "ptmp": 0, "ta4": 0}

    def nxt(key, n):
        v = cnt[key] % n
        cnt[key] += 1
        return v

    bank_set = [list(range(8))]

    def bank():
        bs = bank_set[0]
        return bs[nxt("bank", len(bs))]

    seq = []
    seqA = []
    XBr = X[:, 16384:32768]
    slot_ap = [RING[:, i * 2048:(i + 1) * 2048] for i in range(NSLOT)] + \
              [XBr[:, 10240 + i * 2048:10240 + (i + 1) * 2048] for i in range(3)]
    NS_A = NSLOT + 3
    rSLOT = [Reg(f"slot{i}") for i in range(NS_A)]
    slot_occ = [-1] * NS_A
    unit_slot = {}
    ring_state = {"issued": 0, "pos": 0, "done": 0, "last": -1}

    def ring_try_issue(q):
        pool = NS_A if seqA[q] else NSLOT
        for d_ in range(1, pool + 1):
            s = (ring_state["last"] + d_) % pool
            if slot_occ[s] < ring_state["done"]:
                break
        else:
            return False
        u = seq[q]
        sc.dma("pool", slot_ap[s], wts_d[u * 128:(u + 1) * 128, :], f"ring{s}", writes=[rSLOT[s]])
        slot_occ[s] = q
        unit_slot[q] = s
        ring_state["last"] = s
        ring_state["issued"] += 1
        return True

    def ring_begin(keep=0):
        ring_state["done"] = ring_state["pos"] - keep

    def ring_next(u):
        q = ring_state["pos"]
        assert seq[q] == u, (q, seq[q], u)
        while ring_state["issued"] <= q:
            ok = ring_try_issue(ring_state["issued"])
            assert ok, "ring: no free slot for a unit that is needed now"
        ring_state["pos"] += 1
        return unit_slot[q]

    def ring_prefetch(*_):
        while ring_state["issued"] < len(seq) and ring_state["issued"] < ring_state["pos"] + NS_A:
            if not ring_try_issue(ring_state["issued"]):
                break

    def plan():
        def ext(r, isa):
            seq.extend(r)
            seqA.extend([isa] * len(r))
        for l in layers:
            for hf in range(2):
                if l < 2:
                    ext(range(widx["WV", l], widx["WV", l] + 8), l < 2)
                    ext(range(widx["WUG", l], widx["WUG", l] + 16), l < 2)
                    ext(range(widx["WO", l], widx["WO", l] + 8), l < 2)
                else:
                    if l == 2 or (l == 3 and 2 not in layers):
                        pass
                    ext(range(widx["QG", l], widx["QG", l] + 8), l < 2)
                    ext(range(widx["BO", l], widx["BO", l] + 4), l < 2)
                ext(range(widx["PW", l], widx["PW", l] + 5), l < 2)
            if l == 1 and (2 in layers or 3 in layers):
                for hf in range(2):
                    ext(range(widx["WK"], widx["WK"] + 4), False)
                    ext(range(widx["WVV"], widx["WVV"] + 4), False)

    def slot3(slot, k):
        return slot_ap[slot].rearrange("p (k c) -> p k c", k=k)

    sc.dma("pool", CST[:, :], cst_d[:, :], "cst", writes=[rCST])
    for i in range(16):
        sc.dma("sp", H3[:, i, :], x_d[i * 128:(i + 1) * 128, :], f"x{i}", writes=[rH[i]])

    JUNK = EE[:, :].bitcast(BF16)
    stats_ready = [False, False]

    def norm_stats(hf):
        if stats_ready[hf]:
            return
        for ii in range(8):
            i = hf * 8 + ii
            sc.op("act", lambda e: e.activation(out=JUNK, in_=H3[:, i, :], func=AF.Square, accum_out=SS[:, i:i + 1]),
                  reads=[rH[i]], writes=[rEE, rSS])
        sl = slice(hf * 8, hf * 8 + 8)
        sc.op("dve", lambda e: e.tensor_scalar(out=RS[:, sl], in0=SS[:, sl], scalar1=1.0 / 1024, scalar2=EPS,
                                                op0=ALU.mult, op1=ALU.add), reads=[rSS], writes=[rRS])
        sc.op("act", lambda e: e.activation(out=RS[:, sl], in_=RS[:, sl], func=AF.Sqrt), reads=[rRS], writes=[rRS])
        sc.op("dve", lambda e: e.reciprocal(out=RS[:, sl], in_=RS[:, sl]), reads=[rRS], writes=[rRS])
        stats_ready[hf] = True

    def phase_norm(goff, hf):
        if gb_loaded["goff"] == goff:
            gb_loaded["goff"] = None
        else:
            sc.dma("sp", GB[:, :], vecs_d[:, goff:goff + 1024], "gb", writes=rGBL)
        norm_stats(hf)
        for ii in range(8):
            i = hf * 8 + ii
            hb = ii % 2
            sc.op("dve", lambda e: e.scalar_tensor_tensor(out=HN[hb][:, :], in0=H3[:, i, :], scalar=RS[:, i:i + 1],
                                                           in1=GB[:, :], op0=ALU.mult, op1=ALU.mult),
                  reads=[rH[i], rRS] + rGBL, writes=[rHN[hb]])
            transpose_tile(HN[hb], rHN[hb], 8, None, [rHT[ii]], HT3[:, :, ii * 128:(ii + 1) * 128])

    def ple_prep_tile(hf, ii):
        i = hf * 8 + ii
        hb = nxt("hn", 2)
        sc.op("dve", lambda e: e.tensor_copy(out=HN[hb][:, :], in_=H3[:, i, :]), reads=[rH[i]], writes=[rHN[hb]])
        transpose_tile(HN[hb], rHN[hb], 8, None, [rHT[ii]], HT3[:, :, ii * 128:(ii + 1) * 128])

    def ple_load_pt(l, hf):
        sc.dma("pool", PT3[:, :, :], pT_d[l * 128:(l + 1) * 128, :].rearrange("p (k t) -> p k t", t=S)[:, :, hf * HALF:(hf + 1) * HALF],
               "pt", writes=rPTL)

    def transpose_tile(src, rsrc, nk, dst_fn, wregs, dst_all):
        b = bank()
        pv = PS[b][:, :].bitcast(BF16).rearrange("p (k t) -> p k t", t=128)
        for k in range(nk):
            sc.op("pe", lambda e: e.transpose(out=pv[:, k, :], in_=src[:, k * 128:(k + 1) * 128], identity=IDENT),
                  reads=[rsrc, rCST], writes=[rPS[b]], signal=(k == nk - 1))
        sc.op("act", lambda e: e.copy(out=dst_all, in_=pv[:, 0:nk, :]), reads=[rPS[b]], writes=wregs)

    def phase_A(l, hf):
        if True:
            X3 = X[:, 0:16384].rearrange("p (i f) -> p i f", f=2048)
            WST3 = WST.rearrange("p (g t) -> p g t", t=128)
            BSHL3 = BSHL.rearrange("p (g t) -> p g t", t=128)

            if hf == 0:
                sc.dma("sp", LNG, vecs_d[:, VOFF[f"lng{l}"]:VOFF[f"lng{l}"] + 2048], "lng", writes=[rLNG])
                sc.dma("sp", LNB, vecs_d[:, VOFF[f"lnb{l}"]:VOFF[f"lnb{l}"] + 2048], "lnb", writes=[rLNB])
                sc.dma("pool", WST, wst_d[l * 128:(l + 1) * 128, :], "wst", writes=[rWST])
                sc.op("dve", lambda e: e.memset(WST3[64:128, :, 0:64], 0.0), writes=[rWST])
                sc.dma("sp", BSF, vecs_d[:, VOFF[f"bs{l}"]:VOFF[f"bs{l}"] + 1024], "bsf", writes=[rBS])
                sc.op("dve", lambda e: e.tensor_copy(out=BSH, in_=BSF), reads=[rBS], writes=[rBS])
                sc.op("dve", lambda e: e.tensor_tensor(out=BSF, in0=BSF, in1=BSH, op=ALU.subtract),
                      reads=[rBS], writes=[rBS])
                sc.op("dve", lambda e: e.tensor_copy(out=BSL, in_=BSF), reads=[rBS], writes=[rBS])
                sc.op("dve", lambda e: e.memset(BSHL, 0.0), writes=[rBSHL])
                sc.op("dve", lambda e: e.memset(ONES, 0.0), writes=[rBSHL])
                sc.op("dve", lambda e: e.tensor_copy(out=BSHL[0:1, :], in_=BSH[0:1, :]), reads=[rBS], writes=[rBSHL])
                sc.op("dve", lambda e: e.tensor_copy(out=BSHL[32:33, :], in_=BSL[32:33, :]), reads=[rBS], writes=[rBSHL])
                sc.op("dve", lambda e: e.memset(ONES[0:1, :], 1.0), writes=[rBSHL])
                sc.op("dve", lambda e: e.memset(ONES[32:33, :], 1.0), writes=[rBSHL])
                BSR4 = BSR.rearrange("p (g d t) -> p g d t", d=2, t=128)
                for d_ in range(2):
                    sc.op("dve", lambda e: e.tensor_copy(out=BSR4[:, :, d_, :], in_=BSHL3), reads=[rBSHL], writes=[rBS, rBSR])

            phase_norm(VOFF[f"ng{l}"], hf)
            if "s" in OPT:
                norm_stats(1 - hf)
            if l == 0 and hf == 0:
                dump("HT", HT[:, :], rHT)

            MV83 = MV8.rearrange("p (i t) -> p i t", t=2)

            def stats_batch(t0, t1):
                sl_ = slice(t0, t1)
                sc.op("dve", lambda e: e.tensor_scalar(out=RSV8[:, sl_], in0=MV83[:, sl_, 1], scalar1=EPS, scalar2=None, op0=ALU.add),
                      reads=[rMV], writes=[rSV])
                sc.op("act", lambda e: e.activation(out=RSV8[:, sl_], in_=RSV8[:, sl_], func=AF.Sqrt), reads=[rSV], writes=[rSV])
                sc.op("dve", lambda e: e.reciprocal(out=RSV8[:, sl_], in_=RSV8[:, sl_]), reads=[rSV], writes=[rSV])
                sc.op("dve", lambda e: e.scalar_tensor_tensor(out=NMR8[:, sl_], in0=MV83[:, sl_, 0], scalar=-1.0, in1=RSV8[:, sl_],
                                                               op0=ALU.mult, op1=ALU.mult), reads=[rMV, rSV], writes=[rSV])

            TA4 = [TA[0], TA[1], EE[:, :], LSUM[:, :]]
            rTA4 = [rTA[0], rTA[1], rEE, rLS]

            def ps_produce(ii):
                VHAT = VHAT2[ii % 2]
                rVH = rVH2[ii % 2]
                for q in range(4):
                    ta = nxt("ta4", 4)
                    T_, rT_ = TA4[ta], rTA4[ta]
                    qs = slice(q * 512, (q + 1) * 512)
                    sc.op("act", lambda e: e.activation(out=T_, in_=X3[:, ii, qs], func=AF.Identity,
                                                         bias=NMR8[:, ii:ii + 1], scale=RSV8[:, ii:ii + 1]),
                          reads=[rX[ii], rSV], writes=[rT_])
                    sc.op("dve", lambda e: e.tensor_tensor(out=T_, in0=T_, in1=LNG[:, qs], op=ALU.mult),
                          reads=[rT_, rLNG], writes=[rT_])
                    sc.op("dve", lambda e: e.tensor_tensor(out=VHAT[:, qs], in0=T_, in1=LNB[:, qs], op=ALU.add),
                          reads=[rT_, rLNB], writes=[rVH])

            def ps_consume(ii):
                VHAT = VHAT2[ii % 2]
                rVH = rVH2[ii % 2]
                Xs = X3[:, ii, :].rearrange("p (j t) -> p j t", t=128)
                for jb in range(4):
                    b = bank()
                    sc.op("pe", lambda e: e.matmul(PS[b][:, :], lhsT=ONES[0:33, :], rhs=BSR[0:33, jb * 512:(jb + 1) * 512],
                                                    start=True, stop=False, skip_group_check=True),
                          reads=[rBSHL, rBSR], writes=[rPS[b]], signal=False)
                    for jj in range(4):
                        j = jb * 4 + jj
                        g = j // 2
                        sc.op("pe", lambda e: e.matmul(PS[b][:, jj * 128:(jj + 1) * 128], lhsT=VHAT[:, j * 128:(j + 1) * 128],
                                                        rhs=WST3[:, g, :], start=False, stop=(jj == 3), skip_group_check=True),
                              reads=[rVH, rWST], writes=[rPS[b]], signal=(jj == 3))
                    sc.op("act", lambda e: e.copy(out=Xs[:, jb * 4:(jb + 1) * 4, :],
                                                   in_=PS[b][:, :].rearrange("p (j t) -> p j t", t=128)),
                          reads=[rPS[b]], writes=[rX[ii]])

            for c in range(4):
                ring_begin()
                sA = ring_next(widx["WV", l] + c * 2)
                sB = ring_next(widx["WV", l] + c * 2 + 1)
                ring_prefetch(2)
                for ii in range(8):
                    b = bank()
                    for k in range(8):
                        s_ = sA if k < 4 else sB
                        sc.op("pe", lambda e: e.matmul(PS[b][:, :], lhsT=HT3[:, k, ii * 128:(ii + 1) * 128],
                                                        rhs=slot3(s_, 4)[:, k % 4, :], start=(k == 0), stop=(k == 7)),
                              reads=[rHT[ii], rSLOT[s_]], writes=[rPS[b]], signal=(k == 7))
                    sc.op("act", lambda e: e.activation(out=X3[:, ii, c * 512:(c + 1) * 512], in_=PS[b][:, :], func=AF.Gelu),
                          reads=[rPS[b]], writes=[rX[ii]])
                    sc.op("dve", lambda e: e.bn_stats(out=ST8[:, ii * 24 + c * 6:ii * 24 + (c + 1) * 6],
                                                       in_=X3[:, ii, c * 512:(c + 1) * 512]),
                          reads=[rX[ii]], writes=[rST8[ii]])
                    if c == 3:
                        sc.op("dve", lambda e: e.bn_aggr(out=MV8[:, ii * 2:(ii + 1) * 2], in_=ST8[:, ii * 24:(ii + 1) * 24]),
                              reads=[rST8[ii]], writes=[rMV])
                        if ii == 3:
                            stats_batch(0, 4)
                            ps_produce(0)
                            ps_produce(1)
                        if ii == 7:
                            stats_batch(4, 8)
            if l == 0 and hf == 0:
                dump("GV", X[:, 0:16384], rX)
            for ii in range(8):
                ps_consume(ii)
                if ii + 2 < 8:
                    ps_produce(ii + 2)

            if l == 0 and hf == 0:
                dump("SVT", X[:, 0:16384], rX)
            for j in range(16):
                ring_begin()
                s_ = ring_next(widx["WUG", l] + j)
                ring_prefetch(1)
                w3 = slot3(s_, 8)
                for st in range(2):
                    bu = bank()
                    for k in range(8):
                        sc.op("pe", lambda e: e.matmul(PS[bu][:, :], lhsT=w3[:, k, 0:128], rhs=HT3[:, k, st * 512:(st + 1) * 512],
                                                        start=(k == 0), stop=(k == 7)),
                              reads=[rSLOT[s_]] + rHT[st * 4:(st + 1) * 4], writes=[rPS[bu]], signal=(k == 7))
                    bg = bank()
                    for k in range(8):
                        sc.op("pe", lambda e: e.matmul(PS[bg][:, :], lhsT=w3[:, k, 128:256], rhs=HT3[:, k, st * 512:(st + 1) * 512],
                                                        start=(k == 0), stop=(k == 7)),
                              reads=[rSLOT[s_]] + rHT[st * 4:(st + 1) * 4], writes=[rPS[bg]], signal=(k == 7))
                    tb = nxt("tb", 4)
                    sc.op("act", lambda e: e.activation(out=TB[tb][:, :], in_=PS[bu][:, :], func=AF.Gelu),
                          reads=[rPS[bu]], writes=[rTB[tb]])
                    ta2 = nxt("ta", 2)
                    tb2 = nxt("tb", 4)
                    sc.op("act", lambda e: e.activation(out=TA[ta2][:, :], in_=PS[bg][:, :], func=AF.Tanh, scale=0.5),
                          reads=[rPS[bg]], writes=[rTA[ta2]])
                    sc.op("dve", lambda e: e.scalar_tensor_tensor(out=TB[tb2][:, :], in0=TA[ta2][:, :], scalar=1.0, in1=PS[bg][:, :],
                                                                   op0=ALU.add, op1=ALU.mult),
                          reads=[rTA[ta2], rPS[bg]], writes=[rTB[tb2]])
                    sc.op("dve", lambda e: e.tensor_tensor(out=TB[tb][:, :], in0=TB[tb][:, :], in1=TB[tb2][:, :], op=ALU.mult),
                          reads=[rTB[tb], rTB[tb2]], writes=[rTB[tb]])
                    xv = X3[:, st * 4:(st + 1) * 4, j * 128:(j + 1) * 128]
                    sc.op("dve", lambda e: e.scalar_tensor_tensor(out=xv, in0=TB[tb][:, :].rearrange("p (i t) -> p i t", t=128),
                                                                   scalar=0.5, in1=xv, op0=ALU.mult, op1=ALU.mult),
                          reads=[rTB[tb]] + rX[st * 4:(st + 1) * 4], writes=rX[st * 4:(st + 1) * 4])

            if l == 0 and hf == 0:
                dump("GAT", X[:, 0:16384], rX)
            ple_load_pt(l, hf)
            for dc in range(2):
                ring_begin()
                ss_ = [ring_next(widx["WO", l] + dc * 4 + kq) for kq in range(4)]
                ring_prefetch(4)
                for ii in range(8):
                    i = hf * 8 + ii
                    b = bank()
                    for k in range(16):
                        s_ = ss_[k // 4]
                        sc.op("pe", lambda e: e.matmul(PS[b][:, :], lhsT=X3[:, ii, k * 128:(k + 1) * 128],
                                                        rhs=slot3(s_, 4)[:, k % 4, :], start=(k == 0), stop=(k == 15)),
                              reads=[rX[ii], rSLOT[s_]], writes=[rPS[b]], signal=(k == 15))
                    hv = H3[:, i, dc * 512:(dc + 1) * 512]
                    sc.op("dve", lambda e: e.tensor_tensor(out=hv, in0=hv, in1=PS[b][:, :], op=ALU.add),
                          reads=[rPS[b], rH[i]], writes=[rH[i]])
                    if dc == 1 and "p" in OPT:
                        if ii >= 1:
                            ple_prep_tile(hf, ii - 1)
                        if ii == 7:
                            ple_prep_tile(hf, 7)
        if l == 0 and hf == 0:
            dump("H1", H[:, 0:8192], rH[0:8])
        phase_ple(l, hf)

    gb_loaded = {"goff": None}

    def phase_ple(l, hf):
        nn = next_norm_of.get((l, hf))
        if nn is not None:
            sc.dma("sp", GB[:, :], vecs_d[:, nn[0]:nn[0] + 1024], "gb", writes=rGBL)
            gb_loaded["goff"] = nn[0]
        if "p" not in OPT:
            for ii in range(8):
                ple_prep_tile(hf, ii)
        base = widx["PW", l]
        sw = None
        wp = None
        for dc in range(2):
            if dc == 0:
                ring_begin()
                sg = [ring_next(base + 0), ring_next(base + 1)]
                sw = ring_next(base + 2)
                wp = slot_ap[sw].rearrange("p (d k c) -> p d k c", d=2, k=2)
            else:
                ring_begin(keep=1)
                sg = [ring_next(base + 3), ring_next(base + 4)]
            ring_prefetch(3)
            for ii in range(8):
                i = hf * 8 + ii
                bg = bank()
                for k in range(8):
                    s_ = sg[k // 4]
                    sc.op("pe", lambda e: e.matmul(PS[bg][:, :], lhsT=HT3[:, k, ii * 128:(ii + 1) * 128],
                                                    rhs=slot3(s_, 4)[:, k % 4, :], start=(k == 0), stop=(k == 7)),
                          reads=[rHT[ii], rSLOT[s_]], writes=[rPS[bg]], signal=(k == 7))
                bp = bank()
                for k in range(2):
                    sc.op("pe", lambda e: e.matmul(PS[bp][:, :], lhsT=PT3[:, k, ii * 128:(ii + 1) * 128],
                                                    rhs=wp[:, dc, k, :], start=(k == 0), stop=(k == 1)),
                          reads=rPTL + [rSLOT[sw]], writes=[rPS[bp]], signal=(k == 1))
                tp_ = nxt("ptmp", 2)
                T_ = (EE, LSUM)[tp_]
                rT_ = (rEE, rLS)[tp_]
                sc.op("act", lambda e: e.activation(out=T_[:, :], in_=PS[bg][:, :], func=AF.Tanh, scale=0.5),
                      reads=[rPS[bg]], writes=[rT_])
                sc.op("dve", lambda e: e.scalar_tensor_tensor(out=T_[:, :], in0=T_[:, :], scalar=1.0, in1=PS[bp][:, :],
                                                               op0=ALU.add, op1=ALU.mult),
                      reads=[rT_, rPS[bp]], writes=[rT_])
                hv = H3[:, i, dc * 512:(dc + 1) * 512]
                sc.op("dve", lambda e: e.scalar_tensor_tensor(out=hv, in0=T_[:, :], scalar=0.5, in1=hv, op0=ALU.mult, op1=ALU.add),
                      reads=[rT_, rH[i]], writes=[rH[i]])
        stats_ready[hf] = False

    KT3 = X[:, 0:16384].rearrange("p (h t) -> p h t", t=S)
    V3 = X[:, 16384:32768].rearrange("p (i f) -> p i f", f=1024)
    rKT = [Reg(f"KT{h}") for h in range(8)]
    rV = [Reg(f"V{i}") for i in range(16)]

    def phase_kv(hf):
        phase_norm(VOFF["kvg"], hf)
        for hp in range(4):
            ring_begin()
            s_ = ring_next(widx["WK"] + hp)
            ring_prefetch(1)
            w4 = slot_ap[s_].rearrange("p (e k c) -> p e k c", e=2, k=8)
            for e_ in range(2):
                hd = hp * 2 + e_
                for st in range(2):
                    b = bank()
                    for k in range(8):
                        sc.op("pe", lambda e: e.matmul(PS[b][:, :], lhsT=w4[:, e_, k, :], rhs=HT3[:, k, st * 512:(st + 1) * 512],
                                                        start=(k == 0), stop=(k == 7)),
                              reads=[rSLOT[s_]] + rHT[st * 4:(st + 1) * 4], writes=[rPS[b]], signal=(k == 7))
                    c0 = hf * HALF + st * 512
                    sc.op("act", lambda e: e.copy(out=KT3[:, hd, c0:c0 + 512], in_=PS[b][:, :]), reads=[rPS[b]], writes=[rKT[hd]])
        for vc in range(2):
            ring_begin()
            sv_ = [ring_next(widx["WVV"] + vc * 2 + kh) for kh in range(2)]
            ring_prefetch(2)
            for ii in range(8):
                i = hf * 8 + ii
                b = bank()
                for k in range(8):
                    s_ = sv_[k // 4]
                    sc.op("pe", lambda e: e.matmul(PS[b][:, :], lhsT=HT3[:, k, ii * 128:(ii + 1) * 128],
                                                    rhs=slot3(s_, 4)[:, k % 4, :], start=(k == 0), stop=(k == 7)),
                          reads=[rHT[ii], rSLOT[s_]], writes=[rPS[b]], signal=(k == 7))
                sc.op("dve", lambda e: e.tensor_copy(out=V3[:, i, vc * 512:(vc + 1) * 512], in_=PS[b][:, :]),
                      reads=[rPS[b]], writes=[rV[i]] + rSLOT[NSLOT:])

    def phase_B(l, hf):
        if True:
            SCALE = 1.0 / np.sqrt(128.0)

            phase_norm(VOFF[f"ng{l}"], hf)
            if "s" in OPT:
                norm_stats(1 - hf)
            if final and l == 3 and hf == 1:
                final_half(0)
            bank_set[0] = [0, 1, 2, 3, 6]

            def proj_items(hd, per):
                st8 = {}
                qb = hd % 2

                def start():
                    ring_begin()
                    st8["s"] = ring_next(widx["QG", l] + hd)
                    ring_prefetch()

                def group(st, isg):
                    g8 = {}

                    def mm(k0, k1):
                        def f():
                            if "s" not in st8:
                                start()
                            s_ = st8["s"]
                            w3 = slot3(s_, 8)
                            if "b" not in g8:
                                g8["b"] = bank()
                            b = g8["b"]
                            co = 128 if isg else 0
                            for k in range(k0, k1):
                                sc.op("pe", lambda e: e.matmul(PS[b][:, :], lhsT=w3[:, k, co:co + 128],
                                                                rhs=HT3[:, k, st * 512:(st + 1) * 512], start=(k == 0), stop=(k == 7)),
                                      reads=[rSLOT[s_]] + rHT[st * 4:(st + 1) * 4], writes=[rPS[b]], signal=(k == 7))
                            if k1 == 8:
                                if not isg:
                                    sc.op("dve", lambda e: e.tensor_scalar(out=QT[qb][:, st * 512:(st + 1) * 512], in0=PS[b][:, :],
                                                                            scalar1=float(SCALE), scalar2=None, op0=ALU.mult),
                                          reads=[rPS[b]], writes=[rQT[qb]])
                                else:
                                    ta = nxt("ta", 2)
                                    sc.op("act", lambda e: e.activation(out=TA[ta][:, :], in_=PS[b][:, :], func=AF.Exp, scale=-1.0),
                                          reads=[rPS[b]], writes=[rTA[ta]])
                                    sc.op("dve", lambda e: e.tensor_scalar(out=TA[ta][:, :], in0=TA[ta][:, :], scalar1=1.0, scalar2=None,
                                                                            op0=ALU.add), reads=[rTA[ta]], writes=[rTA[ta]])
                                    g8["ta"] = ta
                                    sc.op("dve", lambda e: e.tensor_copy(out=SG[qb][:, st * 512:(st + 1) * 512], in_=PS[b][:, :]),
                                          reads=[rPS[b]], writes=rSG[qb])
                        return f
                    its = [("mm", mm(k0, min(8, k0 + per)), None) for k0 in range(0, 8, per)]
                    if isg:
                        def rc(q):
                            def f():
                                ta = g8["ta"]
                                sl_ = slice(q * 128, (q + 1) * 128)
                                sc.op("dve", lambda e: e.reciprocal(out=TA[ta][:, sl_], in_=TA[ta][:, sl_]),
                                      reads=[rTA[ta]], writes=[rTA[ta]])
                            return f

                        def fin():
                            ta = g8["ta"]
                            sgv = SG[qb][:, st * 512:(st + 1) * 512]
                            sc.op("dve", lambda e: e.tensor_tensor(out=sgv, in0=sgv, in1=TA[ta][:, :], op=ALU.mult),
                                  reads=[rTA[ta]] + rSG[qb], writes=rSG[qb])
                        rdy = lambda: "ta" in g8
                        its += [("ch", rc(q), rdy) for q in range(4)] + [("ch", fin, rdy)]
                    return its
                items = []
                for st in range(2):
                    items += group(st, False)
                for st in range(2):
                    items += group(st, True)
                return items

            steps = []
            for hd in range(8):
                for c2 in range(2):
                    c = hf * 2 + c2
                    nkb = 4 * c + 4
                    prev = None
                    for kb in range(nkb - 1, -1, -1):
                        s = dict(hd=hd, c2=c2, c=c, kb=kb, first=(kb == nkb - 1), last=(kb == 0), idx=nkb - 1 - kb,
                                 prev=prev, nxt_last=(kb == 1))
                        steps.append(s)
                        prev = s
            n = len(steps)
            st_ = {"lsb": 0}

            def geom(s):
                c0 = max(0, s["kb"] - 4 * s["c"]) * 128
                return c0, slice(c0, NQ), slice(s["c2"] * 512 + c0, s["c2"] * 512 + NQ)

            def emit_Z(s):
                c0, cs, qs = geom(s)
                hd, kb, qb = s["hd"], s["kb"], s["hd"] % 2
                zb = bank()
                s["zb"] = zb
                Z = PS[zb]
                diag = kb >= 4 * s["c"]
                sc.op("pe", lambda e: e.matmul(Z[:, cs], lhsT=KT3[:, hd, kb * 128:(kb + 1) * 128], rhs=QT[qb][:, qs],
                                                start=True, stop=not diag, skip_group_check=True),
                      reads=[rKT[hd], rQT[qb]], writes=[rPS[zb]], signal=not diag)
                if diag:
                    sc.op("pe", lambda e: e.matmul(Z[:, c0:c0 + 128], lhsT=IDENT, rhs=NEGMASK, start=False, stop=True,
                                                    skip_group_check=True),
                          reads=[rCST], writes=[rPS[zb]], signal=True)

            def emit_ELP(s):
                c0, cs, qs = geom(s)
                zb = s["zb"]
                lb = nxt("lp", 3)
                s["lb"] = lb
                sc.op("act", lambda e: e.activation(out=EE[:, cs], in_=PS[zb][:, cs], func=AF.Exp),
                      reads=[rPS[zb]], writes=[rEE])
                sc.op("act", lambda e: e.activation(out=LP[lb][:, cs], in_=EE[:, cs], func=AF.Ln, bias=1.0),
                      reads=[rEE], writes=[rLP[lb]])

            def emit_TriOnes(s):
                c0, cs, qs = geom(s)
                zb, lb = s["zb"], s["lb"]
                Z = PS[zb]
                sc.op("pe", lambda e: e.matmul(Z[:, cs], lhsT=NEGTRI, rhs=LP[lb][:, cs], start=False, stop=s["first"],
                                                skip_group_check=True),
                      reads=[rCST, rLP[lb]], writes=[rPS[zb]], signal=s["first"])
                if not s["first"]:
                    k_ = s["prev"]["lsb_out"]
                    sc.op("pe", lambda e: e.matmul(Z[:, cs], lhsT=NEGONES, rhs=LSUMB[k_][:, cs], start=False, stop=True,
                                                    skip_group_check=True),
                          reads=[rCST, rLSB[k_]], writes=[rPS[zb]], signal=True)

            def emit_AT(s):
                c0, cs, qs = geom(s)
                zb = s["zb"]
                ab = nxt("at", 2)
                s["ab"] = ab
                sc.op("act", lambda e: e.activation(out=AT[ab][:, cs], in_=PS[zb][:, cs], func=AF.Exp),
                      reads=[rPS[zb]], writes=[rAT[ab]])

            def emit_LSUM(s):
                if s["last"]:
                    return
                c0, cs, qs = geom(s)
                lb = s["lb"]
                if s["first"]:
                    sc.op("dve", lambda e: e.memset(LSUM[:, :], 0.0), writes=[rLS])
                sc.op("dve", lambda e: e.tensor_tensor(out=LSUM[:, cs], in0=LSUM[:, cs], in1=LP[lb][:, cs], op=ALU.add),
                      reads=[rLS, rLP[lb]], writes=[rLS])
                k_ = 1 - st_["lsb"]
                sc.op("dve", lambda e: e.tensor_copy(out=LSUMB[k_][:, :], in_=LSUM[:, :]), reads=[rLS], writes=[rLSB[k_]])
                st_["lsb"] = k_
                s["lsb_out"] = k_

            def emit_AV(s):
                c0, cs, qs = geom(s)
                hd, kb, c2, qb = s["hd"], s["kb"], s["c2"], s["hd"] % 2
                ob = 4 + c2
                ab = s["ab"]
                sc.op("pe", lambda e: e.matmul(PS[ob][:, cs], lhsT=V3[:, kb, hd * 128:(hd + 1) * 128], rhs=AT[ab][:, cs],
                                                start=s["first"], stop=s["last"], skip_group_check=True),
                      reads=[rV[kb], rAT[ab]], writes=[rPS[ob]], signal=s["last"])
                if s["last"]:
                    sc.op("dve", lambda e: e.tensor_tensor(out=OGT3[:, hd, c2 * 512:(c2 + 1) * 512], in0=PS[ob][:, :],
                                                            in1=SG[qb][:, c2 * 512:(c2 + 1) * 512], op=ALU.mult),
                          reads=[rPS[ob]] + rSG[qb], writes=[rOGT[hd]])

            bg_mm = []
            bg_ch = []

            def bg_add(items):
                for kind, f, rdy in items:
                    (bg_mm if kind == "mm" else bg_ch).append((f, rdy))

            def bg_flush():
                while bg_mm:
                    bg_mm.pop(0)[0]()
                while bg_ch:
                    bg_ch.pop(0)[0]()

            def bg_step():
                if bg_mm:
                    bg_mm.pop(0)[0]()
                if bg_ch and (bg_ch[0][1] is None or bg_ch[0][1]()):
                    bg_ch.pop(0)[0]()

            bg_add(proj_items(0, 8))
            bg_flush()
            emit_Z(steps[0])
            emit_ELP(steps[0])
            emit_LSUM(steps[0])
            for i in range(n):
                s = steps[i]
                if s["first"] and s["c2"] == 0 and s["hd"] + 1 < 8:
                    bg_add(proj_items(s["hd"] + 1, 2 if hf == 1 else 4))
                if i + 1 < n:
                    s1 = steps[i + 1]
                    if s1["first"] and s1["c2"] == 0:
                        bg_flush()
                    emit_Z(s1)
                emit_TriOnes(s)
                if i + 1 < n:
                    emit_ELP(steps[i + 1])
                emit_AT(s)
                if i + 1 < n:
                    emit_LSUM(steps[i + 1])
                if i >= 1:
                    emit_AV(steps[i - 1])
                bg_step()
                for _ in range(N_WARM):
                    sc.op("pe", lambda e: e.matmul(PS[7][:, :], lhsT=IDENT, rhs=CST[:, :], start=True, stop=True,
                                                    skip_group_check=True),
                          reads=[rCST], writes=[rPS[7]], signal=False)
            emit_AV(steps[n - 1])
            bank_set[0] = list(range(8))
            ple_load_pt(l, hf)
            for dc in range(2):
                ring_begin()
                so = [ring_next(widx["BO", l] + dc * 2 + kh) for kh in range(2)]
                ring_prefetch(2)
                for ii in range(8):
                    i = hf * 8 + ii
                    b = bank()
                    for k in range(8):
                        s_ = so[k // 4]
                        sc.op("pe", lambda e: e.matmul(PS[b][:, :], lhsT=OGT3[:, k, ii * 128:(ii + 1) * 128],
                                                        rhs=slot3(s_, 4)[:, k % 4, :], start=(k == 0), stop=(k == 7)),
                              reads=[rOGT[k], rSLOT[s_]], writes=[rPS[b]], signal=(k == 7))
                    hv = H3[:, i, dc * 512:(dc + 1) * 512]
                    sc.op("dve", lambda e: e.tensor_tensor(out=hv, in0=hv, in1=PS[b][:, :], op=ALU.add),
                          reads=[rPS[b], rH[i]], writes=[rH[i]])
                    if dc == 1 and "p" in OPT:
                        if ii >= 1:
                            ple_prep_tile(hf, ii - 1)
                        if ii == 7:
                            ple_prep_tile(hf, 7)
        phase_ple(l, hf)

    final_done = [False, False]

    def final_half(hf):
        norm_stats(hf)
        sc.dma("sp", GB[:, :], vecs_d[:, VOFF["fing"]:VOFF["fing"] + 1024], "gb", writes=rGBL)
        for ii in range(8):
            i = hf * 8 + ii
            sc.op("dve", lambda e: e.scalar_tensor_tensor(out=H3[:, i, :], in0=H3[:, i, :], scalar=RS[:, i:i + 1], in1=GB[:, :],
                                                           op0=ALU.mult, op1=ALU.mult), reads=[rH[i], rRS] + rGBL, writes=[rH[i]])
            sc.dma("sp", y_d[i * 128:(i + 1) * 128, :], H3[:, i, :], "y", reads=[rH[i]], writes=[rY[i]])
        final_done[hf] = True

    def phase_final():
        for hf in range(2):
            if not final_done[hf]:
                final_half(hf)

    def store_raw():
        for i in range(16):
            sc.dma("sp", y_d[i * 128:(i + 1) * 128, :], H3[:, i, :], "y", reads=[rH[i]], writes=[rY[i]])

    next_norm_of = {}
    for l in layers:
        next_norm_of[(l, 0)] = (VOFF[f"ng{l}"], 1)
        if l == 1 and (2 in layers or 3 in layers):
            next_norm_of[(l, 1)] = (VOFF["kvg"], 0)
        elif (l + 1) in layers and l != 1:
            next_norm_of[(l, 1)] = (VOFF[f"ng{l + 1}"], 0)
    plan()
    for l in layers:
        for hf in range(2):
            if l < 2:
                phase_A(l, hf)
            else:
                phase_B(l, hf)
        if l == 1 and (2 in layers or 3 in layers):
            for hf in range(2):
                phase_kv(hf)
    if final:
        phase_final()
    else:
        store_raw()
    sc.wait_all("sp", rY)
    sc.wait_all("pool", [rDBG])
    assert ring_state["pos"] == len(seq), (ring_state, len(seq))
    return nc, es


def make_in_maps(inputs):
    f = lambda a: np.asarray(a, dtype=np.float32)
    x = f(inputs["x"]); p = f(inputs["p"])
    wts, _ = pack_weights(f(inputs["a_w_in"]), f(inputs["a_w_out"]), f(inputs["w_kv"]), f(inputs["b_w_in"]),
                          f(inputs["b_w_out"]), f(inputs["ple_w"]), f(inputs["ple_gate_w"]))
    vecs = pack_vecs(f(inputs["norm_g"]), f(inputs["kv_norm_g"]), f(inputs["final_g"]), f(inputs["a_ln_g"]),
                     f(inputs["a_ln_b"]), f(inputs["a_b_s"]))
    wst = np.ascontiguousarray(f(inputs["a_w_s"]).transpose(0, 3, 1, 2)).reshape(2 * 128, 1024)
    cst = make_consts()
    maps = []
    for b in range(8):
        pT = np.ascontiguousarray(p[:, b].reshape(4, S, 2, 128).transpose(0, 3, 2, 1)).reshape(4 * 128, 2 * S)
        maps.append({"x": np.ascontiguousarray(x[b]), "pT": pT, "wts": wts, "vecs": vecs, "wst": wst, "cst": cst})
    return maps


def kernel(**inputs):
    nc, es = build()
    maps = make_in_maps(inputs)
    res = run_bass_kernel_spmd(nc, maps, core_ids=list(range(8)))
    return np.stack([np.asarray(r["y"], dtype=np.float32) for r in res.results], axis=0)
```

```python
import numpy as np
OPT = "sp"
from contextlib import ExitStack
import concourse.bass as bass
import concourse.mybir as mybir
from concourse.bass_utils import run_bass_kernel_spmd

F32 = mybir.dt.float32
BF16 = mybir.dt.bfloat16
AF = mybir.ActivationFunctionType
ALU = mybir.AluOpType

S = 2048
D = 1024
HALF = 1024
EPS = 1e-6
NSLOT = 4
FAST_RECIP = True
SEM_CHUNK = 30000


class Tok:
    __slots__ = ("eng", "sem", "val")

    def __init__(self, eng):
        self.eng = eng
        self.sem = None
        self.val = None


class Reg:
    __slots__ = ("w", "rs", "name")

    def __init__(self, name=""):
        self.w = None
        self.rs = {}
        self.name = name


class Sched:
    def __init__(self, nc, es):
        self.nc = nc
        self.es = es
        self.E = {"pe": nc.tensor, "act": nc.scalar, "dve": nc.vector, "pool": nc.gpsimd, "sp": nc.sync}
        self.sems = {}
        self.cnt = {}
        self.pending = {k: [] for k in self.E}
        self.seen = {k: {} for k in self.E}
        self.nsem = 0
        for k in self.E:
            self._newsem(k)
        self.dsem = {}

    def _newsem(self, k):
        self.nsem += 1
        self.sems[k] = self.nc.alloc_semaphore(name=f"s_{k}_{self.nsem}")
        self.cnt[k] = 0

    def _wait(self, eng, toks):
        best = {}
        for t in toks:
            if t is None:
                continue
            assert t.sem is not None, f"unresolved token from {t.eng} needed by {eng}"
            key = t.sem
            if key not in best or best[key].val < t.val:
                best[key] = t
        for key, t in best.items():
            if self.seen[eng].get(key, 0) >= t.val:
                continue
            self.E[eng].wait_ge(t.sem, t.val)
            self.seen[eng][key] = t.val

    def _deps(self, eng, reads, writes):
        deps = []
        for r in reads:
            if r.w is not None:
                if r.w.eng == eng and eng == "pe":
                    continue
                deps.append(r.w)
        for r in writes:
            if r.w is not None and not (r.w.eng == eng and eng == "pe"):
                deps.append(r.w)
            for e2, t in r.rs.items():
                if not (e2 == eng and eng == "pe"):
                    deps.append(t)
        return deps

    def _mark(self, tok, reads, writes):
        for r in reads:
            r.rs[tok.eng] = tok
        for r in writes:
            r.w = tok
            r.rs = {}

    def op(self, eng, fn, reads=(), writes=(), signal=True):
        self._wait(eng, self._deps(eng, reads, writes))
        inst = fn(self.E[eng])
        tok = Tok(eng)
        if signal:
            if self.cnt[eng] >= SEM_CHUNK:
                self._newsem(eng)
            self.cnt[eng] += 1
            inst.then_inc(self.sems[eng], 1)
            tok.sem = self.sems[eng]
            tok.val = self.cnt[eng]
            for p in self.pending[eng]:
                p.sem = tok.sem
                p.val = tok.val
            self.pending[eng] = []
        else:
            self.pending[eng].append(tok)
        self._mark(tok, reads, writes)
        return tok

    def dma(self, q, out, in_, semname, reads=(), writes=()):
        self._wait(q, self._deps("dma:" + semname, reads, writes))
        if semname not in self.dsem:
            self.dsem[semname] = [self.nc.alloc_semaphore(name="d_" + semname), 0]
        ent = self.dsem[semname]
        ent[1] += 16
        self.E[q].dma_start(out=out, in_=in_).then_inc(ent[0], 16)
        tok = Tok("dma:" + semname)
        tok.sem = ent[0]
        tok.val = ent[1]
        self._mark(tok, reads, writes)
        return tok

    def wait_all(self, eng, regs):
        toks = []
        for r in regs:
            if r.w is not None:
                toks.append(r.w)
        self._wait(eng, toks)


def _kc(w, k0, nk, c0, ncol):
    return w[k0 * 128:(k0 + nk) * 128, c0:c0 + ncol].reshape(nk, 128, ncol).transpose(1, 0, 2)


def pack_weights(a_w_in, a_w_out, w_kv, b_w_in, b_w_out, ple_w, ple_gate_w):
    units = []
    idx = {}

    def add(a):
        units.append(np.ascontiguousarray(a, dtype=np.float32).reshape(128, 2048))

    def ple(l):
        idx["PW", l] = len(units)
        for kh in range(2):
            add(_kc(ple_gate_w[l], kh * 4, 4, 0, 512))
        add(np.stack([_kc(ple_w[l], 0, 2, dc * 512, 512) for dc in range(2)], axis=1))
        for kh in range(2):
            add(_kc(ple_gate_w[l], kh * 4, 4, 512, 512))

    for l in range(2):
        w = a_w_in[l]
        idx["WV", l] = len(units)
        for c in range(4):
            for kh in range(2):
                add(_kc(w, kh * 4, 4, 2048 + c * 512, 512))
        idx["WUG", l] = len(units)
        for j in range(16):
            add(np.concatenate([_kc(w, 0, 8, j * 128, 128), _kc(w, 0, 8, 4096 + j * 128, 128)], axis=2))
        idx["WO", l] = len(units)
        for dc in range(2):
            for kq in range(4):
                add(_kc(a_w_out[l], kq * 4, 4, dc * 512, 512))
        ple(l)
    idx["WK"] = len(units)
    for hp in range(4):
        add(np.stack([_kc(w_kv, 0, 8, (hp * 2 + e) * 128, 128) for e in range(2)], axis=1))
    idx["WVV"] = len(units)
    for vc in range(2):
        for kh in range(2):
            add(_kc(w_kv, kh * 4, 4, 1024 + vc * 512, 512))
    for j in range(2):
        l = 2 + j
        idx["QG", l] = len(units)
        for hd in range(8):
            add(np.concatenate([_kc(b_w_in[j], 0, 8, hd * 128, 128), _kc(b_w_in[j], 0, 8, 1024 + hd * 128, 128)], axis=2))
        idx["BO", l] = len(units)
        for dc in range(2):
            for kh in range(2):
                add(_kc(b_w_out[j], kh * 4, 4, dc * 512, 512))
        ple(l)
    return np.stack(units, axis=0).reshape(len(units) * 128, 2048), idx


VOFF = {}
_o = 0
for _n, _sz in [("ng0", 1024), ("ng1", 1024), ("ng2", 1024), ("ng3", 1024), ("kvg", 1024), ("fing", 1024),
                ("lng0", 2048), ("lng1", 2048), ("lnb0", 2048), ("lnb1", 2048), ("bs0", 1024), ("bs1", 1024)]:
    VOFF[_n] = _o
    _o += _sz
NV = _o


def pack_vecs(norm_g, kv_norm_g, final_g, a_ln_g, a_ln_b, a_b_s):
    v = np.concatenate([norm_g[0], norm_g[1], norm_g[2], norm_g[3], kv_norm_g, final_g,
                        a_ln_g[0], a_ln_g[1], a_ln_b[0], a_ln_b[1],
                        a_b_s[0].reshape(-1), a_b_s[1].reshape(-1)]).astype(np.float32)
    return np.ascontiguousarray(np.broadcast_to(v[None, :], (128, NV)))


def make_consts():
    c = np.zeros((128, 512), np.float32)
    i = np.arange(128)
    c[:, 0:128] = np.eye(128)
    c[:, 128:256] = -1.0 * (i[:, None] >= i[None, :])
    c[:, 256:384] = -1.0
    c[:, 384:512] = -30000.0 * (i[:, None] >= i[None, :])
    return c


NUNITS = 2 * (8 + 16 + 8 + 5) + 8 + 2 * (8 + 4 + 5)


def build(layers=(0, 1, 2, 3), final=True, dbg=False):
    nc = bass.Bass("TRN2", target_bir_lowering=False)
    es = ExitStack()
    sc = Sched(nc, es)
    _, widx = pack_weights(*[np.zeros(s, np.float32) for s in
                             [(2, 1024, 6144), (2, 2048, 1024), (1024, 2048), (2, 1024, 2048),
                              (2, 1024, 1024), (4, 256, 1024), (4, 1024, 1024)]])

    x_d = nc.dram_tensor("x", [S, D], F32, kind="ExternalInput").ap()
    pT_d = nc.dram_tensor("pT", [4 * 128, 2 * S], F32, kind="ExternalInput").ap()
    wts_d = nc.dram_tensor("wts", [NUNITS * 128, 2048], F32, kind="ExternalInput").ap()
    vecs_d = nc.dram_tensor("vecs", [128, NV], F32, kind="ExternalInput").ap()
    wst_d = nc.dram_tensor("wst", [2 * 128, 1024], F32, kind="ExternalInput").ap()
    cst_d = nc.dram_tensor("cst", [128, 512], F32, kind="ExternalInput").ap()
    y_d = nc.dram_tensor("y", [S, D], F32, kind="ExternalOutput").ap()
    dbg_d = {}
    rDBG = Reg("dbg")

    def dump(name, ap, regs):
        if not dbg:
            return
        shp = list(ap.shape)
        dbg_d[name] = nc.dram_tensor("dbg_" + name, shp, F32, kind="ExternalOutput").ap()
        sc.dma("pool", dbg_d[name], ap, "dbg", reads=regs, writes=[rDBG])

    def sb(name, shape, dt):
        return es.enter_context(nc.sbuf_tensor(name, shape, dt))

    H = sb("H", [128, 16 * 1024], F32)
    HT = sb("HT", [128, 8 * HALF], BF16)
    OGT = sb("OGT", [128, 8 * HALF], BF16)
    X = sb("X", [128, 32768], BF16)
    RING = sb("RING", [128, NSLOT * 2048], BF16)
    TAA = sb("TAA", [128, 1024], F32)
    GB = TAA
    TA = [TAA[:, 0:512], TAA[:, 512:1024]]
    HN = [sb(f"HN{i}", [128, 1024], BF16) for i in range(2)]
    CST = sb("CST", [128, 512], BF16)
    PT = sb("PT", [128, 2 * HALF], BF16)
    SS = sb("SS", [128, 16], F32)
    RS = sb("RS", [128, 16], F32)
    TBB = sb("TBB", [128, 2048], BF16)
    TB = [TBB[:, i * 512:(i + 1) * 512] for i in range(4)]
    PS = [es.enter_context(nc.psum_tensor(f"ps{i}", [128, 512], F32)) for i in range(8)]
    NQ = 512
    LSUM = sb("LSUM", [128, NQ], F32)
    LSUMB = [sb(f"LSUMB{i}", [128, NQ], BF16) for i in range(2)]
    EE = sb("EE", [128, NQ], F32)
    LP = [sb(f"LP{i}", [128, NQ], BF16) for i in range(3)]
    AT = [sb(f"AT{i}", [128, NQ], BF16) for i in range(2)]
    QT = [PT[:, 0:HALF], PT[:, HALF:2 * HALF]]
    SG = [TBB[:, 0:HALF], TBB[:, HALF:2 * HALF]]
    OGF = OGT[:, :].bitcast(F32)
    LNG = OGF[:, 0:2048]
    LNB = OGF[:, 2048:4096]
    XB = X[:, 16384:32768]
    VHAT2 = [XB[:, 0:2048], XB[:, 8192:10240]]
    WST = XB[:, 2048:3072]
    BSH = XB[:, 3072:4096]
    BSL = XB[:, 4096:5120]
    BSHL = XB[0:64, 5120:6144]
    ONES = XB[0:64, 6144:6272]
    BSR = XB[0:64, 3072:5120]
    BSF = XB[:, 8192:10240].bitcast(F32)
    SML = sb("SML", [128, 64], F32)
    ST8 = sb("ST8", [128, 192], F32)
    ST = SML[:, 0:24]
    MV8 = SML[:, 24:40]
    RSV8 = SML[:, 40:48]
    NMR8 = SML[:, 48:56]

    H3 = H[:, :].rearrange("p (i d) -> p i d", d=1024)
    HT3 = HT[:, :].rearrange("p (k t) -> p k t", t=HALF)
    OGT3 = OGT[:, :].rearrange("p (k t) -> p k t", t=HALF)
    PT3 = PT[:, :].rearrange("p (k t) -> p k t", t=HALF)
    IDENT = CST[:, 0:128]
    NEGTRI = CST[:, 128:256]
    NEGONES = CST[:, 256:384]
    NEGMASK = CST[:, 384:512]

    rH = [Reg(f"H{i}") for i in range(16)]
    rHT = [Reg(f"HT{i}") for i in range(8)]
    rOGT = [Reg(f"OGT{h}") for h in range(8)]
    rHN = [Reg(), Reg()]
    rCST = Reg("CST")
    rSS = Reg("SS")
    rRS = Reg("RS")
    rTA = [Reg(), Reg()]
    rTB = [Reg() for _ in range(4)]
    rGBL = rTA
    rQT = [Reg(), Reg()]
    rPTL = rQT
    rSG = [[rTB[0], rTB[1]], [rTB[2], rTB[3]]]
    rLS = Reg(); rLSB = [Reg(), Reg()]; rEE = Reg()
    rLP = [Reg(), Reg(), Reg()]; rAT = [Reg(), Reg()]
    rLNG = Reg(); rLNB = Reg(); rWST = Reg(); rBS = Reg(); rVH2 = [Reg(), rBS]; rBSHL = Reg(); rBSR = Reg()
    rST = Reg(); rMV = Reg(); rSV = Reg()
    rST8 = [Reg() for _ in range(8)]
    rX = [Reg(f"X{i}") for i in range(8)]
    rPS = [Reg(f"ps{i}") for i in range(8)]
    rY = [Reg(f"Y{i}") for i in range(16)]

    cnt = {"ta": 0, "tb": 0, "hn": 0, "bank": 0, "lp": 0, "at": 0, "ptmp": 0, "ta4": 0}

    def nxt(key, n):
        v = cnt[key] % n
        cnt[key] += 1
        return v

    bank_set = [list(range(8))]

    def bank():
        bs = bank_set[0]
        return bs[nxt("bank", len(bs))]

    seq = []
    seqA = []
    XBr = X[:, 16384:32768]
    slot_ap = [RING[:, i * 2048:(i + 1) * 2048] for i in range(NSLOT)] + \
              [XBr[:, 10240 + i * 2048:10240 + (i + 1) * 2048] for i in range(3)]
    NS_A = NSLOT + 3
    rSLOT = [Reg(f"slot{i}") for i in range(NS_A)]
    slot_occ = [-1] * NS_A
    unit_slot = {}
    ring_state = {"issued": 0, "pos": 0, "done": 0, "last": -1}

    def ring_try_issue(q):
        pool = NS_A if seqA[q] else NSLOT
        for d_ in range(1, pool + 1):
            s = (ring_state["last"] + d_) % pool
            if slot_occ[s] < ring_state["done"]:
                break
        else:
            return False
        u = seq[q]
        sc.dma("pool", slot_ap[s], wts_d[u * 128:(u + 1) * 128, :], f"ring{s}", writes=[rSLOT[s]])
        slot_occ[s] = q
        unit_slot[q] = s
        ring_state["last"] = s
        ring_state["issued"] += 1
        return True

    def ring_begin(keep=0):
        ring_state["done"] = ring_state["pos"] - keep

    def ring_next(u):
        q = ring_state["pos"]
        assert seq[q] == u, (q, seq[q], u)
        while ring_state["issued"] <= q:
            ok = ring_try_issue(ring_state["issued"])
            assert ok, "ring: no free slot for a unit that is needed now"
        ring_state["pos"] += 1
        return unit_slot[q]

    def ring_prefetch(*_):
        while ring_state["issued"] < len(seq) and ring_state["issued"] < ring_state["pos"] + NS_A:
            if not ring_try_issue(ring_state["issued"]):
                break

    def plan():
        def ext(r, isa):
            seq.extend(r)
            seqA.extend([isa] * len(r))
        for l in layers:
            for hf in range(2):
                if l < 2:
                    ext(range(widx["WV", l], widx["WV", l] + 8), l < 2)
                    ext(range(widx["WUG", l], widx["WUG", l] + 16), l < 2)
                    ext(range(widx["WO", l], widx["WO", l] + 8), l < 2)
                else:
                    if l == 2 or (l == 3 and 2 not in layers):
                        pass
                    ext(range(widx["QG", l], widx["QG", l] + 8), l < 2)
                    ext(range(widx["BO", l], widx["BO", l] + 4), l < 2)
                ext(range(widx["PW", l], widx["PW", l] + 5), l < 2)
            if l == 1 and (2 in layers or 3 in layers):
                for hf in range(2):
                    ext(range(widx["WK"], widx["WK"] + 4), False)
                    ext(range(widx["WVV"], widx["WVV"] + 4), False)

    def slot3(slot, k):
        return slot_ap[slot].rearrange("p (k c) -> p k c", k=k)

    sc.dma("pool", CST[:, :], cst_d[:, :], "cst", writes=[rCST])
    for i in range(16):
        sc.dma("sp", H3[:, i, :], x_d[i * 128:(i + 1) * 128, :], f"x{i}", writes=[rH[i]])

    JUNK = EE[:, :].bitcast(BF16)
    stats_ready = [False, False]

    def norm_stats(hf):
        if stats_ready[hf]:
            return
        for ii in range(8):
            i = hf * 8 + ii
            sc.op("act", lambda e: e.activation(out=JUNK, in_=H3[:, i, :], func=AF.Square, accum_out=SS[:, i:i + 1]),
                  reads=[rH[i]], writes=[rEE, rSS])
        sl = slice(hf * 8, hf * 8 + 8)
        sc.op("dve", lambda e: e.tensor_scalar(out=RS[:, sl], in0=SS[:, sl], scalar1=1.0 / 1024, scalar2=EPS,
                                                op0=ALU.mult, op1=ALU.add), reads=[rSS], writes=[rRS])
        sc.op("act", lambda e: e.activation(out=RS[:, sl], in_=RS[:, sl], func=AF.Sqrt), reads=[rRS], writes=[rRS])
        sc.op("dve", lambda e: e.reciprocal(out=RS[:, sl], in_=RS[:, sl]), reads=[rRS], writes=[rRS])
        stats_ready[hf] = True

    def phase_norm(goff, hf):
        if gb_loaded["goff"] == goff:
            gb_loaded["goff"] = None
        else:
            sc.dma("sp", GB[:, :], vecs_d[:, goff:goff + 1024], "gb", writes=rGBL)
        norm_stats(hf)
        for ii in range(8):
            i = hf * 8 + ii
            hb = ii % 2
            sc.op("dve", lambda e: e.scalar_tensor_tensor(out=HN[hb][:, :], in0=H3[:, i, :], scalar=RS[:, i:i + 1],
                                                           in1=GB[:, :], op0=ALU.mult, op1=ALU.mult),
                  reads=[rH[i], rRS] + rGBL, writes=[rHN[hb]])
            transpose_tile(HN[hb], rHN[hb], 8, None, [rHT[ii]], HT3[:, :, ii * 128:(ii + 1) * 128])

    def ple_prep_tile(hf, ii):
        i = hf * 8 + ii
        hb = nxt("hn", 2)
        sc.op("dve", lambda e: e.tensor_copy(out=HN[hb][:, :], in_=H3[:, i, :]), reads=[rH[i]], writes=[rHN[hb]])
        transpose_tile(HN[hb], rHN[hb], 8, None, [rHT[ii]], HT3[:, :, ii * 128:(ii + 1) * 128])

    def ple_load_pt(l, hf):
        sc.dma("pool", PT3[:, :, :], pT_d[l * 128:(l + 1) * 128, :].rearrange("p (k t) -> p k t", t=S)[:, :, hf * HALF:(hf + 1) * HALF],
               "pt", writes=rPTL)

    def transpose_tile(src, rsrc, nk, dst_fn, wregs, dst_all):
        b = bank()
        pv = PS[b][:, :].bitcast(BF16).rearrange("p (k t) -> p k t", t=128)
        for k in range(nk):
            sc.op("pe", lambda e: e.transpose(out=pv[:, k, :], in_=src[:, k * 128:(k + 1) * 128], identity=IDENT),
                  reads=[rsrc, rCST], writes=[rPS[b]], signal=(k == nk - 1))
        sc.op("act", lambda e: e.copy(out=dst_all, in_=pv[:, 0:nk, :]), reads=[rPS[b]], writes=wregs)

    def phase_A(l, hf):
        if True:
            X3 = X[:, 0:16384].rearrange("p (i f) -> p i f", f=2048)
            WST3 = WST.rearrange("p (g t) -> p g t", t=128)
            BSHL3 = BSHL.rearrange("p (g t) -> p g t", t=128)

            if hf == 0:
                sc.dma("sp", LNG, vecs_d[:, VOFF[f"lng{l}"]:VOFF[f"lng{l}"] + 2048], "lng", writes=[rLNG])
                sc.dma("sp", LNB, vecs_d[:, VOFF[f"lnb{l}"]:VOFF[f"lnb{l}"] + 2048], "lnb", writes=[rLNB])
                sc.dma("pool", WST, wst_d[l * 128:(l + 1) * 128, :], "wst", writes=[rWST])
                sc.op("dve", lambda e: e.memset(WST3[64:128, :, 0:64], 0.0), writes=[rWST])
                sc.dma("sp", BSF, vecs_d[:, VOFF[f"bs{l}"]:VOFF[f"bs{l}"] + 1024], "bsf", writes=[rBS])
                sc.op("dve", lambda e: e.tensor_copy(out=BSH, in_=BSF), reads=[rBS], writes=[rBS])
                sc.op("dve", lambda e: e.tensor_tensor(out=BSF, in0=BSF, in1=BSH, op=ALU.subtract),
                      reads=[rBS], writes=[rBS])
                sc.op("dve", lambda e: e.tensor_copy(out=BSL, in_=BSF), reads=[rBS], writes=[rBS])
                sc.op("dve", lambda e: e.memset(BSHL, 0.0), writes=[rBSHL])
                sc.op("dve", lambda e: e.memset(ONES, 0.0), writes=[rBSHL])
                sc.op("dve", lambda e: e.tensor_copy(out=BSHL[0:1, :], in_=BSH[0:1, :]), reads=[rBS], writes=[rBSHL])
                sc.op("dve", lambda e: e.tensor_copy(out=BSHL[32:33, :], in_=BSL[32:33, :]), reads=[rBS], writes=[rBSHL])
                sc.op("dve", lambda e: e.memset(ONES[0:1, :], 1.0), writes=[rBSHL])
                sc.op("dve", lambda e: e.memset(ONES[32:33, :], 1.0), writes=[rBSHL])
                BSR4 = BSR.rearrange("p (g d t) -> p g d t", d=2, t=128)
                for d_ in range(2):
                    sc.op("dve", lambda e: e.tensor_copy(out=BSR4[:, :, d_, :], in_=BSHL3), reads=[rBSHL], writes=[rBS, rBSR])

            phase_norm(VOFF[f"ng{l}"], hf)
            if "s" in OPT:
                norm_stats(1 - hf)
            if l == 0 and hf == 0:
                dump("HT", HT[:, :], rHT)

            MV83 = MV8.rearrange("p (i t) -> p i t", t=2)

            def stats_batch(t0, t1):
                sl_ = slice(t0, t1)
                sc.op("dve", lambda e: e.tensor_scalar(out=RSV8[:, sl_], in0=MV83[:, sl_, 1], scalar1=EPS, scalar2=None, op0=ALU.add),
                      reads=[rMV], writes=[rSV])
                sc.op("act", lambda e: e.activation(out=RSV8[:, sl_], in_=RSV8[:, sl_], func=AF.Sqrt), reads=[rSV], writes=[rSV])
                sc.op("dve", lambda e: e.reciprocal(out=RSV8[:, sl_], in_=RSV8[:, sl_]), reads=[rSV], writes=[rSV])
                sc.op("dve", lambda e: e.scalar_tensor_tensor(out=NMR8[:, sl_], in0=MV83[:, sl_, 0], scalar=-1.0, in1=RSV8[:, sl_],
                                                               op0=ALU.mult, op1=ALU.mult), reads=[rMV, rSV], writes=[rSV])

            TA4 = [TA[0], TA[1], EE[:, :], LSUM[:, :]]
            rTA4 = [rTA[0], rTA[1], rEE, rLS]

            def ps_produce(ii):
                VHAT = VHAT2[ii % 2]
                rVH = rVH2[ii % 2]
                for q in range(4):
                    ta = nxt("ta4", 4)
                    T_, rT_ = TA4[ta], rTA4[ta]
                    qs = slice(q * 512, (q + 1) * 512)
                    sc.op("act", lambda e: e.activation(out=T_, in_=X3[:, ii, qs], func=AF.Identity,
                                                         bias=NMR8[:, ii:ii + 1], scale=RSV8[:, ii:ii + 1]),
                          reads=[rX[ii], rSV], writes=[rT_])
                    sc.op("dve", lambda e: e.tensor_tensor(out=T_, in0=T_, in1=LNG[:, qs], op=ALU.mult),
                          reads=[rT_, rLNG], writes=[rT_])
                    sc.op("dve", lambda e: e.tensor_tensor(out=VHAT[:, qs], in0=T_, in1=LNB[:, qs], op=ALU.add),
                          reads=[rT_, rLNB], writes=[rVH])

            def ps_consume(ii):
                VHAT = VHAT2[ii % 2]
                rVH = rVH2[ii % 2]
                Xs = X3[:, ii, :].rearrange("p (j t) -> p j t", t=128)
                for jb in range(4):
                    b = bank()
                    sc.op("pe", lambda e: e.matmul(PS[b][:, :], lhsT=ONES[0:33, :], rhs=BSR[0:33, jb * 512:(jb + 1) * 512],
                                                    start=True, stop=False, skip_group_check=True),
                          reads=[rBSHL, rBSR], writes=[rPS[b]], signal=False)
                    for jj in range(4):
                        j = jb * 4 + jj
                        g = j // 2
                        sc.op("pe", lambda e: e.matmul(PS[b][:, jj * 128:(jj + 1) * 128], lhsT=VHAT[:, j * 128:(j + 1) * 128],
                                                        rhs=WST3[:, g, :], start=False, stop=(jj == 3), skip_group_check=True),
                              reads=[rVH, rWST], writes=[rPS[b]], signal=(jj == 3))
                    sc.op("act", lambda e: e.copy(out=Xs[:, jb * 4:(jb + 1) * 4, :],
                                                   in_=PS[b][:, :].rearrange("p (j t) -> p j t", t=128)),
                          reads=[rPS[b]], writes=[rX[ii]])

            for c in range(4):
                ring_begin()
                sA = ring_next(widx["WV", l] + c * 2)
                sB = ring_next(widx["WV", l] + c * 2 + 1)
                ring_prefetch(2)
                for ii in range(8):
                    b = bank()
                    for k in range(8):
                        s_ = sA if k < 4 else sB
                        sc.op("pe", lambda e: e.matmul(PS[b][:, :], lhsT=HT3[:, k, ii * 128:(ii + 1) * 128],
                                                        rhs=slot3(s_, 4)[:, k % 4, :], start=(k == 0), stop=(k == 7)),
                              reads=[rHT[ii], rSLOT[s_]], writes=[rPS[b]], signal=(k == 7))
                    sc.op("act", lambda e: e.activation(out=X3[:, ii, c * 512:(c + 1) * 512], in_=PS[b][:, :], func=AF.Gelu),
                          reads=[rPS[b]], writes=[rX[ii]])
                    sc.op("dve", lambda e: e.bn_stats(out=ST8[:, ii * 24 + c * 6:ii * 24 + (c + 1) * 6],
                                                       in_=X3[:, ii, c * 512:(c + 1) * 512]),
                          reads=[rX[ii]], writes=[rST8[ii]])
                    if c == 3:
                        sc.op("dve", lambda e: e.bn_aggr(out=MV8[:, ii * 2:(ii + 1) * 2], in_=ST8[:, ii * 24:(ii + 1) * 24]),
                              reads=[rST8[ii]], writes=[rMV])
                        if ii == 3:
                            stats_batch(0, 4)
                            ps_produce(0)
                            ps_produce(1)
                        if ii == 7:
                            stats_batch(4, 8)
            if l == 0 and hf == 0:
                dump("GV", X[:, 0:16384], rX)
            for ii in range(8):
                ps_consume(ii)
                if ii + 2 < 8:
                    ps_produce(ii + 2)

            if l == 0 and hf == 0:
                dump("SVT", X[:, 0:16384], rX)
            for j in range(16):
                ring_begin()
                s_ = ring_next(widx["WUG", l] + j)
                ring_prefetch(1)
                w3 = slot3(s_, 8)
                for st in range(2):
                    bu = bank()
                    for k in range(8):
                        sc.op("pe", lambda e: e.matmul(PS[bu][:, :], lhsT=w3[:, k, 0:128], rhs=HT3[:, k, st * 512:(st + 1) * 512],
                                                        start=(k == 0), stop=(k == 7)),
                              reads=[rSLOT[s_]] + rHT[st * 4:(st + 1) * 4], writes=[rPS[bu]], signal=(k == 7))
                    bg = bank()
                    for k in range(8):
                        sc.op("pe", lambda e: e.matmul(PS[bg][:, :], lhsT=w3[:, k, 128:256], rhs=HT3[:, k, st * 512:(st + 1) * 512],
                                                        start=(k == 0), stop=(k == 7)),
                              reads=[rSLOT[s_]] + rHT[st * 4:(st + 1) * 4], writes=[rPS[bg]], signal=(k == 7))
                    tb = nxt("tb", 4)
                    sc.op("act", lambda e: e.activation(out=TB[tb][:, :], in_=PS[bu][:, :], func=AF.Gelu),
                          reads=[rPS[bu]], writes=[rTB[tb]])
                    ta2 = nxt("ta", 2)
                    tb2 = nxt("tb", 4)
                    sc.op("act", lambda e: e.activation(out=TA[ta2][:, :], in_=PS[bg][:, :], func=AF.Tanh, scale=0.5),
                          reads=[rPS[bg]], writes=[rTA[ta2]])
                    sc.op("dve", lambda e: e.scalar_tensor_tensor(out=TB[tb2][:, :], in0=TA[ta2][:, :], scalar=1.0, in1=PS[bg][:, :],
                                                                   op0=ALU.add, op1=ALU.mult),
                          reads=[rTA[ta2], rPS[bg]], writes=[rTB[tb2]])
                    sc.op("dve", lambda e: e.tensor_tensor(out=TB[tb][:, :], in0=TB[tb][:, :], in1=TB[tb2][:, :], op=ALU.mult),
                          reads=[rTB[tb], rTB[tb2]], writes=[rTB[tb]])
                    xv = X3[:, st * 4:(st + 1) * 4, j * 128:(j + 1) * 128]
                    sc.op("dve", lambda e: e.scalar_tensor_tensor(out=xv, in0=TB[tb][:, :].rearrange("p (i t) -> p i t", t=128),
                                                                   scalar=0.5, in1=xv, op0=ALU.mult, op1=ALU.mult),
                          reads=[rTB[tb]] + rX[st * 4:(st + 1) * 4], writes=rX[st * 4:(st + 1) * 4])

            if l == 0 and hf == 0:
                dump("GAT", X[:, 0:16384], rX)
            ple_load_pt(l, hf)
            for dc in range(2):
                ring_begin()
                ss_ = [ring_next(widx["WO", l] + dc * 4 + kq) for kq in range(4)]
                ring_prefetch(4)
                for ii in range(8):
                    i = hf * 8 + ii
                    b = bank()
                    for k in range(16):
                        s_ = ss_[k // 4]
                        sc.op("pe", lambda e: e.matmul(PS[b][:, :], lhsT=X3[:, ii, k * 128:(k + 1) * 128],
                                                        rhs=slot3(s_, 4)[:, k % 4, :], start=(k == 0), stop=(k == 15)),
                              reads=[rX[ii], rSLOT[s_]], writes=[rPS[b]], signal=(k == 15))
                    hv = H3[:, i, dc * 512:(dc + 1) * 512]
                    sc.op("dve", lambda e: e.tensor_tensor(out=hv, in0=hv, in1=PS[b][:, :], op=ALU.add),
                          reads=[rPS[b], rH[i]], writes=[rH[i]])
                    if dc == 1 and "p" in OPT:
                        if ii >= 1:
                            ple_prep_tile(hf, ii - 1)
                        if ii == 7:
                            ple_prep_tile(hf, 7)
        if l == 0 and hf == 0:
            dump("H1", H[:, 0:8192], rH[0:8])
        phase_ple(l, hf)

    gb_loaded = {"goff": None}
    final_done = [False, False]
    rFS = [Reg(f"fs{i}") for i in range(16)]

    def final_tile(i):
        hb = i % 2
        sc.op("act", lambda e: e.activation(out=HN[hb][:, :], in_=H3[:, i, :], func=AF.Square, accum_out=SS[:, i:i + 1]),
              reads=[rH[i]], writes=[rHN[hb], rFS[i]])
        sc.op("dve", lambda e: e.tensor_scalar(out=RS[:, i:i + 1], in0=SS[:, i:i + 1], scalar1=1.0 / 1024, scalar2=EPS,
                                                op0=ALU.mult, op1=ALU.add), reads=[rFS[i]], writes=[rFS[i]])
        sc.op("act", lambda e: e.activation(out=RS[:, i:i + 1], in_=RS[:, i:i + 1], func=AF.Sqrt), reads=[rFS[i]], writes=[rFS[i]])
        sc.op("dve", lambda e: e.reciprocal(out=RS[:, i:i + 1], in_=RS[:, i:i + 1]), reads=[rFS[i]], writes=[rFS[i]])
        sc.op("dve", lambda e: e.scalar_tensor_tensor(out=H3[:, i, :], in0=H3[:, i, :], scalar=RS[:, i:i + 1], in1=GB[:, :],
                                                       op0=ALU.mult, op1=ALU.mult), reads=[rH[i], rFS[i]] + rGBL, writes=[rH[i]])
        sc.dma("sp", y_d[i * 128:(i + 1) * 128, :], H3[:, i, :], "y", reads=[rH[i]], writes=[rY[i]])

    def phase_ple(l, hf):
        nn = next_norm_of.get((l, hf))
        fin_here = final and l == 3 and hf == 1
        if nn is not None:
            sc.dma("sp", GB[:, :], vecs_d[:, nn[0]:nn[0] + 1024], "gb", writes=rGBL)
            gb_loaded["goff"] = nn[0]
        elif fin_here:
            sc.dma("sp", GB[:, :], vecs_d[:, VOFF["fing"]:VOFF["fing"] + 1024], "gb", writes=rGBL)
        if "p" not in OPT:
            for ii in range(8):
                ple_prep_tile(hf, ii)
        base = widx["PW", l]
        sw = None
        wp = None
        for dc in range(2):
            if dc == 0:
                ring_begin()
                sg = [ring_next(base + 0), ring_next(base + 1)]
                sw = ring_next(base + 2)
                wp = slot_ap[sw].rearrange("p (d k c) -> p d k c", d=2, k=2)
            else:
                ring_begin(keep=1)
                sg = [ring_next(base + 3), ring_next(base + 4)]
            ring_prefetch(3)
            for ii in range(8):
                i = hf * 8 + ii
                bg = bank()
                for k in range(8):
                    s_ = sg[k // 4]
                    sc.op("pe", lambda e: e.matmul(PS[bg][:, :], lhsT=HT3[:, k, ii * 128:(ii + 1) * 128],
                                                    rhs=slot3(s_, 4)[:, k % 4, :], start=(k == 0), stop=(k == 7)),
                          reads=[rHT[ii], rSLOT[s_]], writes=[rPS[bg]], signal=(k == 7))
                bp = bank()
                for k in range(2):
                    sc.op("pe", lambda e: e.matmul(PS[bp][:, :], lhsT=PT3[:, k, ii * 128:(ii + 1) * 128],
                                                    rhs=wp[:, dc, k, :], start=(k == 0), stop=(k == 1)),
                          reads=rPTL + [rSLOT[sw]], writes=[rPS[bp]], signal=(k == 1))
                tp_ = nxt("ptmp", 2)
                T_ = (EE, LSUM)[tp_]
                rT_ = (rEE, rLS)[tp_]
                sc.op("act", lambda e: e.activation(out=T_[:, :], in_=PS[bg][:, :], func=AF.Tanh, scale=0.5),
                      reads=[rPS[bg]], writes=[rT_])
                sc.op("dve", lambda e: e.scalar_tensor_tensor(out=T_[:, :], in0=T_[:, :], scalar=1.0, in1=PS[bp][:, :],
                                                               op0=ALU.add, op1=ALU.mult),
                      reads=[rT_, rPS[bp]], writes=[rT_])
                hv = H3[:, i, dc * 512:(dc + 1) * 512]
                sc.op("dve", lambda e: e.scalar_tensor_tensor(out=hv, in0=T_[:, :], scalar=0.5, in1=hv, op0=ALU.mult, op1=ALU.add),
                      reads=[rT_, rH[i]], writes=[rH[i]])
                if fin_here and dc == 1:
                    final_tile(i)
        stats_ready[hf] = False
        if fin_here:
            final_done[hf] = True

    KT3 = X[:, 0:16384].rearrange("p (h t) -> p h t", t=S)
    V3 = X[:, 16384:32768].rearrange("p (i f) -> p i f", f=1024)
    rKT = [Reg(f"KT{h}") for h in range(8)]
    rV = [Reg(f"V{i}") for i in range(16)]

    def phase_kv(hf):
        phase_norm(VOFF["kvg"], hf)
        if hf == 0:
            norm_stats(1)
            gb_loaded["goff"] = VOFF["kvg"]
        elif 2 in layers:
            sc.dma("sp", GB[:, :], vecs_d[:, VOFF["ng2"]:VOFF["ng2"] + 1024], "gb", writes=rGBL)
            gb_loaded["goff"] = VOFF["ng2"]
        for hp in range(4):
            ring_begin()
            s_ = ring_next(widx["WK"] + hp)
            ring_prefetch(1)
            w4 = slot_ap[s_].rearrange("p (e k c) -> p e k c", e=2, k=8)
            for e_ in range(2):
                hd = hp * 2 + e_
                for st in range(2):
                    b = bank()
                    for k in range(8):
                        sc.op("pe", lambda e: e.matmul(PS[b][:, :], lhsT=w4[:, e_, k, :], rhs=HT3[:, k, st * 512:(st + 1) * 512],
                                                        start=(k == 0), stop=(k == 7)),
                              reads=[rSLOT[s_]] + rHT[st * 4:(st + 1) * 4], writes=[rPS[b]], signal=(k == 7))
                    c0 = hf * HALF + st * 512
                    sc.op("act", lambda e: e.copy(out=KT3[:, hd, c0:c0 + 512], in_=PS[b][:, :]), reads=[rPS[b]], writes=[rKT[hd]])
        for vc in range(2):
            ring_begin()
            sv_ = [ring_next(widx["WVV"] + vc * 2 + kh) for kh in range(2)]
            ring_prefetch(2)
            for ii in range(8):
                i = hf * 8 + ii
                b = bank()
                for k in range(8):
                    s_ = sv_[k // 4]
                    sc.op("pe", lambda e: e.matmul(PS[b][:, :], lhsT=HT3[:, k, ii * 128:(ii + 1) * 128],
                                                    rhs=slot3(s_, 4)[:, k % 4, :], start=(k == 0), stop=(k == 7)),
                          reads=[rHT[ii], rSLOT[s_]], writes=[rPS[b]], signal=(k == 7))
                sc.op("dve", lambda e: e.tensor_copy(out=V3[:, i, vc * 512:(vc + 1) * 512], in_=PS[b][:, :]),
                      reads=[rPS[b]], writes=[rV[i]] + rSLOT[NSLOT:])

    def phase_B(l, hf):
        if True:
            SCALE = 1.0 / np.sqrt(128.0)

            phase_norm(VOFF[f"ng{l}"], hf)
            if "s" in OPT:
                norm_stats(1 - hf)
            if final and l == 3 and hf == 1:
                final_half(0)
            bank_set[0] = [0, 1, 2, 3, 6]

            def proj_items(hd, per):
                st8 = {}
                qb = hd % 2

                def start():
                    ring_begin()
                    st8["s"] = ring_next(widx["QG", l] + hd)
                    ring_prefetch()

                def group(st, isg):
                    g8 = {}

                    def mm(k0, k1):
                        def f():
                            if "s" not in st8:
                                start()
                            s_ = st8["s"]
                            w3 = slot3(s_, 8)
                            if "b" not in g8:
                                g8["b"] = bank()
                            b = g8["b"]
                            co = 128 if isg else 0
                            for k in range(k0, k1):
                                sc.op("pe", lambda e: e.matmul(PS[b][:, :], lhsT=w3[:, k, co:co + 128],
                                                                rhs=HT3[:, k, st * 512:(st + 1) * 512], start=(k == 0), stop=(k == 7)),
                                      reads=[rSLOT[s_]] + rHT[st * 4:(st + 1) * 4], writes=[rPS[b]], signal=(k == 7))
                            if k1 == 8:
                                if not isg:
                                    sc.op("dve", lambda e: e.tensor_scalar(out=QT[qb][:, st * 512:(st + 1) * 512], in0=PS[b][:, :],
                                                                            scalar1=float(SCALE), scalar2=None, op0=ALU.mult),
                                          reads=[rPS[b]], writes=[rQT[qb]])
                                else:
                                    ta = nxt("ta", 2)
                                    sc.op("act", lambda e: e.activation(out=TA[ta][:, :], in_=PS[b][:, :], func=AF.Exp, scale=-1.0),
                                          reads=[rPS[b]], writes=[rTA[ta]])
                                    sc.op("dve", lambda e: e.tensor_scalar(out=TA[ta][:, :], in0=TA[ta][:, :], scalar1=1.0, scalar2=None,
                                                                            op0=ALU.add), reads=[rTA[ta]], writes=[rTA[ta]])
                                    g8["ta"] = ta
                                    sc.op("dve", lambda e: e.tensor_copy(out=SG[qb][:, st * 512:(st + 1) * 512], in_=PS[b][:, :]),
                                          reads=[rPS[b]], writes=rSG[qb])
                        return f
                    its = [("mm", mm(k0, min(8, k0 + per)), None) for k0 in range(0, 8, per)]
                    if isg:
                        def rc(q):
                            def f():
                                ta = g8["ta"]
                                sl_ = slice(q * 128, (q + 1) * 128)
                                sc.op("dve", lambda e: e.reciprocal(out=TA[ta][:, sl_], in_=TA[ta][:, sl_]),
                                      reads=[rTA[ta]], writes=[rTA[ta]])
                            return f

                        def fin():
                            ta = g8["ta"]
                            sgv = SG[qb][:, st * 512:(st + 1) * 512]
                            sc.op("dve", lambda e: e.tensor_tensor(out=sgv, in0=sgv, in1=TA[ta][:, :], op=ALU.mult),
                                  reads=[rTA[ta]] + rSG[qb], writes=rSG[qb])
                        rdy = lambda: "ta" in g8
                        its += [("ch", rc(q), rdy) for q in range(4)] + [("ch", fin, rdy)]
                    return its
                items = []
                for st in range(2):
                    items += group(st, False)
                for st in range(2):
                    items += group(st, True)
                return items

            steps = []
            for hd in range(8):
                for c2 in range(2):
                    c = hf * 2 + c2
                    nkb = 4 * c + 4
                    prev = None
                    for kb in range(nkb - 1, -1, -1):
                        s = dict(hd=hd, c2=c2, c=c, kb=kb, first=(kb == nkb - 1), last=(kb == 0), idx=nkb - 1 - kb,
                                 prev=prev, nxt_last=(kb == 1))
                        steps.append(s)
                        prev = s
            n = len(steps)
            st_ = {"lsb": 0}

            def geom(s):
                c0 = max(0, s["kb"] - 4 * s["c"]) * 128
                return c0, slice(c0, NQ), slice(s["c2"] * 512 + c0, s["c2"] * 512 + NQ)

            def emit_Z(s):
                c0, cs, qs = geom(s)
                hd, kb, qb = s["hd"], s["kb"], s["hd"] % 2
                zb = bank()
                s["zb"] = zb
                Z = PS[zb]
                diag = kb >= 4 * s["c"]
                sc.op("pe", lambda e: e.matmul(Z[:, cs], lhsT=KT3[:, hd, kb * 128:(kb + 1) * 128], rhs=QT[qb][:, qs],
                                                start=True, stop=not diag, skip_group_check=True),
                      reads=[rKT[hd], rQT[qb]], writes=[rPS[zb]], signal=not diag)
                if diag:
                    sc.op("pe", lambda e: e.matmul(Z[:, c0:c0 + 128], lhsT=IDENT, rhs=NEGMASK, start=False, stop=True,
                                                    skip_group_check=True),
                          reads=[rCST], writes=[rPS[zb]], signal=True)

            def emit_ELP(s):
                c0, cs, qs = geom(s)
                zb = s["zb"]
                lb = nxt("lp", 3)
                s["lb"] = lb
                sc.op("act", lambda e: e.activation(out=PS[7][:, cs], in_=PS[zb][:, cs], func=AF.Exp),
                      reads=[rPS[zb]], writes=[rPS[7]])
                sc.op("act", lambda e: e.activation(out=LP[lb][:, cs], in_=PS[7][:, cs], func=AF.Ln, bias=1.0),
                      reads=[rPS[7]], writes=[rLP[lb]])

            def emit_TriOnes(s):
                c0, cs, qs = geom(s)
                zb, lb = s["zb"], s["lb"]
                Z = PS[zb]
                sc.op("pe", lambda e: e.matmul(Z[:, cs], lhsT=NEGTRI, rhs=LP[lb][:, cs], start=False, stop=s["first"],
                                                skip_group_check=True),
                      reads=[rCST, rLP[lb]], writes=[rPS[zb]], signal=s["first"])
                if not s["first"]:
                    k_ = s["prev"]["lsb_out"]
                    sc.op("pe", lambda e: e.matmul(Z[:, cs], lhsT=NEGONES, rhs=LSUMB[k_][:, cs], start=False, stop=True,
                                                    skip_group_check=True),
                          reads=[rCST, rLSB[k_]], writes=[rPS[zb]], signal=True)

            def emit_AT(s):
                c0, cs, qs = geom(s)
                zb = s["zb"]
                ab = nxt("at", 2)
                s["ab"] = ab
                sc.op("act", lambda e: e.activation(out=AT[ab][:, cs], in_=PS[zb][:, cs], func=AF.Exp),
                      reads=[rPS[zb]], writes=[rAT[ab]])

            def emit_LSUM(s):
                if s["last"]:
                    return
                c0, cs, qs = geom(s)
                lb = s["lb"]
                if s["first"]:
                    sc.op("dve", lambda e: e.memset(LSUM[:, :], 0.0), writes=[rLS])
                sc.op("dve", lambda e: e.tensor_tensor(out=LSUM[:, cs], in0=LSUM[:, cs], in1=LP[lb][:, cs], op=ALU.add),
                      reads=[rLS, rLP[lb]], writes=[rLS])
                k_ = 1 - st_["lsb"]
                sc.op("dve", lambda e: e.tensor_copy(out=LSUMB[k_][:, :], in_=LSUM[:, :]), reads=[rLS], writes=[rLSB[k_]])
                st_["lsb"] = k_
                s["lsb_out"] = k_

            def emit_AV(s):
                c0, cs, qs = geom(s)
                hd, kb, c2, qb = s["hd"], s["kb"], s["c2"], s["hd"] % 2
                ob = 4 + c2
                ab = s["ab"]
                sc.op("pe", lambda e: e.matmul(PS[ob][:, cs], lhsT=V3[:, kb, hd * 128:(hd + 1) * 128], rhs=AT[ab][:, cs],
                                                start=s["first"], stop=s["last"], skip_group_check=True),
                      reads=[rV[kb], rAT[ab]], writes=[rPS[ob]], signal=s["last"])
                if s["last"]:
                    sc.op("dve", lambda e: e.tensor_tensor(out=OGT3[:, hd, c2 * 512:(c2 + 1) * 512], in0=PS[ob][:, :],
                                                            in1=SG[qb][:, c2 * 512:(c2 + 1) * 512], op=ALU.mult),
                          reads=[rPS[ob]] + rSG[qb], writes=[rOGT[hd]])

            bg_mm = []
            bg_ch = []

            def bg_add(items):
                for kind, f, rdy in items:
                    (bg_mm if kind == "mm" else bg_ch).append((f, rdy))

            def bg_flush():
                while bg_mm:
                    bg_mm.pop(0)[0]()
                while bg_ch:
                    bg_ch.pop(0)[0]()

            def bg_step():
                if bg_mm:
                    bg_mm.pop(0)[0]()
                if bg_ch and (bg_ch[0][1] is None or bg_ch[0][1]()):
                    bg_ch.pop(0)[0]()

            bg_add(proj_items(0, 8))
            bg_flush()
            emit_Z(steps[0])
            emit_ELP(steps[0])
            emit_LSUM(steps[0])
            for i in range(n):
                s = steps[i]
                if s["first"] and s["c2"] == 0 and s["hd"] + 1 < 8:
                    bg_add(proj_items(s["hd"] + 1, 2 if hf == 1 else 4))
                if i + 1 < n:
                    s1 = steps[i + 1]
                    if s1["first"] and s1["c2"] == 0:
                        bg_flush()
                    emit_Z(s1)
                emit_TriOnes(s)
                if i + 1 < n:
                    emit_ELP(steps[i + 1])
                emit_AT(s)
                if i + 1 < n:
                    emit_LSUM(steps[i + 1])
                if i >= 1:
                    emit_AV(steps[i - 1])
                bg_step()
            emit_AV(steps[n - 1])
            bank_set[0] = list(range(8))
            ple_load_pt(l, hf)
            for dc in range(2):
                ring_begin()
                so = [ring_next(widx["BO", l] + dc * 2 + kh) for kh in range(2)]
                ring_prefetch(2)
                for ii in range(8):
                    i = hf * 8 + ii
                    b = bank()
                    for k in range(8):
                        s_ = so[k // 4]
                        sc.op("pe", lambda e: e.matmul(PS[b][:, :], lhsT=OGT3[:, k, ii * 128:(ii + 1) * 128],
                                                        rhs=slot3(s_, 4)[:, k % 4, :], start=(k == 0), stop=(k == 7)),
                              reads=[rOGT[k], rSLOT[s_]], writes=[rPS[b]], signal=(k == 7))
                    hv = H3[:, i, dc * 512:(dc + 1) * 512]
                    sc.op("dve", lambda e: e.tensor_tensor(out=hv, in0=hv, in1=PS[b][:, :], op=ALU.add),
                          reads=[rPS[b], rH[i]], writes=[rH[i]])
                    if dc == 1 and "p" in OPT:
                        if ii >= 1:
                            ple_prep_tile(hf, ii - 1)
                        if ii == 7:
                            ple_prep_tile(hf, 7)
        phase_ple(l, hf)

    def final_half(hf):
        norm_stats(hf)
        sc.dma("sp", GB[:, :], vecs_d[:, VOFF["fing"]:VOFF["fing"] + 1024], "gb", writes=rGBL)
        for ii in range(8):
            i = hf * 8 + ii
            sc.op("dve", lambda e: e.scalar_tensor_tensor(out=H3[:, i, :], in0=H3[:, i, :], scalar=RS[:, i:i + 1], in1=GB[:, :],
                                                           op0=ALU.mult, op1=ALU.mult), reads=[rH[i], rRS] + rGBL, writes=[rH[i]])
            sc.dma("sp", y_d[i * 128:(i + 1) * 128, :], H3[:, i, :], "y", reads=[rH[i]], writes=[rY[i]])
        final_done[hf] = True

    def phase_final():
        for hf in range(2):
            if not final_done[hf]:
                final_half(hf)

    def store_raw():
        for i in range(16):
            sc.dma("sp", y_d[i * 128:(i + 1) * 128, :], H3[:, i, :], "y", reads=[rH[i]], writes=[rY[i]])

    next_norm_of = {}
    for l in layers:
        next_norm_of[(l, 0)] = (VOFF[f"ng{l}"], 1)
        if l == 1 and (2 in layers or 3 in layers):
            next_norm_of[(l, 1)] = (VOFF["kvg"], 0)
        elif (l + 1) in layers and l != 1:
            next_norm_of[(l, 1)] = (VOFF[f"ng{l + 1}"], 0)
    plan()
    for l in layers:
        for hf in range(2):
            if l < 2:
                phase_A(l, hf)
            else:
                phase_B(l, hf)
        if l == 1 and (2 in layers or 3 in layers):
            for hf in range(2):
                phase_kv(hf)
    if final:
        phase_final()
    else:
        store_raw()
    sc.wait_all("sp", rY)
    sc.wait_all("pool", [rDBG])
    assert ring_state["pos"] == len(seq), (ring_state, len(seq))
    return nc, es


def make_in_maps(inputs):
    f = lambda a: np.asarray(a, dtype=np.float32)
    x = f(inputs["x"]); p = f(inputs["p"])
    wts, _ = pack_weights(f(inputs["a_w_in"]), f(inputs["a_w_out"]), f(inputs["w_kv"]), f(inputs["b_w_in"]),
                          f(inputs["b_w_out"]), f(inputs["ple_w"]), f(inputs["ple_gate_w"]))
    vecs = pack_vecs(f(inputs["norm_g"]), f(inputs["kv_norm_g"]), f(inputs["final_g"]), f(inputs["a_ln_g"]),
                     f(inputs["a_ln_b"]), f(inputs["a_b_s"]))
    wst = np.ascontiguousarray(f(inputs["a_w_s"]).transpose(0, 3, 1, 2)).reshape(2 * 128, 1024)
    cst = make_consts()
    maps = []
    for b in range(8):
        pT = np.ascontiguousarray(p[:, b].reshape(4, S, 2, 128).transpose(0, 3, 2, 1)).reshape(4 * 128, 2 * S)
        maps.append({"x": np.ascontiguousarray(x[b]), "pT": pT, "wts": wts, "vecs": vecs, "wst": wst, "cst": cst})
    return maps


def kernel(**inputs):
    nc, es = build()
    maps = make_in_maps(inputs)
    res = run_bass_kernel_spmd(nc, maps, core_ids=list(range(8)))
    return np.stack([np.asarray(r["y"], dtype=np.float32) for r in res.results], axis=0)
```

```python
import numpy as np
OPT = "sp"
from contextlib import ExitStack
import concourse.bass as bass
import concourse.mybir as mybir
from concourse.bass_utils import run_bass_kernel_spmd

F32 = mybir.dt.float32
BF16 = mybir.dt.bfloat16
AF = mybir.ActivationFunctionType
ALU = mybir.AluOpType

S = 2048
D = 1024
HALF = 1024
EPS = 1e-6
NSLOT = 4
FAST_RECIP = True
SEM_CHUNK = 30000


class Tok:
    __slots__ = ("eng", "sem", "val")

    def __init__(self, eng):
        self.eng = eng
        self.sem = None
        self.val = None


class Reg:
    __slots__ = ("w", "rs", "name")

    def __init__(self, name=""):
        self.w = None
        self.rs = {}
        self.name = name


class Sched:
    def __init__(self, nc, es):
        self.nc = nc
        self.es = es
        self.E = {"pe": nc.tensor, "act": nc.scalar, "dve": nc.vector, "pool": nc.gpsimd, "sp": nc.sync}
        self.sems = {}
        self.cnt = {}
        self.pending = {k: [] for k in self.E}
        self.seen = {k: {} for k in self.E}
        self.nsem = 0
        for k in self.E:
            self._newsem(k)
        self.dsem = {}

    def _newsem(self, k):
        self.nsem += 1
        self.sems[k] = self.nc.alloc_semaphore(name=f"s_{k}_{self.nsem}")
        self.cnt[k] = 0

    def _wait(self, eng, toks):
        best = {}
        for t in toks:
            if t is None:
                continue
            assert t.sem is not None, f"unresolved token from {t.eng} needed by {eng}"
            key = t.sem
            if key not in best or best[key].val < t.val:
                best[key] = t
        for key, t in best.items():
            if self.seen[eng].get(key, 0) >= t.val:
                continue
            self.E[eng].wait_ge(t.sem, t.val)
            self.seen[eng][key] = t.val

    def _deps(self, eng, reads, writes):
        deps = []
        for r in reads:
            if r.w is not None:
                if r.w.eng == eng and eng == "pe":
                    continue
                deps.append(r.w)
        for r in writes:
            if r.w is not None and not (r.w.eng == eng and eng == "pe"):
                deps.append(r.w)
            for e2, t in r.rs.items():
                if not (e2 == eng and eng == "pe"):
                    deps.append(t)
        return deps

    def _mark(self, tok, reads, writes):
        for r in reads:
            r.rs[tok.eng] = tok
        for r in writes:
            r.w = tok
            r.rs = {}

    def op(self, eng, fn, reads=(), writes=(), signal=True):
        self._wait(eng, self._deps(eng, reads, writes))
        inst = fn(self.E[eng])
        tok = Tok(eng)
        if signal:
            if self.cnt[eng] >= SEM_CHUNK:
                self._newsem(eng)
            self.cnt[eng] += 1
            inst.then_inc(self.sems[eng], 1)
            tok.sem = self.sems[eng]
            tok.val = self.cnt[eng]
            for p in self.pending[eng]:
                p.sem = tok.sem
                p.val = tok.val
            self.pending[eng] = []
        else:
            self.pending[eng].append(tok)
        self._mark(tok, reads, writes)
        return tok

    def dma(self, q, out, in_, semname, reads=(), writes=()):
        self._wait(q, self._deps("dma:" + semname, reads, writes))
        if semname not in self.dsem:
            self.dsem[semname] = [self.nc.alloc_semaphore(name="d_" + semname), 0]
        ent = self.dsem[semname]
        ent[1] += 16
        self.E[q].dma_start(out=out, in_=in_).then_inc(ent[0], 16)
        tok = Tok("dma:" + semname)
        tok.sem = ent[0]
        tok.val = ent[1]
        self._mark(tok, reads, writes)
        return tok

    def wait_all(self, eng, regs):
        toks = []
        for r in regs:
            if r.w is not None:
                toks.append(r.w)
        self._wait(eng, toks)


def _kc(w, k0, nk, c0, ncol):
    return w[k0 * 128:(k0 + nk) * 128, c0:c0 + ncol].reshape(nk, 128, ncol).transpose(1, 0, 2)


def pack_weights(a_w_in, a_w_out, w_kv, b_w_in, b_w_out, ple_w, ple_gate_w):
    units = []
    idx = {}

    def add(a):
        units.append(np.ascontiguousarray(a, dtype=np.float32).reshape(128, 2048))

    def ple(l):
        idx["PW", l] = len(units)
        for kh in range(2):
            add(_kc(ple_gate_w[l], kh * 4, 4, 0, 512))
        add(np.stack([_kc(ple_w[l], 0, 2, dc * 512, 512) for dc in range(2)], axis=1))
        for kh in range(2):
            add(_kc(ple_gate_w[l], kh * 4, 4, 512, 512))

    for l in range(2):
        w = a_w_in[l]
        idx["WV", l] = len(units)
        for c in range(4):
            for kh in range(2):
                add(_kc(w, kh * 4, 4, 2048 + c * 512, 512))
        idx["WUG", l] = len(units)
        for j in range(16):
            add(np.concatenate([_kc(w, 0, 8, j * 128, 128), _kc(w, 0, 8, 4096 + j * 128, 128)], axis=2))
        idx["WO", l] = len(units)
        for dc in range(2):
            for kq in range(4):
                add(_kc(a_w_out[l], kq * 4, 4, dc * 512, 512))
        ple(l)
    idx["WK"] = len(units)
    for hp in range(4):
        add(np.stack([_kc(w_kv, 0, 8, (hp * 2 + e) * 128, 128) for e in range(2)], axis=1))
    idx["WVV"] = len(units)
    for vc in range(2):
        for kh in range(2):
            add(_kc(w_kv, kh * 4, 4, 1024 + vc * 512, 512))
    for j in range(2):
        l = 2 + j
        idx["QG", l] = len(units)
        for hd in range(8):
            add(np.concatenate([_kc(b_w_in[j], 0, 8, hd * 128, 128), _kc(b_w_in[j], 0, 8, 1024 + hd * 128, 128)], axis=2))
        idx["BO", l] = len(units)
        for dc in range(2):
            for kh in range(2):
                add(_kc(b_w_out[j], kh * 4, 4, dc * 512, 512))
        ple(l)
    return np.stack(units, axis=0).reshape(len(units) * 128, 2048), idx


VOFF = {}
_o = 0
for _n, _sz in [("ng0", 1024), ("ng1", 1024), ("ng2", 1024), ("ng3", 1024), ("kvg", 1024), ("fing", 1024),
                ("lng0", 2048), ("lng1", 2048), ("lnb0", 2048), ("lnb1", 2048), ("bs0", 1024), ("bs1", 1024)]:
    VOFF[_n] = _o
    _o += _sz
NV = _o


def pack_vecs(norm_g, kv_norm_g, final_g, a_ln_g, a_ln_b, a_b_s):
    v = np.concatenate([norm_g[0], norm_g[1], norm_g[2], norm_g[3], kv_norm_g, final_g,
                        a_ln_g[0], a_ln_g[1], a_ln_b[0], a_ln_b[1],
                        a_b_s[0].reshape(-1), a_b_s[1].reshape(-1)]).astype(np.float32)
    return np.ascontiguousarray(np.broadcast_to(v[None, :], (128, NV)))


def make_consts():
    c = np.zeros((128, 512), np.float32)
    i = np.arange(128)
    c[:, 0:128] = np.eye(128)
    c[:, 128:256] = -1.0 * (i[:, None] >= i[None, :])
    c[:, 256:384] = -1.0
    c[:, 384:512] = -30000.0 * (i[:, None] >= i[None, :])
    return c


NUNITS = 2 * (8 + 16 + 8 + 5) + 8 + 2 * (8 + 4 + 5)


def build(layers=(0, 1, 2, 3), final=True, dbg=False):
    nc = bass.Bass("TRN2", target_bir_lowering=False)
    es = ExitStack()
    sc = Sched(nc, es)
    _, widx = pack_weights(*[np.zeros(s, np.float32) for s in
                             [(2, 1024, 6144), (2, 2048, 1024), (1024, 2048), (2, 1024, 2048),
                              (2, 1024, 1024), (4, 256, 1024), (4, 1024, 1024)]])

    x_d = nc.dram_tensor("x", [S, D], F32, kind="ExternalInput").ap()
    pT_d = nc.dram_tensor("pT", [4 * 128, 2 * S], F32, kind="ExternalInput").ap()
    wts_d = nc.dram_tensor("wts", [NUNITS * 128, 2048], F32, kind="ExternalInput").ap()
    vecs_d = nc.dram_tensor("vecs", [128, NV], F32, kind="ExternalInput").ap()
    wst_d = nc.dram_tensor("wst", [2 * 128, 1024], F32, kind="ExternalInput").ap()
    cst_d = nc.dram_tensor("cst", [128, 512], F32, kind="ExternalInput").ap()
    y_d = nc.dram_tensor("y", [S, D], F32, kind="ExternalOutput").ap()
    dbg_d = {}
    rDBG = Reg("dbg")

    def dump(name, ap, regs):
        if not dbg:
            return
        shp = list(ap.shape)
        dbg_d[name] = nc.dram_tensor("dbg_" + name, shp, F32, kind="ExternalOutput").ap()
        sc.dma("pool", dbg_d[name], ap, "dbg", reads=regs, writes=[rDBG])

    def sb(name, shape, dt):
        return es.enter_context(nc.sbuf_tensor(name, shape, dt))

    H = sb("H", [128, 16 * 1024], F32)
    HT = sb("HT", [128, 8 * HALF], BF16)
    OGT = sb("OGT", [128, 8 * HALF], BF16)
    X = sb("X", [128, 32768], BF16)
    RING = sb("RING", [128, NSLOT * 2048], BF16)
    TAA = sb("TAA", [128, 1024], F32)
    GB = TAA
    TA = [TAA[:, 0:512], TAA[:, 512:1024]]
    HN = [sb(f"HN{i}", [128, 1024], BF16) for i in range(2)]
    CST = sb("CST", [128, 512], BF16)
    PT = sb("PT", [128, 2 * HALF], BF16)
    SS = sb("SS", [128, 16], F32)
    RS = sb("RS", [128, 16], F32)
    TBB = sb("TBB", [128, 2048], BF16)
    TB = [TBB[:, i * 512:(i + 1) * 512] for i in range(4)]
    PS = [es.enter_context(nc.psum_tensor(f"ps{i}", [128, 512], F32)) for i in range(8)]
    NQ = 512
    LSUM = sb("LSUM", [128, NQ], F32)
    LSUMB = [sb(f"LSUMB{i}", [128, NQ], BF16) for i in range(2)]
    EE = sb("EE", [128, NQ], F32)
    LP = [sb(f"LP{i}", [128, NQ], BF16) for i in range(3)]
    AT = [sb(f"AT{i}", [128, NQ], BF16) for i in range(2)]
    QT = [PT[:, 0:HALF], PT[:, HALF:2 * HALF]]
    SG = [TBB[:, 0:HALF], TBB[:, HALF:2 * HALF]]
    OGF = OGT[:, :].bitcast(F32)
    LNG = OGF[:, 0:2048]
    LNB = OGF[:, 2048:4096]
    XB = X[:, 16384:32768]
    VHAT2 = [XB[:, 0:2048], XB[:, 8192:10240]]
    WST = XB[:, 2048:3072]
    BSH = XB[:, 3072:4096]
    BSL = XB[:, 4096:5120]
    BSHL = XB[0:64, 5120:6144]
    ONES = XB[0:64, 6144:6272]
    BSR = XB[0:64, 3072:5120]
    BSF = XB[:, 8192:10240].bitcast(F32)
    SML = sb("SML", [128, 64], F32)
    ST8 = sb("ST8", [128, 192], F32)
    ST = SML[:, 0:24]
    MV8 = SML[:, 24:40]
    RSV8 = SML[:, 40:48]
    NMR8 = SML[:, 48:56]

    H3 = H[:, :].rearrange("p (i d) -> p i d", d=1024)
    HT3 = HT[:, :].rearrange("p (k t) -> p k t", t=HALF)
    OGT3 = OGT[:, :].rearrange("p (k t) -> p k t", t=HALF)
    PT3 = PT[:, :].rearrange("p (k t) -> p k t", t=HALF)
    IDENT = CST[:, 0:128]
    NEGTRI = CST[:, 128:256]
    NEGONES = CST[:, 256:384]
    NEGMASK = CST[:, 384:512]

    rH = [Reg(f"H{i}") for i in range(16)]
    rHT = [Reg(f"HT{i}") for i in range(8)]
    rOGT = [Reg(f"OGT{h}") for h in range(8)]
    rHN = [Reg(), Reg()]
    rCST = Reg("CST")
    rSS = Reg("SS")
    rRS = Reg("RS")
    rTA = [Reg(), Reg()]
    rTB = [Reg() for _ in range(4)]
    rGBL = rTA
    rQT = [Reg(), Reg()]
    rPTL = rQT
    rSG = [[rTB[0], rTB[1]], [rTB[2], rTB[3]]]
    rLS = Reg(); rLSB = [Reg(), Reg()]; rEE = Reg()
    rLP = [Reg(), Reg(), Reg()]; rAT = [Reg(), Reg()]
    rLNG = Reg(); rLNB = Reg(); rWST = Reg(); rBS = Reg(); rVH2 = [Reg(), rBS]; rBSHL = Reg(); rBSR = Reg()
    rST = Reg(); rMV = Reg(); rSV = Reg()
    rST8 = [Reg() for _ in range(8)]
    rX = [Reg(f"X{i}") for i in range(8)]
    rPS = [Reg(f"ps{i}") for i in range(8)]
    rY = [Reg(f"Y{i}") for i in range(16)]

    cnt = {"ta": 0, "tb": 0, "hn": 0, "bank": 0, "lp": 0, "at": 0, "ptmp": 0, "ta4": 0}

    def nxt(key, n):
        v = cnt[key] % n
        cnt[key] += 1
        return v

    bank_set = [list(range(8))]

    def bank():
        bs = bank_set[0]
        return bs[nxt("bank", len(bs))]

    seq = []
    seqA = []
    XBr = X[:, 16384:32768]
    slot_ap = [RING[:, i * 2048:(i + 1) * 2048] for i in range(NSLOT)] + \
              [XBr[:, 10240 + i * 2048:10240 + (i + 1) * 2048] for i in range(3)]
    NS_A = NSLOT + 3
    rSLOT = [Reg(f"slot{i}") for i in range(NS_A)]
    slot_occ = [-1] * NS_A
    unit_slot = {}
    ring_state = {"issued": 0, "pos": 0, "done": 0, "last": -1}

    def ring_try_issue(q):
        if q == 2:
            sc.wait_all("pool", [rH[7]])
        pool = NS_A if seqA[q] else NSLOT
        for d_ in range(1, pool + 1):
            s = (ring_state["last"] + d_) % pool
            if slot_occ[s] < ring_state["done"]:
                break
        else:
            return False
        u = seq[q]
        sc.dma("pool", slot_ap[s], wts_d[u * 128:(u + 1) * 128, :], f"ring{s}", writes=[rSLOT[s]])
        slot_occ[s] = q
        unit_slot[q] = s
        ring_state["last"] = s
        ring_state["issued"] += 1
        return True

    def ring_begin(keep=0):
        ring_state["done"] = ring_state["pos"] - keep

    def ring_next(u):
        q = ring_state["pos"]
        assert seq[q] == u, (q, seq[q], u)
        while ring_state["issued"] <= q:
            ok = ring_try_issue(ring_state["issued"])
            assert ok, "ring: no free slot for a unit that is needed now"
        ring_state["pos"] += 1
        return unit_slot[q]

    def ring_prefetch(*_):
        while ring_state["issued"] < len(seq) and ring_state["issued"] < ring_state["pos"] + NS_A:
            if not ring_try_issue(ring_state["issued"]):
                break

    def plan():
        def ext(r, isa):
            seq.extend(r)
            seqA.extend([isa] * len(r))
        for l in layers:
            for hf in range(2):
                if l < 2:
                    ext(range(widx["WV", l], widx["WV", l] + 8), l < 2)
                    ext(range(widx["WUG", l], widx["WUG", l] + 16), l < 2)
                    ext(range(widx["WO", l], widx["WO", l] + 8), l < 2)
                else:
                    if l == 2 or (l == 3 and 2 not in layers):
                        pass
                    ext(range(widx["QG", l], widx["QG", l] + 8), l < 2)
                    ext(range(widx["BO", l], widx["BO", l] + 4), l < 2)
                ext(range(widx["PW", l], widx["PW", l] + 5), l < 2)
            if l == 1 and (2 in layers or 3 in layers):
                for hf in range(2):
                    ext(range(widx["WK"], widx["WK"] + 4), False)
                    ext(range(widx["WVV"], widx["WVV"] + 4), False)

    def slot3(slot, k):
        return slot_ap[slot].rearrange("p (k c) -> p k c", k=k)

    sc.dma("pool", CST[:, :], cst_d[:, :], "cst", writes=[rCST])
    for i in range(16):
        sc.dma("sp", H3[:, i, :], x_d[i * 128:(i + 1) * 128, :], f"x{i}", writes=[rH[i]])

    JUNK = EE[:, :].bitcast(BF16)
    stats_ready = [False, False]

    def norm_stats(hf):
        if stats_ready[hf]:
            return
        for ii in range(8):
            i = hf * 8 + ii
            sc.op("act", lambda e: e.activation(out=JUNK, in_=H3[:, i, :], func=AF.Square, accum_out=SS[:, i:i + 1]),
                  reads=[rH[i]], writes=[rEE, rSS])
        sl = slice(hf * 8, hf * 8 + 8)
        sc.op("dve", lambda e: e.tensor_scalar(out=RS[:, sl], in0=SS[:, sl], scalar1=1.0 / 1024, scalar2=EPS,
                                                op0=ALU.mult, op1=ALU.add), reads=[rSS], writes=[rRS])
        sc.op("act", lambda e: e.activation(out=RS[:, sl], in_=RS[:, sl], func=AF.Sqrt), reads=[rRS], writes=[rRS])
        sc.op("dve", lambda e: e.reciprocal(out=RS[:, sl], in_=RS[:, sl]), reads=[rRS], writes=[rRS])
        stats_ready[hf] = True

    def phase_norm(goff, hf):
        if gb_loaded["goff"] == goff:
            gb_loaded["goff"] = None
        else:
            sc.dma("sp", GB[:, :], vecs_d[:, goff:goff + 1024], "gb", writes=rGBL)
        norm_stats(hf)
        for ii in range(8):
            i = hf * 8 + ii
            hb = ii % 2
            sc.op("dve", lambda e: e.scalar_tensor_tensor(out=HN[hb][:, :], in0=H3[:, i, :], scalar=RS[:, i:i + 1],
                                                           in1=GB[:, :], op0=ALU.mult, op1=ALU.mult),
                  reads=[rH[i], rRS] + rGBL, writes=[rHN[hb]])
            transpose_tile(HN[hb], rHN[hb], 8, None, [rHT[ii]], HT3[:, :, ii * 128:(ii + 1) * 128])

    def ple_prep_tile(hf, ii):
        i = hf * 8 + ii
        hb = nxt("hn", 2)
        sc.op("dve", lambda e: e.tensor_copy(out=HN[hb][:, :], in_=H3[:, i, :]), reads=[rH[i]], writes=[rHN[hb]])
        transpose_tile(HN[hb], rHN[hb], 8, None, [rHT[ii]], HT3[:, :, ii * 128:(ii + 1) * 128])

    def ple_load_pt(l, hf):
        sc.dma("pool", PT3[:, :, :], pT_d[l * 128:(l + 1) * 128, :].rearrange("p (k t) -> p k t", t=S)[:, :, hf * HALF:(hf + 1) * HALF],
               "pt", writes=rPTL)

    def transpose_tile(src, rsrc, nk, dst_fn, wregs, dst_all):
        b = bank()
        pv = PS[b][:, :].bitcast(BF16).rearrange("p (k t) -> p k t", t=128)
        for k in range(nk):
            sc.op("pe", lambda e: e.transpose(out=pv[:, k, :], in_=src[:, k * 128:(k + 1) * 128], identity=IDENT),
                  reads=[rsrc, rCST], writes=[rPS[b]], signal=(k == nk - 1))
        sc.op("act", lambda e: e.copy(out=dst_all, in_=pv[:, 0:nk, :]), reads=[rPS[b]], writes=wregs)

    def phase_A(l, hf):
        if True:
            X3 = X[:, 0:16384].rearrange("p (i f) -> p i f", f=2048)
            WST3 = WST.rearrange("p (g t) -> p g t", t=128)
            BSHL3 = BSHL.rearrange("p (g t) -> p g t", t=128)

            if hf == 0:
                sc.dma("sp", LNG, vecs_d[:, VOFF[f"lng{l}"]:VOFF[f"lng{l}"] + 2048], "lng", writes=[rLNG])
                sc.dma("sp", LNB, vecs_d[:, VOFF[f"lnb{l}"]:VOFF[f"lnb{l}"] + 2048], "lnb", writes=[rLNB])
                sc.dma("pool", WST, wst_d[l * 128:(l + 1) * 128, :], "wst", writes=[rWST])
                sc.op("dve", lambda e: e.memset(WST3[64:128, :, 0:64], 0.0), writes=[rWST])
                sc.dma("sp", BSF, vecs_d[:, VOFF[f"bs{l}"]:VOFF[f"bs{l}"] + 1024], "bsf", writes=[rBS])
                sc.op("dve", lambda e: e.tensor_copy(out=BSH, in_=BSF), reads=[rBS], writes=[rBS])
                sc.op("dve", lambda e: e.tensor_tensor(out=BSF, in0=BSF, in1=BSH, op=ALU.subtract),
                      reads=[rBS], writes=[rBS])
                sc.op("dve", lambda e: e.tensor_copy(out=BSL, in_=BSF), reads=[rBS], writes=[rBS])
                sc.op("dve", lambda e: e.memset(BSHL, 0.0), writes=[rBSHL])
                sc.op("dve", lambda e: e.memset(ONES, 0.0), writes=[rBSHL])
                sc.op("dve", lambda e: e.tensor_copy(out=BSHL[0:1, :], in_=BSH[0:1, :]), reads=[rBS], writes=[rBSHL])
                sc.op("dve", lambda e: e.tensor_copy(out=BSHL[32:33, :], in_=BSL[32:33, :]), reads=[rBS], writes=[rBSHL])
                sc.op("dve", lambda e: e.memset(ONES[0:1, :], 1.0), writes=[rBSHL])
                sc.op("dve", lambda e: e.memset(ONES[32:33, :], 1.0), writes=[rBSHL])
                BSR4 = BSR.rearrange("p (g d t) -> p g d t", d=2, t=128)
                for d_ in range(2):
                    sc.op("dve", lambda e: e.tensor_copy(out=BSR4[:, :, d_, :], in_=BSHL3), reads=[rBSHL], writes=[rBS, rBSR])

            phase_norm(VOFF[f"ng{l}"], hf)
            if "s" in OPT:
                norm_stats(1 - hf)
            if l == 0 and hf == 0:
                dump("HT", HT[:, :], rHT)

            MV83 = MV8.rearrange("p (i t) -> p i t", t=2)

            def stats_batch(t0, t1):
                sl_ = slice(t0, t1)
                sc.op("dve", lambda e: e.tensor_scalar(out=RSV8[:, sl_], in0=MV83[:, sl_, 1], scalar1=EPS, scalar2=None, op0=ALU.add),
                      reads=[rMV], writes=[rSV])
                sc.op("act", lambda e: e.activation(out=RSV8[:, sl_], in_=RSV8[:, sl_], func=AF.Sqrt), reads=[rSV], writes=[rSV])
                sc.op("dve", lambda e: e.reciprocal(out=RSV8[:, sl_], in_=RSV8[:, sl_]), reads=[rSV], writes=[rSV])
                sc.op("dve", lambda e: e.scalar_tensor_tensor(out=NMR8[:, sl_], in0=MV83[:, sl_, 0], scalar=-1.0, in1=RSV8[:, sl_],
                                                               op0=ALU.mult, op1=ALU.mult), reads=[rMV, rSV], writes=[rSV])

            TA4 = [TA[0], TA[1], EE[:, :], LSUM[:, :]]
            rTA4 = [rTA[0], rTA[1], rEE, rLS]

            def ps_produce(ii):
                VHAT = VHAT2[ii % 2]
                rVH = rVH2[ii % 2]
                for q in range(4):
                    ta = nxt("ta4", 4)
                    T_, rT_ = TA4[ta], rTA4[ta]
                    qs = slice(q * 512, (q + 1) * 512)
                    sc.op("act", lambda e: e.activation(out=T_, in_=X3[:, ii, qs], func=AF.Identity,
                                                         bias=NMR8[:, ii:ii + 1], scale=RSV8[:, ii:ii + 1]),
                          reads=[rX[ii], rSV], writes=[rT_])
                    sc.op("dve", lambda e: e.tensor_tensor(out=T_, in0=T_, in1=LNG[:, qs], op=ALU.mult),
                          reads=[rT_, rLNG], writes=[rT_])
                    sc.op("dve", lambda e: e.tensor_tensor(out=VHAT[:, qs], in0=T_, in1=LNB[:, qs], op=ALU.add),
                          reads=[rT_, rLNB], writes=[rVH])

            def ps_consume(ii):
                VHAT = VHAT2[ii % 2]
                rVH = rVH2[ii % 2]
                Xs = X3[:, ii, :].rearrange("p (j t) -> p j t", t=128)
                for jb in range(4):
                    b = bank()
                    sc.op("pe", lambda e: e.matmul(PS[b][:, :], lhsT=ONES[0:33, :], rhs=BSR[0:33, jb * 512:(jb + 1) * 512],
                                                    start=True, stop=False, skip_group_check=True),
                          reads=[rBSHL, rBSR], writes=[rPS[b]], signal=False)
                    for jj in range(4):
                        j = jb * 4 + jj
                        g = j // 2
                        sc.op("pe", lambda e: e.matmul(PS[b][:, jj * 128:(jj + 1) * 128], lhsT=VHAT[:, j * 128:(j + 1) * 128],
                                                        rhs=WST3[:, g, :], start=False, stop=(jj == 3), skip_group_check=True),
                              reads=[rVH, rWST], writes=[rPS[b]], signal=(jj == 3))
                    sc.op("act", lambda e: e.copy(out=Xs[:, jb * 4:(jb + 1) * 4, :],
                                                   in_=PS[b][:, :].rearrange("p (j t) -> p j t", t=128)),
                          reads=[rPS[b]], writes=[rX[ii]])

            for c in range(4):
                ring_begin()
                sA = ring_next(widx["WV", l] + c * 2)
                sB = ring_next(widx["WV", l] + c * 2 + 1)
                ring_prefetch(2)
                for ii in range(8):
                    b = bank()
                    for k in range(8):
                        s_ = sA if k < 4 else sB
                        sc.op("pe", lambda e: e.matmul(PS[b][:, :], lhsT=HT3[:, k, ii * 128:(ii + 1) * 128],
                                                        rhs=slot3(s_, 4)[:, k % 4, :], start=(k == 0), stop=(k == 7)),
                              reads=[rHT[ii], rSLOT[s_]], writes=[rPS[b]], signal=(k == 7))
                    sc.op("act", lambda e: e.activation(out=X3[:, ii, c * 512:(c + 1) * 512], in_=PS[b][:, :], func=AF.Gelu),
                          reads=[rPS[b]], writes=[rX[ii]])
                    sc.op("dve", lambda e: e.bn_stats(out=ST8[:, ii * 24 + c * 6:ii * 24 + (c + 1) * 6],
                                                       in_=X3[:, ii, c * 512:(c + 1) * 512]),
                          reads=[rX[ii]], writes=[rST8[ii]])
                    if c == 3:
                        sc.op("dve", lambda e: e.bn_aggr(out=MV8[:, ii * 2:(ii + 1) * 2], in_=ST8[:, ii * 24:(ii + 1) * 24]),
                              reads=[rST8[ii]], writes=[rMV])
                        if ii == 3:
                            stats_batch(0, 4)
                            ps_produce(0)
                            ps_produce(1)
                        if ii == 7:
                            stats_batch(4, 8)
            if l == 0 and hf == 0:
                dump("GV", X[:, 0:16384], rX)
            for ii in range(8):
                ps_consume(ii)
                if ii + 2 < 8:
                    ps_produce(ii + 2)

            if l == 0 and hf == 0:
                dump("SVT", X[:, 0:16384], rX)
            for j in range(16):
                ring_begin()
                s_ = ring_next(widx["WUG", l] + j)
                ring_prefetch(1)
                w3 = slot3(s_, 8)
                for st in range(2):
                    bu = bank()
                    for k in range(8):
                        sc.op("pe", lambda e: e.matmul(PS[bu][:, :], lhsT=w3[:, k, 0:128], rhs=HT3[:, k, st * 512:(st + 1) * 512],
                                                        start=(k == 0), stop=(k == 7)),
                              reads=[rSLOT[s_]] + rHT[st * 4:(st + 1) * 4], writes=[rPS[bu]], signal=(k == 7))
                    bg = bank()
                    for k in range(8):
                        sc.op("pe", lambda e: e.matmul(PS[bg][:, :], lhsT=w3[:, k, 128:256], rhs=HT3[:, k, st * 512:(st + 1) * 512],
                                                        start=(k == 0), stop=(k == 7)),
                              reads=[rSLOT[s_]] + rHT[st * 4:(st + 1) * 4], writes=[rPS[bg]], signal=(k == 7))
                    tb = nxt("tb", 4)
                    sc.op("act", lambda e: e.activation(out=TB[tb][:, :], in_=PS[bu][:, :], func=AF.Gelu),
                          reads=[rPS[bu]], writes=[rTB[tb]])
                    ta2 = nxt("ta", 2)
                    tb2 = nxt("tb", 4)
                    sc.op("act", lambda e: e.activation(out=TA[ta2][:, :], in_=PS[bg][:, :], func=AF.Tanh, scale=0.5),
                          reads=[rPS[bg]], writes=[rTA[ta2]])
                    sc.op("dve", lambda e: e.scalar_tensor_tensor(out=TB[tb2][:, :], in0=TA[ta2][:, :], scalar=1.0, in1=PS[bg][:, :],
                                                                   op0=ALU.add, op1=ALU.mult),
                          reads=[rTA[ta2], rPS[bg]], writes=[rTB[tb2]])
                    sc.op("dve", lambda e: e.tensor_tensor(out=TB[tb][:, :], in0=TB[tb][:, :], in1=TB[tb2][:, :], op=ALU.mult),
                          reads=[rTB[tb], rTB[tb2]], writes=[rTB[tb]])
                    xv = X3[:, st * 4:(st + 1) * 4, j * 128:(j + 1) * 128]
                    sc.op("dve", lambda e: e.scalar_tensor_tensor(out=xv, in0=TB[tb][:, :].rearrange("p (i t) -> p i t", t=128),
                                                                   scalar=0.5, in1=xv, op0=ALU.mult, op1=ALU.mult),
                          reads=[rTB[tb]] + rX[st * 4:(st + 1) * 4], writes=rX[st * 4:(st + 1) * 4])

            if l == 0 and hf == 0:
                dump("GAT", X[:, 0:16384], rX)
            ple_load_pt(l, hf)
            for dc in range(2):
                ring_begin()
                ss_ = [ring_next(widx["WO", l] + dc * 4 + kq) for kq in range(4)]
                ring_prefetch(4)
                for ii in range(8):
                    i = hf * 8 + ii
                    b = bank()
                    for k in range(16):
                        s_ = ss_[k // 4]
                        sc.op("pe", lambda e: e.matmul(PS[b][:, :], lhsT=X3[:, ii, k * 128:(k + 1) * 128],
                                                        rhs=slot3(s_, 4)[:, k % 4, :], start=(k == 0), stop=(k == 15)),
                              reads=[rX[ii], rSLOT[s_]], writes=[rPS[b]], signal=(k == 15))
                    hv = H3[:, i, dc * 512:(dc + 1) * 512]
                    sc.op("dve", lambda e: e.tensor_tensor(out=hv, in0=hv, in1=PS[b][:, :], op=ALU.add),
                          reads=[rPS[b], rH[i]], writes=[rH[i]])
                    if dc == 1 and "p" in OPT:
                        if ii >= 1:
                            ple_prep_tile(hf, ii - 1)
                        if ii == 7:
                            ple_prep_tile(hf, 7)
        if l == 0 and hf == 0:
            dump("H1", H[:, 0:8192], rH[0:8])
        phase_ple(l, hf)

    gb_loaded = {"goff": None}
    final_done = [False, False]
    rFS = [Reg(f"fs{i}") for i in range(16)]

    def final_tile(i):
        hb = i % 2
        sc.op("act", lambda e: e.activation(out=HN[hb][:, :], in_=H3[:, i, :], func=AF.Square, accum_out=SS[:, i:i + 1]),
              reads=[rH[i]], writes=[rHN[hb], rFS[i]])
        sc.op("dve", lambda e: e.tensor_scalar(out=RS[:, i:i + 1], in0=SS[:, i:i + 1], scalar1=1.0 / 1024, scalar2=EPS,
                                                op0=ALU.mult, op1=ALU.add), reads=[rFS[i]], writes=[rFS[i]])
        sc.op("act", lambda e: e.activation(out=RS[:, i:i + 1], in_=RS[:, i:i + 1], func=AF.Sqrt), reads=[rFS[i]], writes=[rFS[i]])
        sc.op("dve", lambda e: e.reciprocal(out=RS[:, i:i + 1], in_=RS[:, i:i + 1]), reads=[rFS[i]], writes=[rFS[i]])
        sc.op("dve", lambda e: e.scalar_tensor_tensor(out=H3[:, i, :], in0=H3[:, i, :], scalar=RS[:, i:i + 1], in1=GB[:, :],
                                                       op0=ALU.mult, op1=ALU.mult), reads=[rH[i], rFS[i]] + rGBL, writes=[rH[i]])
        sc.dma("sp", y_d[i * 128:(i + 1) * 128, :], H3[:, i, :], "y", reads=[rH[i]], writes=[rY[i]])

    def phase_ple(l, hf):
        nn = next_norm_of.get((l, hf))
        fin_here = final and l == 3 and hf == 1
        if nn is not None:
            sc.dma("sp", GB[:, :], vecs_d[:, nn[0]:nn[0] + 1024], "gb", writes=rGBL)
            gb_loaded["goff"] = nn[0]
        elif fin_here:
            sc.dma("sp", GB[:, :], vecs_d[:, VOFF["fing"]:VOFF["fing"] + 1024], "gb", writes=rGBL)
        if "p" not in OPT:
            for ii in range(8):
                ple_prep_tile(hf, ii)
        base = widx["PW", l]
        sw = None
        wp = None
        for dc in range(2):
            if dc == 0:
                ring_begin()
                sg = [ring_next(base + 0), ring_next(base + 1)]
                sw = ring_next(base + 2)
                wp = slot_ap[sw].rearrange("p (d k c) -> p d k c", d=2, k=2)
            else:
                ring_begin(keep=1)
                sg = [ring_next(base + 3), ring_next(base + 4)]
            ring_prefetch(3)
            for ii in range(8):
                i = hf * 8 + ii
                bg = bank()
                for k in range(8):
                    s_ = sg[k // 4]
                    sc.op("pe", lambda e: e.matmul(PS[bg][:, :], lhsT=HT3[:, k, ii * 128:(ii + 1) * 128],
                                                    rhs=slot3(s_, 4)[:, k % 4, :], start=(k == 0), stop=(k == 7)),
                          reads=[rHT[ii], rSLOT[s_]], writes=[rPS[bg]], signal=(k == 7))
                bp = bank()
                for k in range(2):
                    sc.op("pe", lambda e: e.matmul(PS[bp][:, :], lhsT=PT3[:, k, ii * 128:(ii + 1) * 128],
                                                    rhs=wp[:, dc, k, :], start=(k == 0), stop=(k == 1)),
                          reads=rPTL + [rSLOT[sw]], writes=[rPS[bp]], signal=(k == 1))
                tp_ = nxt("ptmp", 2)
                T_ = (EE, LSUM)[tp_]
                rT_ = (rEE, rLS)[tp_]
                sc.op("act", lambda e: e.activation(out=T_[:, :], in_=PS[bg][:, :], func=AF.Tanh, scale=0.5),
                      reads=[rPS[bg]], writes=[rT_])
                sc.op("dve", lambda e: e.scalar_tensor_tensor(out=T_[:, :], in0=T_[:, :], scalar=1.0, in1=PS[bp][:, :],
                                                               op0=ALU.add, op1=ALU.mult),
                      reads=[rT_, rPS[bp]], writes=[rT_])
                hv = H3[:, i, dc * 512:(dc + 1) * 512]
                sc.op("dve", lambda e: e.scalar_tensor_tensor(out=hv, in0=T_[:, :], scalar=0.5, in1=hv, op0=ALU.mult, op1=ALU.add),
                      reads=[rT_, rH[i]], writes=[rH[i]])
                if fin_here and dc == 1:
                    final_tile(i)
        stats_ready[hf] = False
        if fin_here:
            final_done[hf] = True

    KT3 = X[:, 0:16384].rearrange("p (h t) -> p h t", t=S)
    V3 = X[:, 16384:32768].rearrange("p (i f) -> p i f", f=1024)
    rKT = [Reg(f"KT{h}") for h in range(8)]
    rV = [Reg(f"V{i}") for i in range(16)]

    def phase_kv(hf):
        phase_norm(VOFF["kvg"], hf)
        if hf == 0:
            norm_stats(1)
            gb_loaded["goff"] = VOFF["kvg"]
        elif 2 in layers:
            sc.dma("sp", GB[:, :], vecs_d[:, VOFF["ng2"]:VOFF["ng2"] + 1024], "gb", writes=rGBL)
            gb_loaded["goff"] = VOFF["ng2"]
        for hp in range(4):
            ring_begin()
            s_ = ring_next(widx["WK"] + hp)
            ring_prefetch(1)
            w4 = slot_ap[s_].rearrange("p (e k c) -> p e k c", e=2, k=8)
            for e_ in range(2):
                hd = hp * 2 + e_
                for st in range(2):
                    b = bank()
                    for k in range(8):
                        sc.op("pe", lambda e: e.matmul(PS[b][:, :], lhsT=w4[:, e_, k, :], rhs=HT3[:, k, st * 512:(st + 1) * 512],
                                                        start=(k == 0), stop=(k == 7)),
                              reads=[rSLOT[s_]] + rHT[st * 4:(st + 1) * 4], writes=[rPS[b]], signal=(k == 7))
                    c0 = hf * HALF + st * 512
                    sc.op("act", lambda e: e.copy(out=KT3[:, hd, c0:c0 + 512], in_=PS[b][:, :]), reads=[rPS[b]], writes=[rKT[hd]])
        for vc in range(2):
            ring_begin()
            sv_ = [ring_next(widx["WVV"] + vc * 2 + kh) for kh in range(2)]
            ring_prefetch(2)
            for ii in range(8):
                i = hf * 8 + ii
                b = bank()
                for k in range(8):
                    s_ = sv_[k // 4]
                    sc.op("pe", lambda e: e.matmul(PS[b][:, :], lhsT=HT3[:, k, ii * 128:(ii + 1) * 128],
                                                    rhs=slot3(s_, 4)[:, k % 4, :], start=(k == 0), stop=(k == 7)),
                          reads=[rHT[ii], rSLOT[s_]], writes=[rPS[b]], signal=(k == 7))
                sc.op("dve", lambda e: e.tensor_copy(out=V3[:, i, vc * 512:(vc + 1) * 512], in_=PS[b][:, :]),
                      reads=[rPS[b]], writes=[rV[i]] + rSLOT[NSLOT:])

    def phase_B(l, hf):
        if True:
            SCALE = 1.0 / np.sqrt(128.0)

            phase_norm(VOFF[f"ng{l}"], hf)
            if "s" in OPT:
                norm_stats(1 - hf)
            if final and l == 3 and hf == 1:
                final_half(0)
            bank_set[0] = [0, 1, 2, 3, 6]

            def proj_items(hd, per):
                st8 = {}
                qb = hd % 2

                def start():
                    ring_begin()
                    st8["s"] = ring_next(widx["QG", l] + hd)
                    ring_prefetch()

                def group(st, isg):
                    g8 = {}

                    def mm(k0, k1):
                        def f():
                            if "s" not in st8:
                                start()
                            s_ = st8["s"]
                            w3 = slot3(s_, 8)
                            if "b" not in g8:
                                g8["b"] = bank()
                            b = g8["b"]
                            co = 128 if isg else 0
                            for k in range(k0, k1):
                                sc.op("pe", lambda e: e.matmul(PS[b][:, :], lhsT=w3[:, k, co:co + 128],
                                                                rhs=HT3[:, k, st * 512:(st + 1) * 512], start=(k == 0), stop=(k == 7)),
                                      reads=[rSLOT[s_]] + rHT[st * 4:(st + 1) * 4], writes=[rPS[b]], signal=(k == 7))
                            if k1 == 8:
                                if not isg:
                                    sc.op("dve", lambda e: e.tensor_scalar(out=QT[qb][:, st * 512:(st + 1) * 512], in0=PS[b][:, :],
                                                                            scalar1=float(SCALE), scalar2=None, op0=ALU.mult),
                                          reads=[rPS[b]], writes=[rQT[qb]])
                                else:
                                    ta = nxt("ta", 2)
                                    sc.op("act", lambda e: e.activation(out=TA[ta][:, :], in_=PS[b][:, :], func=AF.Exp, scale=-1.0),
                                          reads=[rPS[b]], writes=[rTA[ta]])
                                    sc.op("dve", lambda e: e.tensor_scalar(out=TA[ta][:, :], in0=TA[ta][:, :], scalar1=1.0, scalar2=None,
                                                                            op0=ALU.add), reads=[rTA[ta]], writes=[rTA[ta]])
                                    g8["ta"] = ta
                                    sc.op("dve", lambda e: e.tensor_copy(out=SG[qb][:, st * 512:(st + 1) * 512], in_=PS[b][:, :]),
                                          reads=[rPS[b]], writes=rSG[qb])
                        return f
                    its = [("mm", mm(k0, min(8, k0 + per)), None) for k0 in range(0, 8, per)]
                    if isg:
                        def rc(q):
                            def f():
                                ta = g8["ta"]
                                sl_ = slice(q * 128, (q + 1) * 128)
                                sc.op("dve", lambda e: e.reciprocal(out=TA[ta][:, sl_], in_=TA[ta][:, sl_]),
                                      reads=[rTA[ta]], writes=[rTA[ta]])
                            return f

                        def fin():
                            ta = g8["ta"]
                            sgv = SG[qb][:, st * 512:(st + 1) * 512]
                            sc.op("dve", lambda e: e.tensor_tensor(out=sgv, in0=sgv, in1=TA[ta][:, :], op=ALU.mult),
                                  reads=[rTA[ta]] + rSG[qb], writes=rSG[qb])
                        rdy = lambda: "ta" in g8
                        its += [("ch", rc(q), rdy) for q in range(4)] + [("ch", fin, rdy)]
                    return its
                items = []
                for st in range(2):
                    items += group(st, False)
                for st in range(2):
                    items += group(st, True)
                return items

            steps = []
            for hd in range(8):
                for c2 in range(2):
                    c = hf * 2 + c2
                    nkb = 4 * c + 4
                    prev = None
                    for kb in range(nkb - 1, -1, -1):
                        s = dict(hd=hd, c2=c2, c=c, kb=kb, first=(kb == nkb - 1), last=(kb == 0), idx=nkb - 1 - kb,
                                 prev=prev, nxt_last=(kb == 1))
                        steps.append(s)
                        prev = s
            n = len(steps)
            st_ = {"lsb": 0}

            def geom(s):
                c0 = max(0, s["kb"] - 4 * s["c"]) * 128
                return c0, slice(c0, NQ), slice(s["c2"] * 512 + c0, s["c2"] * 512 + NQ)

            def emit_Z(s):
                c0, cs, qs = geom(s)
                hd, kb, qb = s["hd"], s["kb"], s["hd"] % 2
                zb = bank()
                s["zb"] = zb
                Z = PS[zb]
                diag = kb >= 4 * s["c"]
                sc.op("pe", lambda e: e.matmul(Z[:, cs], lhsT=KT3[:, hd, kb * 128:(kb + 1) * 128], rhs=QT[qb][:, qs],
                                                start=True, stop=not diag, skip_group_check=True),
                      reads=[rKT[hd], rQT[qb]], writes=[rPS[zb]], signal=not diag)
                if diag:
                    sc.op("pe", lambda e: e.matmul(Z[:, c0:c0 + 128], lhsT=IDENT, rhs=NEGMASK, start=False, stop=True,
                                                    skip_group_check=True),
                          reads=[rCST], writes=[rPS[zb]], signal=True)

            def emit_ELP(s):
                c0, cs, qs = geom(s)
                zb = s["zb"]
                lb = nxt("lp", 3)
                s["lb"] = lb
                sc.op("act", lambda e: e.activation(out=PS[7][:, cs], in_=PS[zb][:, cs], func=AF.Exp),
                      reads=[rPS[zb]], writes=[rPS[7]])
                sc.op("act", lambda e: e.activation(out=LP[lb][:, cs], in_=PS[7][:, cs], func=AF.Ln, bias=1.0),
                      reads=[rPS[7]], writes=[rLP[lb]])

            def emit_TriOnes(s):
                c0, cs, qs = geom(s)
                zb, lb = s["zb"], s["lb"]
                Z = PS[zb]
                sc.op("pe", lambda e: e.matmul(Z[:, cs], lhsT=NEGTRI, rhs=LP[lb][:, cs], start=False, stop=s["first"],
                                                skip_group_check=True),
                      reads=[rCST, rLP[lb]], writes=[rPS[zb]], signal=s["first"])
                if not s["first"]:
                    k_ = s["prev"]["lsb_out"]
                    sc.op("pe", lambda e: e.matmul(Z[:, cs], lhsT=NEGONES, rhs=LSUMB[k_][:, cs], start=False, stop=True,
                                                    skip_group_check=True),
                          reads=[rCST, rLSB[k_]], writes=[rPS[zb]], signal=True)

            def emit_AT(s):
                c0, cs, qs = geom(s)
                zb = s["zb"]
                ab = nxt("at", 2)
                s["ab"] = ab
                sc.op("act", lambda e: e.activation(out=AT[ab][:, cs], in_=PS[zb][:, cs], func=AF.Exp),
                      reads=[rPS[zb]], writes=[rAT[ab]])

            def emit_LSUM(s):
                if s["last"]:
                    return
                c0, cs, qs = geom(s)
                lb = s["lb"]
                if s["first"]:
                    sc.op("dve", lambda e: e.memset(LSUM[:, :], 0.0), writes=[rLS])
                sc.op("dve", lambda e: e.tensor_tensor(out=LSUM[:, cs], in0=LSUM[:, cs], in1=LP[lb][:, cs], op=ALU.add),
                      reads=[rLS, rLP[lb]], writes=[rLS])
                k_ = 1 - st_["lsb"]
                sc.op("dve", lambda e: e.tensor_copy(out=LSUMB[k_][:, :], in_=LSUM[:, :]), reads=[rLS], writes=[rLSB[k_]])
                st_["lsb"] = k_
                s["lsb_out"] = k_

            def emit_AV(s):
                c0, cs, qs = geom(s)
                hd, kb, c2, qb = s["hd"], s["kb"], s["c2"], s["hd"] % 2
                ob = 4 + c2
                ab = s["ab"]
                sc.op("pe", lambda e: e.matmul(PS[ob][:, cs], lhsT=V3[:, kb, hd * 128:(hd + 1) * 128], rhs=AT[ab][:, cs],
                                                start=s["first"], stop=s["last"], skip_group_check=True),
                      reads=[rV[kb], rAT[ab]], writes=[rPS[ob]], signal=s["last"])
                if s["last"]:
                    sc.op("dve", lambda e: e.tensor_tensor(out=OGT3[:, hd, c2 * 512:(c2 + 1) * 512], in0=PS[ob][:, :],
                                                            in1=SG[qb][:, c2 * 512:(c2 + 1) * 512], op=ALU.mult),
                          reads=[rPS[ob]] + rSG[qb], writes=[rOGT[hd]])

            bg_mm = []
            bg_ch = []

            def bg_add(items):
                for kind, f, rdy in items:
                    (bg_mm if kind == "mm" else bg_ch).append((f, rdy))

            def bg_flush():
                while bg_mm:
                    bg_mm.pop(0)[0]()
                while bg_ch:
                    bg_ch.pop(0)[0]()

            def bg_step():
                if bg_mm:
                    bg_mm.pop(0)[0]()
                if bg_ch and (bg_ch[0][1] is None or bg_ch[0][1]()):
                    bg_ch.pop(0)[0]()

            bg_add(proj_items(0, 8))
            bg_flush()
            emit_Z(steps[0])
            emit_ELP(steps[0])
            emit_LSUM(steps[0])
            for i in range(n):
                s = steps[i]
                if s["first"] and s["c2"] == 0 and s["hd"] + 1 < 8:
                    bg_add(proj_items(s["hd"] + 1, 2 if hf == 1 else 4))
                if i + 1 < n:
                    s1 = steps[i + 1]
                    if s1["first"] and s1["c2"] == 0:
                        bg_flush()
                    emit_Z(s1)
                emit_TriOnes(s)
                if i + 1 < n:
                    emit_ELP(steps[i + 1])
                emit_AT(s)
                if i + 1 < n:
                    emit_LSUM(steps[i + 1])
                if i >= 1:
                    emit_AV(steps[i - 1])
                bg_step()
            emit_AV(steps[n - 1])
            bank_set[0] = list(range(8))
            ple_load_pt(l, hf)
            for dc in range(2):
                ring_begin()
                so = [ring_next(widx["BO", l] + dc * 2 + kh) for kh in range(2)]
                ring_prefetch(2)
                for ii in range(8):
                    i = hf * 8 + ii
                    b = bank()
                    for k in range(8):
                        s_ = so[k // 4]
                        sc.op("pe", lambda e: e.matmul(PS[b][:, :], lhsT=OGT3[:, k, ii * 128:(ii + 1) * 128],
                                                        rhs=slot3(s_, 4)[:, k % 4, :], start=(k == 0), stop=(k == 7)),
                              reads=[rOGT[k], rSLOT[s_]], writes=[rPS[b]], signal=(k == 7))
                    hv = H3[:, i, dc * 512:(dc + 1) * 512]
                    sc.op("dve", lambda e: e.tensor_tensor(out=hv, in0=hv, in1=PS[b][:, :], op=ALU.add),
                          reads=[rPS[b], rH[i]], writes=[rH[i]])
                    if dc == 1 and "p" in OPT:
                        if ii >= 1:
                            ple_prep_tile(hf, ii - 1)
                        if ii == 7:
                            ple_prep_tile(hf, 7)
        phase_ple(l, hf)

    def final_half(hf):
        norm_stats(hf)
        sc.dma("sp", GB[:, :], vecs_d[:, VOFF["fing"]:VOFF["fing"] + 1024], "gb", writes=rGBL)
        for ii in range(8):
            i = hf * 8 + ii
            sc.op("dve", lambda e: e.scalar_tensor_tensor(out=H3[:, i, :], in0=H3[:, i, :], scalar=RS[:, i:i + 1], in1=GB[:, :],
                                                           op0=ALU.mult, op1=ALU.mult), reads=[rH[i], rRS] + rGBL, writes=[rH[i]])
            sc.dma("sp", y_d[i * 128:(i + 1) * 128, :], H3[:, i, :], "y", reads=[rH[i]], writes=[rY[i]])
        final_done[hf] = True

    def phase_final():
        for hf in range(2):
            if not final_done[hf]:
                final_half(hf)

    def store_raw():
        for i in range(16):
            sc.dma("sp", y_d[i * 128:(i + 1) * 128, :], H3[:, i, :], "y", reads=[rH[i]], writes=[rY[i]])

    next_norm_of = {}
    for l in layers:
        next_norm_of[(l, 0)] = (VOFF[f"ng{l}"], 1)
        if l == 1 and (2 in layers or 3 in layers):
            next_norm_of[(l, 1)] = (VOFF["kvg"], 0)
        elif (l + 1) in layers and l != 1:
            next_norm_of[(l, 1)] = (VOFF[f"ng{l + 1}"], 0)
    plan()
    for l in layers:
        for hf in range(2):
            if l < 2:
                phase_A(l, hf)
            else:
                phase_B(l, hf)
        if l == 1 and (2 in layers or 3 in layers):
            for hf in range(2):
                phase_kv(hf)
    if final:
        phase_final()
    else:
        store_raw()
    sc.wait_all("sp", rY)
    sc.wait_all("pool", [rDBG])
    assert ring_state["pos"] == len(seq), (ring_state, len(seq))
    return nc, es


def make_in_maps(inputs):
    f = lambda a: np.asarray(a, dtype=np.float32)
    x = f(inputs["x"]); p = f(inputs["p"])
    wts, _ = pack_weights(f(inputs["a_w_in"]), f(inputs["a_w_out"]), f(inputs["w_kv"]), f(inputs["b_w_in"]),
                          f(inputs["b_w_out"]), f(inputs["ple_w"]), f(inputs["ple_gate_w"]))
    vecs = pack_vecs(f(inputs["norm_g"]), f(inputs["kv_norm_g"]), f(inputs["final_g"]), f(inputs["a_ln_g"]),
                     f(inputs["a_ln_b"]), f(inputs["a_b_s"]))
    wst = np.ascontiguousarray(f(inputs["a_w_s"]).transpose(0, 3, 1, 2)).reshape(2 * 128, 1024)
    cst = make_consts()
    maps = []
    for b in range(8):
        pT = np.ascontiguousarray(p[:, b].reshape(4, S, 2, 128).transpose(0, 3, 2, 1)).reshape(4 * 128, 2 * S)
        maps.append({"x": np.ascontiguousarray(x[b]), "pT": pT, "wts": wts, "vecs": vecs, "wst": wst, "cst": cst})
    return maps


def kernel(**inputs):
    nc, es = build()
    maps = make_in_maps(inputs)
    res = run_bass_kernel_spmd(nc, maps, core_ids=list(range(8)))
    return np.stack([np.asarray(r["y"], dtype=np.float32) for r in res.results], axis=0)
```

```python
import numpy as np
OPT = "sp"
from contextlib import ExitStack
import concourse.bass as bass
import concourse.mybir as mybir
from concourse.bass_utils import run_bass_kernel_spmd

F32 = mybir.dt.float32
BF16 = mybir.dt.bfloat16
AF = mybir.ActivationFunctionType
ALU = mybir.AluOpType

S = 2048
D = 1024
HALF = 1024
EPS = 1e-6
NSLOT = 4
FAST_RECIP = True
SEM_CHUNK = 30000


class Tok:
    __slots__ = ("eng", "sem", "val")

    def __init__(self, eng):
        self.eng = eng
        self.sem = None
        self.val = None


class Reg:
    __slots__ = ("w", "rs", "name")

    def __init__(self, name=""):
        self.w = None
        self.rs = {}
        self.name = name


class Sched:
    def __init__(self, nc, es):
        self.nc = nc
        self.es = es
        self.E = {"pe": nc.tensor, "act": nc.scalar, "dve": nc.vector, "pool": nc.gpsimd, "sp": nc.sync}
        self.sems = {}
        self.cnt = {}
        self.pending = {k: [] for k in self.E}
        self.seen = {k: {} for k in self.E}
        self.nsem = 0
        for k in self.E:
            self._newsem(k)
        self.dsem = {}

    def _newsem(self, k):
        self.nsem += 1
        self.sems[k] = self.nc.alloc_semaphore(name=f"s_{k}_{self.nsem}")
        self.cnt[k] = 0

    def _wait(self, eng, toks):
        best = {}
        for t in toks:
            if t is None:
                continue
            assert t.sem is not None, f"unresolved token from {t.eng} needed by {eng}"
            key = t.sem
            if key not in best or best[key].val < t.val:
                best[key] = t
        for key, t in best.items():
            if self.seen[eng].get(key, 0) >= t.val:
                continue
            self.E[eng].wait_ge(t.sem, t.val)
            self.seen[eng][key] = t.val

    def _deps(self, eng, reads, writes):
        deps = []
        for r in reads:
            if r.w is not None:
                if r.w.eng == eng and eng == "pe":
                    continue
                deps.append(r.w)
        for r in writes:
            if r.w is not None and not (r.w.eng == eng and eng == "pe"):
                deps.append(r.w)
            for e2, t in r.rs.items():
                if not (e2 == eng and eng == "pe"):
                    deps.append(t)
        return deps

    def _mark(self, tok, reads, writes):
        for r in reads:
            r.rs[tok.eng] = tok
        for r in writes:
            r.w = tok
            r.rs = {}

    def op(self, eng, fn, reads=(), writes=(), signal=True):
        self._wait(eng, self._deps(eng, reads, writes))
        inst = fn(self.E[eng])
        tok = Tok(eng)
        if signal:
            if self.cnt[eng] >= SEM_CHUNK:
                self._newsem(eng)
            self.cnt[eng] += 1
            inst.then_inc(self.sems[eng], 1)
            tok.sem = self.sems[eng]
            tok.val = self.cnt[eng]
            for p in self.pending[eng]:
                p.sem = tok.sem
                p.val = tok.val
            self.pending[eng] = []
        else:
            self.pending[eng].append(tok)
        self._mark(tok, reads, writes)
        return tok

    def dma(self, q, out, in_, semname, reads=(), writes=()):
        self._wait(q, self._deps("dma:" + semname, reads, writes))
        if semname not in self.dsem:
            self.dsem[semname] = [self.nc.alloc_semaphore(name="d_" + semname), 0]
        ent = self.dsem[semname]
        ent[1] += 16
        self.E[q].dma_start(out=out, in_=in_).then_inc(ent[0], 16)
        tok = Tok("dma:" + semname)
        tok.sem = ent[0]
        tok.val = ent[1]
        self._mark(tok, reads, writes)
        return tok

    def wait_all(self, eng, regs):
        toks = []
        for r in regs:
            if r.w is not None:
                toks.append(r.w)
        self._wait(eng, toks)


def _kc(w, k0, nk, c0, ncol):
    return w[k0 * 128:(k0 + nk) * 128, c0:c0 + ncol].reshape(nk, 128, ncol).transpose(1, 0, 2)


def pack_weights(a_w_in, a_w_out, w_kv, b_w_in, b_w_out, ple_w, ple_gate_w):
    units = []
    idx = {}

    def add(a):
        units.append(np.ascontiguousarray(a, dtype=np.float32).reshape(128, 2048))

    def ple(l):
        idx["PW", l] = len(units)
        for kh in range(2):
            add(_kc(ple_gate_w[l], kh * 4, 4, 0, 512))
        add(np.stack([_kc(ple_w[l], 0, 2, dc * 512, 512) for dc in range(2)], axis=1))
        for kh in range(2):
            add(_kc(ple_gate_w[l], kh * 4, 4, 512, 512))

    for l in range(2):
        w = a_w_in[l]
        idx["WV", l] = len(units)
        for c in range(4):
            for kh in range(2):
                add(_kc(w, kh * 4, 4, 2048 + c * 512, 512))
        idx["WUG", l] = len(units)
        for j in range(16):
            add(np.concatenate([_kc(w, 0, 8, j * 128, 128), _kc(w, 0, 8, 4096 + j * 128, 128)], axis=2))
        idx["WO", l] = len(units)
        for dc in range(2):
            for kq in range(4):
                add(_kc(a_w_out[l], kq * 4, 4, dc * 512, 512))
        ple(l)
    idx["WK"] = len(units)
    for hp in range(4):
        add(np.stack([_kc(w_kv, 0, 8, (hp * 2 + e) * 128, 128) for e in range(2)], axis=1))
    idx["WVV"] = len(units)
    for vc in range(2):
        for kh in range(2):
            add(_kc(w_kv, kh * 4, 4, 1024 + vc * 512, 512))
    for j in range(2):
        l = 2 + j
        idx["QG", l] = len(units)
        for hd in range(8):
            add(np.concatenate([_kc(b_w_in[j], 0, 8, hd * 128, 128), _kc(b_w_in[j], 0, 8, 1024 + hd * 128, 128)], axis=2))
        idx["BO", l] = len(units)
        for dc in range(2):
            for kh in range(2):
                add(_kc(b_w_out[j], kh * 4, 4, dc * 512, 512))
        ple(l)
    return np.stack(units, axis=0).reshape(len(units) * 128, 2048), idx


VOFF = {}
_o = 0
for _n, _sz in [("ng0", 1024), ("ng1", 1024), ("ng2", 1024), ("ng3", 1024), ("kvg", 1024), ("fing", 1024),
                ("lng0", 2048), ("lng1", 2048), ("lnb0", 2048), ("lnb1", 2048), ("bs0", 1024), ("bs1", 1024)]:
    VOFF[_n] = _o
    _o += _sz
NV = _o


def pack_vecs(norm_g, kv_norm_g, final_g, a_ln_g, a_ln_b, a_b_s):
    v = np.concatenate([norm_g[0], norm_g[1], norm_g[2], norm_g[3], kv_norm_g, final_g,
                        a_ln_g[0], a_ln_g[1], a_ln_b[0], a_ln_b[1],
                        a_b_s[0].reshape(-1), a_b_s[1].reshape(-1)]).astype(np.float32)
    return np.ascontiguousarray(np.broadcast_to(v[None, :], (128, NV)))


def make_consts():
    c = np.zeros((128, 512), np.float32)
    i = np.arange(128)
    c[:, 0:128] = np.eye(128)
    c[:, 128:256] = -1.0 * (i[:, None] >= i[None, :])
    c[:, 256:384] = -1.0
    c[:, 384:512] = -30000.0 * (i[:, None] >= i[None, :])
    return c


NUNITS = 2 * (8 + 16 + 8 + 5) + 8 + 2 * (8 + 4 + 5)


def build(layers=(0, 1, 2, 3), final=True, dbg=False):
    nc = bass.Bass("TRN2", target_bir_lowering=False)
    es = ExitStack()
    sc = Sched(nc, es)
    _, widx = pack_weights(*[np.zeros(s, np.float32) for s in
                             [(2, 1024, 6144), (2, 2048, 1024), (1024, 2048), (2, 1024, 2048),
                              (2, 1024, 1024), (4, 256, 1024), (4, 1024, 1024)]])

    x_d = nc.dram_tensor("x", [S, D], F32, kind="ExternalInput").ap()
    pT_d = nc.dram_tensor("pT", [4 * 128, 2 * S], F32, kind="ExternalInput").ap()
    wts_d = nc.dram_tensor("wts", [NUNITS * 128, 2048], F32, kind="ExternalInput").ap()
    vecs_d = nc.dram_tensor("vecs", [128, NV], F32, kind="ExternalInput").ap()
    wst_d = nc.dram_tensor("wst", [2 * 128, 1024], F32, kind="ExternalInput").ap()
    cst_d = nc.dram_tensor("cst", [128, 512], F32, kind="ExternalInput").ap()
    y_d = nc.dram_tensor("y", [S, D], F32, kind="ExternalOutput").ap()
    dbg_d = {}
    rDBG = Reg("dbg")

    def dump(name, ap, regs):
        if not dbg:
            return
        shp = list(ap.shape)
        dbg_d[name] = nc.dram_tensor("dbg_" + name, shp, F32, kind="ExternalOutput").ap()
        sc.dma("pool", dbg_d[name], ap, "dbg", reads=regs, writes=[rDBG])

    def sb(name, shape, dt):
        return es.enter_context(nc.sbuf_tensor(name, shape, dt))

    H = sb("H", [128, 16 * 1024], F32)
    HT = sb("HT", [128, 8 * HALF], BF16)
    OGT = sb("OGT", [128, 8 * HALF], BF16)
    X = sb("X", [128, 32768], BF16)
    RING = sb("RING", [128, NSLOT * 2048], BF16)
    TAA = sb("TAA", [128, 1024], F32)
    GB = TAA
    TA = [TAA[:, 0:512], TAA[:, 512:1024]]
    HN = [sb(f"HN{i}", [128, 1024], BF16) for i in range(2)]
    CST = sb("CST", [128, 512], BF16)
    PT = sb("PT", [128, 2 * HALF], BF16)
    SS = sb("SS", [128, 16], F32)
    NHALF = sb("NHALF", [128, 16], F32)
    RS = sb("RS", [128, 16], F32)
    TBB = sb("TBB", [128, 2048], BF16)
    TB = [TBB[:, i * 512:(i + 1) * 512] for i in range(4)]
    PS = [es.enter_context(nc.psum_tensor(f"ps{i}", [128, 512], F32)) for i in range(8)]
    NQ = 512
    LSUM = sb("LSUM", [128, NQ], F32)
    LSUMB = [sb(f"LSUMB{i}", [128, NQ], BF16) for i in range(2)]
    EE = sb("EE", [128, NQ], F32)
    LP = [sb(f"LP{i}", [128, NQ], BF16) for i in range(3)]
    AT = [sb(f"AT{i}", [128, NQ], BF16) for i in range(2)]
    QT = [PT[:, 0:HALF], PT[:, HALF:2 * HALF]]
    SG = [TBB[:, 0:HALF], TBB[:, HALF:2 * HALF]]
    OGF = OGT[:, :].bitcast(F32)
    LNG = OGF[:, 0:2048]
    LNB = OGF[:, 2048:4096]
    XB = X[:, 16384:32768]
    VHAT2 = [XB[:, 0:2048], XB[:, 8192:10240]]
    WST = XB[:, 2048:3072]
    BSH = XB[:, 3072:4096]
    BSL = XB[:, 4096:5120]
    BSHL = XB[0:64, 5120:6144]
    ONES = XB[0:64, 6144:6272]
    BSR = XB[0:64, 3072:5120]
    BSF = XB[:, 8192:10240].bitcast(F32)
    SML = sb("SML", [128, 64], F32)
    ST8 = sb("ST8", [128, 192], F32)
    ST = SML[:, 0:24]
    MV8 = SML[:, 24:40]
    RSV8 = SML[:, 40:48]
    NMR8 = SML[:, 48:56]

    H3 = H[:, :].rearrange("p (i d) -> p i d", d=1024)
    HT3 = HT[:, :].rearrange("p (k t) -> p k t", t=HALF)
    OGT3 = OGT[:, :].rearrange("p (k t) -> p k t", t=HALF)
    PT3 = PT[:, :].rearrange("p (k t) -> p k t", t=HALF)
    IDENT = CST[:, 0:128]
    NEGTRI = CST[:, 128:256]
    NEGONES = CST[:, 256:384]
    NEGMASK = CST[:, 384:512]

    rH = [Reg(f"H{i}") for i in range(16)]
    rHT = [Reg(f"HT{i}") for i in range(8)]
    rOGT = [Reg(f"OGT{h}") for h in range(8)]
    rHN = [Reg(), Reg()]
    rCST = Reg("CST")
    rSS = Reg("SS")
    rRS = Reg("RS")
    rTA = [Reg(), Reg()]
    rTB = [Reg() for _ in range(4)]
    rGBL = rTA
    rQT = [Reg(), Reg()]
    rPTL = rQT
    rSG = [[rTB[0], rTB[1]], [rTB[2], rTB[3]]]
    rLS = Reg(); rLSB = [Reg(), Reg()]; rEE = Reg()
    rLP = [Reg(), Reg(), Reg()]; rAT = [Reg(), Reg()]
    rLNG = Reg(); rLNB = Reg(); rWST = Reg(); rBS = Reg(); rVH2 = [Reg(), rBS]; rBSHL = Reg(); rBSR = Reg()
    rST = Reg(); rMV = Reg(); rSV = Reg()
    rST8 = [Reg() for _ in range(8)]
    rX = [Reg(f"X{i}") for i in range(8)]
    rPS = [Reg(f"ps{i}") for i in range(8)]
    rY = [Reg(f"Y{i}") for i in range(16)]

    cnt = {"ta": 0, "tb": 0, "hn": 0, "bank": 0, "lp": 0, "at": 0, "ptmp": 0, "ta4": 0}

    def nxt(key, n):
        v = cnt[key] % n
        cnt[key] += 1
        return v

    bank_set = [list(range(8))]

    def bank():
        bs = bank_set[0]
        return bs[nxt("bank", len(bs))]

    seq = []
    seqA = []
    XBr = X[:, 16384:32768]
    slot_ap = [RING[:, i * 2048:(i + 1) * 2048] for i in range(NSLOT)] + \
              [XBr[:, 10240 + i * 2048:10240 + (i + 1) * 2048] for i in range(3)]
    NS_A = NSLOT + 3
    rSLOT = [Reg(f"slot{i}") for i in range(NS_A)]
    slot_occ = [-1] * NS_A
    unit_slot = {}
    ring_state = {"issued": 0, "pos": 0, "done": 0, "last": -1}

    def ring_try_issue(q):
        pool = NS_A if seqA[q] else NSLOT
        for d_ in range(1, pool + 1):
            s = (ring_state["last"] + d_) % pool
            if slot_occ[s] < ring_state["done"]:
                break
        else:
            return False
        u = seq[q]
        sc.dma("pool", slot_ap[s], wts_d[u * 128:(u + 1) * 128, :], f"ring{s}", writes=[rSLOT[s]])
        slot_occ[s] = q
        unit_slot[q] = s
        ring_state["last"] = s
        ring_state["issued"] += 1
        return True

    def ring_begin(keep=0):
        ring_state["done"] = ring_state["pos"] - keep

    def ring_next(u):
        q = ring_state["pos"]
        assert seq[q] == u, (q, seq[q], u)
        while ring_state["issued"] <= q:
            ok = ring_try_issue(ring_state["issued"])
            assert ok, "ring: no free slot for a unit that is needed now"
        ring_state["pos"] += 1
        return unit_slot[q]

    def ring_prefetch(*_):
        while ring_state["issued"] < len(seq) and ring_state["issued"] < ring_state["pos"] + NS_A:
            if not ring_try_issue(ring_state["issued"]):
                break

    def plan():
        def ext(r, isa):
            seq.extend(r)
            seqA.extend([isa] * len(r))
        for l in layers:
            for hf in range(2):
                if l < 2:
                    ext(range(widx["WV", l], widx["WV", l] + 8), l < 2)
                    ext(range(widx["WUG", l], widx["WUG", l] + 16), l < 2)
                    ext(range(widx["WO", l], widx["WO", l] + 8), l < 2)
                else:
                    if l == 2 or (l == 3 and 2 not in layers):
                        pass
                    ext(range(widx["QG", l], widx["QG", l] + 8), l < 2)
                    ext(range(widx["BO", l], widx["BO", l] + 4), l < 2)
                ext(range(widx["PW", l], widx["PW", l] + 5), l < 2)
            if l == 1 and (2 in layers or 3 in layers):
                for hf in range(2):
                    ext(range(widx["WK"], widx["WK"] + 4), False)
                    ext(range(widx["WVV"], widx["WVV"] + 4), False)

    def slot3(slot, k):
        return slot_ap[slot].rearrange("p (k c) -> p k c", k=k)

    rNH = Reg("nhalf")
    sc.op("pool", lambda e: e.memset(NHALF[:, :], -0.5), writes=[rNH])

    def rsqrt_inplace(ap, ncol, regs_rw):
        sc.op("pool", lambda e: e.tensor_tensor(out=ap, in0=ap, in1=NHALF[:, 0:ncol], op=ALU.pow),
              reads=regs_rw + [rNH], writes=regs_rw)

    sc.dma("pool", CST[:, :], cst_d[:, :], "cst", writes=[rCST])
    for i in range(16):
        sc.dma("sp", H3[:, i, :], x_d[i * 128:(i + 1) * 128, :], f"x{i}", writes=[rH[i]])

    JUNK = EE[:, :].bitcast(BF16)
    stats_ready = [False, False]

    def norm_stats(hf):
        if stats_ready[hf]:
            return
        for ii in range(8):
            i = hf * 8 + ii
            sc.op("act", lambda e: e.activation(out=JUNK, in_=H3[:, i, :], func=AF.Square, accum_out=SS[:, i:i + 1]),
                  reads=[rH[i]], writes=[rEE, rSS])
        sl = slice(hf * 8, hf * 8 + 8)
        sc.op("dve", lambda e: e.tensor_scalar(out=RS[:, sl], in0=SS[:, sl], scalar1=1.0 / 1024, scalar2=EPS,
                                                op0=ALU.mult, op1=ALU.add), reads=[rSS], writes=[rRS])
        rsqrt_inplace(RS[:, sl], 8, [rRS])
        stats_ready[hf] = True

    def phase_norm(goff, hf):
        if gb_loaded["goff"] == goff:
            gb_loaded["goff"] = None
        else:
            sc.dma("sp", GB[:, :], vecs_d[:, goff:goff + 1024], "gb", writes=rGBL)
        norm_stats(hf)
        for ii in range(8):
            i = hf * 8 + ii
            hb = ii % 2
            sc.op("dve", lambda e: e.scalar_tensor_tensor(out=HN[hb][:, :], in0=H3[:, i, :], scalar=RS[:, i:i + 1],
                                                           in1=GB[:, :], op0=ALU.mult, op1=ALU.mult),
                  reads=[rH[i], rRS] + rGBL, writes=[rHN[hb]])
            transpose_tile(HN[hb], rHN[hb], 8, None, [rHT[ii]], HT3[:, :, ii * 128:(ii + 1) * 128])

    def ple_prep_tile(hf, ii):
        i = hf * 8 + ii
        hb = nxt("hn", 2)
        sc.op("dve", lambda e: e.tensor_copy(out=HN[hb][:, :], in_=H3[:, i, :]), reads=[rH[i]], writes=[rHN[hb]])
        transpose_tile(HN[hb], rHN[hb], 8, None, [rHT[ii]], HT3[:, :, ii * 128:(ii + 1) * 128])

    def ple_load_pt(l, hf):
        sc.dma("pool", PT3[:, :, :], pT_d[l * 128:(l + 1) * 128, :].rearrange("p (k t) -> p k t", t=S)[:, :, hf * HALF:(hf + 1) * HALF],
               "pt", writes=rPTL)

    def transpose_tile(src, rsrc, nk, dst_fn, wregs, dst_all):
        b = bank()
        pv = PS[b][:, :].bitcast(BF16).rearrange("p (k t) -> p k t", t=128)
        for k in range(nk):
            sc.op("pe", lambda e: e.transpose(out=pv[:, k, :], in_=src[:, k * 128:(k + 1) * 128], identity=IDENT),
                  reads=[rsrc, rCST], writes=[rPS[b]], signal=(k == nk - 1))
        sc.op("act", lambda e: e.copy(out=dst_all, in_=pv[:, 0:nk, :]), reads=[rPS[b]], writes=wregs)

    def phase_A(l, hf):
        if True:
            X3 = X[:, 0:16384].rearrange("p (i f) -> p i f", f=2048)
            WST3 = WST.rearrange("p (g t) -> p g t", t=128)
            BSHL3 = BSHL.rearrange("p (g t) -> p g t", t=128)

            if hf == 0:
                sc.dma("sp", LNG, vecs_d[:, VOFF[f"lng{l}"]:VOFF[f"lng{l}"] + 2048], "lng", writes=[rLNG])
                sc.dma("sp", LNB, vecs_d[:, VOFF[f"lnb{l}"]:VOFF[f"lnb{l}"] + 2048], "lnb", writes=[rLNB])
                sc.dma("pool", WST, wst_d[l * 128:(l + 1) * 128, :], "wst", writes=[rWST])
                sc.op("dve", lambda e: e.memset(WST3[64:128, :, 0:64], 0.0), writes=[rWST])
                sc.dma("sp", BSF, vecs_d[:, VOFF[f"bs{l}"]:VOFF[f"bs{l}"] + 1024], "bsf", writes=[rBS])
                sc.op("dve", lambda e: e.tensor_copy(out=BSH, in_=BSF), reads=[rBS], writes=[rBS])
                sc.op("dve", lambda e: e.tensor_tensor(out=BSF, in0=BSF, in1=BSH, op=ALU.subtract),
                      reads=[rBS], writes=[rBS])
                sc.op("dve", lambda e: e.tensor_copy(out=BSL, in_=BSF), reads=[rBS], writes=[rBS])
                sc.op("dve", lambda e: e.memset(BSHL, 0.0), writes=[rBSHL])
                sc.op("dve", lambda e: e.memset(ONES, 0.0), writes=[rBSHL])
                sc.op("dve", lambda e: e.tensor_copy(out=BSHL[0:1, :], in_=BSH[0:1, :]), reads=[rBS], writes=[rBSHL])
                sc.op("dve", lambda e: e.tensor_copy(out=BSHL[32:33, :], in_=BSL[32:33, :]), reads=[rBS], writes=[rBSHL])
                sc.op("dve", lambda e: e.memset(ONES[0:1, :], 1.0), writes=[rBSHL])
                sc.op("dve", lambda e: e.memset(ONES[32:33, :], 1.0), writes=[rBSHL])
                BSR4 = BSR.rearrange("p (g d t) -> p g d t", d=2, t=128)
                for d_ in range(2):
                    sc.op("dve", lambda e: e.tensor_copy(out=BSR4[:, :, d_, :], in_=BSHL3), reads=[rBSHL], writes=[rBS, rBSR])

            phase_norm(VOFF[f"ng{l}"], hf)
            if "s" in OPT:
                norm_stats(1 - hf)
            if l == 0 and hf == 0:
                dump("HT", HT[:, :], rHT)

            MV83 = MV8.rearrange("p (i t) -> p i t", t=2)

            def stats_batch(t0, t1):
                sl_ = slice(t0, t1)
                sc.op("dve", lambda e: e.tensor_scalar(out=RSV8[:, sl_], in0=MV83[:, sl_, 1], scalar1=EPS, scalar2=None, op0=ALU.add),
                      reads=[rMV], writes=[rSV])
                rsqrt_inplace(RSV8[:, sl_], t1 - t0, [rSV])
                sc.op("dve", lambda e: e.scalar_tensor_tensor(out=NMR8[:, sl_], in0=MV83[:, sl_, 0], scalar=-1.0, in1=RSV8[:, sl_],
                                                               op0=ALU.mult, op1=ALU.mult), reads=[rMV, rSV], writes=[rSV])

            TA4 = [TA[0], TA[1], EE[:, :], LSUM[:, :]]
            rTA4 = [rTA[0], rTA[1], rEE, rLS]

            def ps_produce(ii):
                VHAT = VHAT2[ii % 2]
                rVH = rVH2[ii % 2]
                for q in range(4):
                    ta = nxt("ta4", 4)
                    T_, rT_ = TA4[ta], rTA4[ta]
                    qs = slice(q * 512, (q + 1) * 512)
                    sc.op("act", lambda e: e.activation(out=T_, in_=X3[:, ii, qs], func=AF.Identity,
                                                         bias=NMR8[:, ii:ii + 1], scale=RSV8[:, ii:ii + 1]),
                          reads=[rX[ii], rSV], writes=[rT_])
                    sc.op("dve", lambda e: e.tensor_tensor(out=T_, in0=T_, in1=LNG[:, qs], op=ALU.mult),
                          reads=[rT_, rLNG], writes=[rT_])
                    sc.op("dve", lambda e: e.tensor_tensor(out=VHAT[:, qs], in0=T_, in1=LNB[:, qs], op=ALU.add),
                          reads=[rT_, rLNB], writes=[rVH])

            def ps_consume(ii):
                VHAT = VHAT2[ii % 2]
                rVH = rVH2[ii % 2]
                Xs = X3[:, ii, :].rearrange("p (j t) -> p j t", t=128)
                for jb in range(4):
                    b = bank()
                    sc.op("pe", lambda e: e.matmul(PS[b][:, :], lhsT=ONES[0:33, :], rhs=BSR[0:33, jb * 512:(jb + 1) * 512],
                                                    start=True, stop=False, skip_group_check=True),
                          reads=[rBSHL, rBSR], writes=[rPS[b]], signal=False)
                    for jj in range(4):
                        j = jb * 4 + jj
                        g = j // 2
                        sc.op("pe", lambda e: e.matmul(PS[b][:, jj * 128:(jj + 1) * 128], lhsT=VHAT[:, j * 128:(j + 1) * 128],
                                                        rhs=WST3[:, g, :], start=False, stop=(jj == 3), skip_group_check=True),
                              reads=[rVH, rWST], writes=[rPS[b]], signal=(jj == 3))
                    sc.op("act", lambda e: e.copy(out=Xs[:, jb * 4:(jb + 1) * 4, :],
                                                   in_=PS[b][:, :].rearrange("p (j t) -> p j t", t=128)),
                          reads=[rPS[b]], writes=[rX[ii]])

            for c in range(4):
                ring_begin()
                sA = ring_next(widx["WV", l] + c * 2)
                sB = ring_next(widx["WV", l] + c * 2 + 1)
                ring_prefetch(2)
                for ii in range(8):
                    b = bank()
                    for k in range(8):
                        s_ = sA if k < 4 else sB
                        sc.op("pe", lambda e: e.matmul(PS[b][:, :], lhsT=HT3[:, k, ii * 128:(ii + 1) * 128],
                                                        rhs=slot3(s_, 4)[:, k % 4, :], start=(k == 0), stop=(k == 7)),
                              reads=[rHT[ii], rSLOT[s_]], writes=[rPS[b]], signal=(k == 7))
                    sc.op("act", lambda e: e.activation(out=X3[:, ii, c * 512:(c + 1) * 512], in_=PS[b][:, :], func=AF.Gelu),
                          reads=[rPS[b]], writes=[rX[ii]])
                    sc.op("dve", lambda e: e.bn_stats(out=ST8[:, ii * 24 + c * 6:ii * 24 + (c + 1) * 6],
                                                       in_=X3[:, ii, c * 512:(c + 1) * 512]),
                          reads=[rX[ii]], writes=[rST8[ii]])
                    if c == 3:
                        sc.op("dve", lambda e: e.bn_aggr(out=MV8[:, ii * 2:(ii + 1) * 2], in_=ST8[:, ii * 24:(ii + 1) * 24]),
                              reads=[rST8[ii]], writes=[rMV])
                        if ii == 3:
                            stats_batch(0, 4)
                            ps_produce(0)
                            ps_produce(1)
                        if ii == 7:
                            stats_batch(4, 8)
            if l == 0 and hf == 0:
                dump("GV", X[:, 0:16384], rX)
            for ii in range(8):
                ps_consume(ii)
                if ii + 2 < 8:
                    ps_produce(ii + 2)

            if l == 0 and hf == 0:
                dump("SVT", X[:, 0:16384], rX)
            for j in range(16):
                ring_begin()
                s_ = ring_next(widx["WUG", l] + j)
                ring_prefetch(1)
                w3 = slot3(s_, 8)
                for st in range(2):
                    bu = bank()
                    for k in range(8):
                        sc.op("pe", lambda e: e.matmul(PS[bu][:, :], lhsT=w3[:, k, 0:128], rhs=HT3[:, k, st * 512:(st + 1) * 512],
                                                        start=(k == 0), stop=(k == 7)),
                              reads=[rSLOT[s_]] + rHT[st * 4:(st + 1) * 4], writes=[rPS[bu]], signal=(k == 7))
                    bg = bank()
                    for k in range(8):
                        sc.op("pe", lambda e: e.matmul(PS[bg][:, :], lhsT=w3[:, k, 128:256], rhs=HT3[:, k, st * 512:(st + 1) * 512],
                                                        start=(k == 0), stop=(k == 7)),
                              reads=[rSLOT[s_]] + rHT[st * 4:(st + 1) * 4], writes=[rPS[bg]], signal=(k == 7))
                    tb = nxt("tb", 4)
                    sc.op("act", lambda e: e.activation(out=TB[tb][:, :], in_=PS[bu][:, :], func=AF.Gelu),
                          reads=[rPS[bu]], writes=[rTB[tb]])
                    ta2 = nxt("ta", 2)
                    tb2 = nxt("tb", 4)
                    sc.op("act", lambda e: e.activation(out=TA[ta2][:, :], in_=PS[bg][:, :], func=AF.Tanh, scale=0.5),
                          reads=[rPS[bg]], writes=[rTA[ta2]])
                    sc.op("dve", lambda e: e.scalar_tensor_tensor(out=TB[tb2][:, :], in0=TA[ta2][:, :], scalar=1.0, in1=PS[bg][:, :],
                                                                   op0=ALU.add, op1=ALU.mult),
                          reads=[rTA[ta2], rPS[bg]], writes=[rTB[tb2]])
                    sc.op("dve", lambda e: e.tensor_tensor(out=TB[tb][:, :], in0=TB[tb][:, :], in1=TB[tb2][:, :], op=ALU.mult),
                          reads=[rTB[tb], rTB[tb2]], writes=[rTB[tb]])
                    xv = X3[:, st * 4:(st + 1) * 4, j * 128:(j + 1) * 128]
                    sc.op("dve", lambda e: e.scalar_tensor_tensor(out=xv, in0=TB[tb][:, :].rearrange("p (i t) -> p i t", t=128),
                                                                   scalar=0.5, in1=xv, op0=ALU.mult, op1=ALU.mult),
                          reads=[rTB[tb]] + rX[st * 4:(st + 1) * 4], writes=rX[st * 4:(st + 1) * 4])

            if l == 0 and hf == 0:
                dump("GAT", X[:, 0:16384], rX)
            ple_load_pt(l, hf)
            for dc in range(2):
                ring_begin()
                ss_ = [ring_next(widx["WO", l] + dc * 4 + kq) for kq in range(4)]
                ring_prefetch(4)
                for ii in range(8):
                    i = hf * 8 + ii
                    b = bank()
                    for k in range(16):
                        s_ = ss_[k // 4]
                        sc.op("pe", lambda e: e.matmul(PS[b][:, :], lhsT=X3[:, ii, k * 128:(k + 1) * 128],
                                                        rhs=slot3(s_, 4)[:, k % 4, :], start=(k == 0), stop=(k == 15)),
                              reads=[rX[ii], rSLOT[s_]], writes=[rPS[b]], signal=(k == 15))
                    hv = H3[:, i, dc * 512:(dc + 1) * 512]
                    sc.op("dve", lambda e: e.tensor_tensor(out=hv, in0=hv, in1=PS[b][:, :], op=ALU.add),
                          reads=[rPS[b], rH[i]], writes=[rH[i]])
                    if dc == 1 and "p" in OPT:
                        if ii >= 1:
                            ple_prep_tile(hf, ii - 1)
                        if ii == 7:
                            ple_prep_tile(hf, 7)
        if l == 0 and hf == 0:
            dump("H1", H[:, 0:8192], rH[0:8])
        phase_ple(l, hf)

    gb_loaded = {"goff": None}
    final_done = [False, False]
    rFS = [Reg(f"fs{i}") for i in range(16)]

    def final_tile(i):
        hb = i % 2
        sc.op("act", lambda e: e.activation(out=HN[hb][:, :], in_=H3[:, i, :], func=AF.Square, accum_out=SS[:, i:i + 1]),
              reads=[rH[i]], writes=[rHN[hb], rFS[i]])
        sc.op("dve", lambda e: e.tensor_scalar(out=RS[:, i:i + 1], in0=SS[:, i:i + 1], scalar1=1.0 / 1024, scalar2=EPS,
                                                op0=ALU.mult, op1=ALU.add), reads=[rFS[i]], writes=[rFS[i]])
        rsqrt_inplace(RS[:, i:i + 1], 1, [rFS[i]])
        sc.op("dve", lambda e: e.scalar_tensor_tensor(out=H3[:, i, :], in0=H3[:, i, :], scalar=RS[:, i:i + 1], in1=GB[:, :],
                                                       op0=ALU.mult, op1=ALU.mult), reads=[rH[i], rFS[i]] + rGBL, writes=[rH[i]])
        sc.dma("sp", y_d[i * 128:(i + 1) * 128, :], H3[:, i, :], "y", reads=[rH[i]], writes=[rY[i]])

    def phase_ple(l, hf):
        nn = next_norm_of.get((l, hf))
        fin_here = final and l == 3 and hf == 1
        if nn is not None:
            sc.dma("sp", GB[:, :], vecs_d[:, nn[0]:nn[0] + 1024], "gb", writes=rGBL)
            gb_loaded["goff"] = nn[0]
        elif fin_here:
            sc.dma("sp", GB[:, :], vecs_d[:, VOFF["fing"]:VOFF["fing"] + 1024], "gb", writes=rGBL)
        if "p" not in OPT:
            for ii in range(8):
                ple_prep_tile(hf, ii)
        base = widx["PW", l]
        sw = None
        wp = None
        for dc in range(2):
            if dc == 0:
                ring_begin()
                sg = [ring_next(base + 0), ring_next(base + 1)]
                sw = ring_next(base + 2)
                wp = slot_ap[sw].rearrange("p (d k c) -> p d k c", d=2, k=2)
            else:
                ring_begin(keep=1)
                sg = [ring_next(base + 3), ring_next(base + 4)]
            ring_prefetch(3)
            for ii in range(8):
                i = hf * 8 + ii
                bg = bank()
                for k in range(8):
                    s_ = sg[k // 4]
                    sc.op("pe", lambda e: e.matmul(PS[bg][:, :], lhsT=HT3[:, k, ii * 128:(ii + 1) * 128],
                                                    rhs=slot3(s_, 4)[:, k % 4, :], start=(k == 0), stop=(k == 7)),
                          reads=[rHT[ii], rSLOT[s_]], writes=[rPS[bg]], signal=(k == 7))
                bp = bank()
                for k in range(2):
                    sc.op("pe", lambda e: e.matmul(PS[bp][:, :], lhsT=PT3[:, k, ii * 128:(ii + 1) * 128],
                                                    rhs=wp[:, dc, k, :], start=(k == 0), stop=(k == 1)),
                          reads=rPTL + [rSLOT[sw]], writes=[rPS[bp]], signal=(k == 1))
                tp_ = nxt("ptmp", 2)
                T_ = (EE, LSUM)[tp_]
                rT_ = (rEE, rLS)[tp_]
                sc.op("act", lambda e: e.activation(out=T_[:, :], in_=PS[bg][:, :], func=AF.Tanh, scale=0.5),
                      reads=[rPS[bg]], writes=[rT_])
                sc.op("dve", lambda e: e.scalar_tensor_tensor(out=T_[:, :], in0=T_[:, :], scalar=1.0, in1=PS[bp][:, :],
                                                               op0=ALU.add, op1=ALU.mult),
                      reads=[rT_, rPS[bp]], writes=[rT_])
                hv = H3[:, i, dc * 512:(dc + 1) * 512]
                sc.op("dve", lambda e: e.scalar_tensor_tensor(out=hv, in0=T_[:, :], scalar=0.5, in1=hv, op0=ALU.mult, op1=ALU.add),
                      reads=[rT_, rH[i]], writes=[rH[i]])
                if fin_here and dc == 1:
                    final_tile(i)
        stats_ready[hf] = False
        if fin_here:
            final_done[hf] = True

    KT3 = X[:, 0:16384].rearrange("p (h t) -> p h t", t=S)
    V3 = X[:, 16384:32768].rearrange("p (i f) -> p i f", f=1024)
    rKT = [Reg(f"KT{h}") for h in range(8)]
    rV = [Reg(f"V{i}") for i in range(16)]

    def phase_kv(hf):
        phase_norm(VOFF["kvg"], hf)
        if hf == 0:
            norm_stats(1)
            gb_loaded["goff"] = VOFF["kvg"]
        elif 2 in layers:
            sc.dma("sp", GB[:, :], vecs_d[:, VOFF["ng2"]:VOFF["ng2"] + 1024], "gb", writes=rGBL)
            gb_loaded["goff"] = VOFF["ng2"]
        for hp in range(4):
            ring_begin()
            s_ = ring_next(widx["WK"] + hp)
            ring_prefetch(1)
            w4 = slot_ap[s_].rearrange("p (e k c) -> p e k c", e=2, k=8)
            for e_ in range(2):
                hd = hp * 2 + e_
                for st in range(2):
                    b = bank()
                    for k in range(8):
                        sc.op("pe", lambda e: e.matmul(PS[b][:, :], lhsT=w4[:, e_, k, :], rhs=HT3[:, k, st * 512:(st + 1) * 512],
                                                        start=(k == 0), stop=(k == 7)),
                              reads=[rSLOT[s_]] + rHT[st * 4:(st + 1) * 4], writes=[rPS[b]], signal=(k == 7))
                    c0 = hf * HALF + st * 512
                    sc.op("act", lambda e: e.copy(out=KT3[:, hd, c0:c0 + 512], in_=PS[b][:, :]), reads=[rPS[b]], writes=[rKT[hd]])
        for vc in range(2):
            ring_begin()
            sv_ = [ring_next(widx["WVV"] + vc * 2 + kh) for kh in range(2)]
            ring_prefetch(2)
            for ii in range(8):
                i = hf * 8 + ii
                b = bank()
                for k in range(8):
                    s_ = sv_[k // 4]
                    sc.op("pe", lambda e: e.matmul(PS[b][:, :], lhsT=HT3[:, k, ii * 128:(ii + 1) * 128],
                                                    rhs=slot3(s_, 4)[:, k % 4, :], start=(k == 0), stop=(k == 7)),
                          reads=[rHT[ii], rSLOT[s_]], writes=[rPS[b]], signal=(k == 7))
                sc.op("dve", lambda e: e.tensor_copy(out=V3[:, i, vc * 512:(vc + 1) * 512], in_=PS[b][:, :]),
                      reads=[rPS[b]], writes=[rV[i]] + rSLOT[NSLOT:])

    def phase_B(l, hf):
        if True:
            SCALE = 1.0 / np.sqrt(128.0)

            phase_norm(VOFF[f"ng{l}"], hf)
            if "s" in OPT:
                norm_stats(1 - hf)
            if final and l == 3 and hf == 1:
                final_half(0)
            bank_set[0] = [0, 1, 2, 3, 6]

            def proj_items(hd, per):
                st8 = {}
                qb = hd % 2

                def start():
                    ring_begin()
                    st8["s"] = ring_next(widx["QG", l] + hd)
                    ring_prefetch()

                def group(st, isg):
                    g8 = {}

                    def mm(k0, k1):
                        def f():
                            if "s" not in st8:
                                start()
                            s_ = st8["s"]
                            w3 = slot3(s_, 8)
                            if "b" not in g8:
                                g8["b"] = bank()
                            b = g8["b"]
                            co = 128 if isg else 0
                            for k in range(k0, k1):
                                sc.op("pe", lambda e: e.matmul(PS[b][:, :], lhsT=w3[:, k, co:co + 128],
                                                                rhs=HT3[:, k, st * 512:(st + 1) * 512], start=(k == 0), stop=(k == 7)),
                                      reads=[rSLOT[s_]] + rHT[st * 4:(st + 1) * 4], writes=[rPS[b]], signal=(k == 7))
                            if k1 == 8:
                                if not isg:
                                    sc.op("dve", lambda e: e.tensor_scalar(out=QT[qb][:, st * 512:(st + 1) * 512], in0=PS[b][:, :],
                                                                            scalar1=float(SCALE), scalar2=None, op0=ALU.mult),
                                          reads=[rPS[b]], writes=[rQT[qb]])
                                else:
                                    ta = nxt("ta", 2)
                                    sc.op("act", lambda e: e.activation(out=TA[ta][:, :], in_=PS[b][:, :], func=AF.Exp, scale=-1.0),
                                          reads=[rPS[b]], writes=[rTA[ta]])
                                    sc.op("dve", lambda e: e.tensor_scalar(out=TA[ta][:, :], in0=TA[ta][:, :], scalar1=1.0, scalar2=None,
                                                                            op0=ALU.add), reads=[rTA[ta]], writes=[rTA[ta]])
                                    g8["ta"] = ta
                                    sc.op("dve", lambda e: e.tensor_copy(out=SG[qb][:, st * 512:(st + 1) * 512], in_=PS[b][:, :]),
                                          reads=[rPS[b]], writes=rSG[qb])
                        return f
                    its = [("mm", mm(k0, min(8, k0 + per)), None) for k0 in range(0, 8, per)]
                    if isg:
                        def rc(q):
                            def f():
                                ta = g8["ta"]
                                sl_ = slice(q * 128, (q + 1) * 128)
                                sc.op("dve", lambda e: e.reciprocal(out=TA[ta][:, sl_], in_=TA[ta][:, sl_]),
                                      reads=[rTA[ta]], writes=[rTA[ta]])
                            return f

                        def fin():
                            ta = g8["ta"]
                            sgv = SG[qb][:, st * 512:(st + 1) * 512]
                            sc.op("dve", lambda e: e.tensor_tensor(out=sgv, in0=sgv, in1=TA[ta][:, :], op=ALU.mult),
                                  reads=[rTA[ta]] + rSG[qb], writes=rSG[qb])
                        rdy = lambda: "ta" in g8
                        its += [("ch", rc(q), rdy) for q in range(4)] + [("ch", fin, rdy)]
                    return its
                items = []
                for st in range(2):
                    items += group(st, False)
                for st in range(2):
                    items += group(st, True)
                return items

            steps = []
            for hd in range(8):
                for c2 in range(2):
                    c = hf * 2 + c2
                    nkb = 4 * c + 4
                    prev = None
                    for kb in range(nkb - 1, -1, -1):
                        s = dict(hd=hd, c2=c2, c=c, kb=kb, first=(kb == nkb - 1), last=(kb == 0), idx=nkb - 1 - kb,
                                 prev=prev, nxt_last=(kb == 1))
                        steps.append(s)
                        prev = s
            n = len(steps)
            st_ = {"lsb": 0}

            def geom(s):
                c0 = max(0, s["kb"] - 4 * s["c"]) * 128
                return c0, slice(c0, NQ), slice(s["c2"] * 512 + c0, s["c2"] * 512 + NQ)

            def emit_Z(s):
                c0, cs, qs = geom(s)
                hd, kb, qb = s["hd"], s["kb"], s["hd"] % 2
                zb = bank()
                s["zb"] = zb
                Z = PS[zb]
                diag = kb >= 4 * s["c"]
                sc.op("pe", lambda e: e.matmul(Z[:, cs], lhsT=KT3[:, hd, kb * 128:(kb + 1) * 128], rhs=QT[qb][:, qs],
                                                start=True, stop=not diag, skip_group_check=True),
                      reads=[rKT[hd], rQT[qb]], writes=[rPS[zb]], signal=not diag)
                if diag:
                    sc.op("pe", lambda e: e.matmul(Z[:, c0:c0 + 128], lhsT=IDENT, rhs=NEGMASK, start=False, stop=True,
                                                    skip_group_check=True),
                          reads=[rCST], writes=[rPS[zb]], signal=True)

            def emit_ELP(s):
                c0, cs, qs = geom(s)
                zb = s["zb"]
                lb = nxt("lp", 3)
                s["lb"] = lb
                sc.op("act", lambda e: e.activation(out=PS[7][:, cs], in_=PS[zb][:, cs], func=AF.Exp),
                      reads=[rPS[zb]], writes=[rPS[7]])
                sc.op("act", lambda e: e.activation(out=LP[lb][:, cs], in_=PS[7][:, cs], func=AF.Ln, bias=1.0),
                      reads=[rPS[7]], writes=[rLP[lb]])

            def emit_TriOnes(s):
                c0, cs, qs = geom(s)
                zb, lb = s["zb"], s["lb"]
                Z = PS[zb]
                sc.op("pe", lambda e: e.matmul(Z[:, cs], lhsT=NEGTRI, rhs=LP[lb][:, cs], start=False, stop=s["first"],
                                                skip_group_check=True),
                      reads=[rCST, rLP[lb]], writes=[rPS[zb]], signal=s["first"])
                if not s["first"]:
                    k_ = s["prev"]["lsb_out"]
                    sc.op("pe", lambda e: e.matmul(Z[:, cs], lhsT=NEGONES, rhs=LSUMB[k_][:, cs], start=False, stop=True,
                                                    skip_group_check=True),
                          reads=[rCST, rLSB[k_]], writes=[rPS[zb]], signal=True)

            def emit_AT(s):
                c0, cs, qs = geom(s)
                zb = s["zb"]
                ab = nxt("at", 2)
                s["ab"] = ab
                sc.op("act", lambda e: e.activation(out=AT[ab][:, cs], in_=PS[zb][:, cs], func=AF.Exp),
                      reads=[rPS[zb]], writes=[rAT[ab]])

            def emit_LSUM(s):
                if s["last"]:
                    return
                c0, cs, qs = geom(s)
                lb = s["lb"]
                if s["first"]:
                    sc.op("dve", lambda e: e.memset(LSUM[:, :], 0.0), writes=[rLS])
                sc.op("dve", lambda e: e.tensor_tensor(out=LSUM[:, cs], in0=LSUM[:, cs], in1=LP[lb][:, cs], op=ALU.add),
                      reads=[rLS, rLP[lb]], writes=[rLS])
                k_ = 1 - st_["lsb"]
                sc.op("dve", lambda e: e.tensor_copy(out=LSUMB[k_][:, :], in_=LSUM[:, :]), reads=[rLS], writes=[rLSB[k_]])
                st_["lsb"] = k_
                s["lsb_out"] = k_

            def emit_AV(s):
                c0, cs, qs = geom(s)
                hd, kb, c2, qb = s["hd"], s["kb"], s["c2"], s["hd"] % 2
                ob = 4 + c2
                ab = s["ab"]
                sc.op("pe", lambda e: e.matmul(PS[ob][:, cs], lhsT=V3[:, kb, hd * 128:(hd + 1) * 128], rhs=AT[ab][:, cs],
                                                start=s["first"], stop=s["last"], skip_group_check=True),
                      reads=[rV[kb], rAT[ab]], writes=[rPS[ob]], signal=s["last"])
                if s["last"]:
                    sc.op("dve", lambda e: e.tensor_tensor(out=OGT3[:, hd, c2 * 512:(c2 + 1) * 512], in0=PS[ob][:, :],
                                                            in1=SG[qb][:, c2 * 512:(c2 + 1) * 512], op=ALU.mult),
                          reads=[rPS[ob]] + rSG[qb], writes=[rOGT[hd]])

            bg_mm = []
            bg_ch = []

            def bg_add(items):
                for kind, f, rdy in items:
                    (bg_mm if kind == "mm" else bg_ch).append((f, rdy))

            def bg_flush():
                while bg_mm:
                    bg_mm.pop(0)[0]()
                while bg_ch:
                    bg_ch.pop(0)[0]()

            def bg_step():
                if bg_mm:
                    bg_mm.pop(0)[0]()
                if bg_ch and (bg_ch[0][1] is None or bg_ch[0][1]()):
                    bg_ch.pop(0)[0]()

            bg_add(proj_items(0, 8))
            bg_flush()
            emit_Z(steps[0])
            emit_ELP(steps[0])
            emit_LSUM(steps[0])
            for i in range(n):
                s = steps[i]
                if s["first"] and s["c2"] == 0 and s["hd"] + 1 < 8:
                    bg_add(proj_items(s["hd"] + 1, 2 if hf == 1 else 4))
                if i + 1 < n:
                    s1 = steps[i + 1]
                    if s1["first"] and s1["c2"] == 0:
                        bg_flush()
                    emit_Z(s1)
                emit_TriOnes(s)
                if i + 1 < n:
                    emit_ELP(steps[i + 1])
                emit_AT(s)
                if i + 1 < n:
                    emit_LSUM(steps[i + 1])
                if i >= 1:
                    emit_AV(steps[i - 1])
                bg_step()
            emit_AV(steps[n - 1])
            bank_set[0] = list(range(8))
            ple_load_pt(l, hf)
            for dc in range(2):
                ring_begin()
                so = [ring_next(widx["BO", l] + dc * 2 + kh) for kh in range(2)]
                ring_prefetch(2)
                for ii in range(8):
                    i = hf * 8 + ii
                    b = bank()
                    for k in range(8):
                        s_ = so[k // 4]
                        sc.op("pe", lambda e: e.matmul(PS[b][:, :], lhsT=OGT3[:, k, ii * 128:(ii + 1) * 128],
                                                        rhs=slot3(s_, 4)[:, k % 4, :], start=(k == 0), stop=(k == 7)),
                              reads=[rOGT[k], rSLOT[s_]], writes=[rPS[b]], signal=(k == 7))
                    hv = H3[:, i, dc * 512:(dc + 1) * 512]
                    sc.op("dve", lambda e: e.tensor_tensor(out=hv, in0=hv, in1=PS[b][:, :], op=ALU.add),
                          reads=[rPS[b], rH[i]], writes=[rH[i]])
                    if dc == 1 and "p" in OPT:
                        if ii >= 1:
                            ple_prep_tile(hf, ii - 1)
                        if ii == 7:
                            ple_prep_tile(hf, 7)
        phase_ple(l, hf)

    def final_half(hf):
        norm_stats(hf)
        sc.dma("sp", GB[:, :], vecs_d[:, VOFF["fing"]:VOFF["fing"] + 1024], "gb", writes=rGBL)
        for ii in range(8):
            i = hf * 8 + ii
            sc.op("dve", lambda e: e.scalar_tensor_tensor(out=H3[:, i, :], in0=H3[:, i, :], scalar=RS[:, i:i + 1], in1=GB[:, :],
                                                           op0=ALU.mult, op1=ALU.mult), reads=[rH[i], rRS] + rGBL, writes=[rH[i]])
            sc.dma("sp", y_d[i * 128:(i + 1) * 128, :], H3[:, i, :], "y", reads=[rH[i]], writes=[rY[i]])
        final_done[hf] = True

    def phase_final():
        for hf in range(2):
            if not final_done[hf]:
                final_half(hf)

    def store_raw():
        for i in range(16):
            sc.dma("sp", y_d[i * 128:(i + 1) * 128, :], H3[:, i, :], "y", reads=[rH[i]], writes=[rY[i]])

    next_norm_of = {}
    for l in layers:
        next_norm_of[(l, 0)] = (VOFF[f"ng{l}"], 1)
        if l == 1 and (2 in layers or 3 in layers):
            next_norm_of[(l, 1)] = (VOFF["kvg"], 0)
        elif (l + 1) in layers and l != 1:
            next_norm_of[(l, 1)] = (VOFF[f"ng{l + 1}"], 0)
    plan()
    for l in layers:
        for hf in range(2):
            if l < 2:
                phase_A(l, hf)
            else:
                phase_B(l, hf)
        if l == 1 and (2 in layers or 3 in layers):
            for hf in range(2):
                phase_kv(hf)
    if final:
        phase_final()
    else:
        store_raw()
    sc.wait_all("sp", rY)
    sc.wait_all("pool", [rDBG])
    assert ring_state["pos"] == len(seq), (ring_state, len(seq))
    return nc, es


def make_in_maps(inputs):
    f = lambda a: np.asarray(a, dtype=np.float32)
    x = f(inputs["x"]); p = f(inputs["p"])
    wts, _ = pack_weights(f(inputs["a_w_in"]), f(inputs["a_w_out"]), f(inputs["w_kv"]), f(inputs["b_w_in"]),
                          f(inputs["b_w_out"]), f(inputs["ple_w"]), f(inputs["ple_gate_w"]))
    vecs = pack_vecs(f(inputs["norm_g"]), f(inputs["kv_norm_g"]), f(inputs["final_g"]), f(inputs["a_ln_g"]),
                     f(inputs["a_ln_b"]), f(inputs["a_b_s"]))
    wst = np.ascontiguousarray(f(inputs["a_w_s"]).transpose(0, 3, 1, 2)).reshape(2 * 128, 1024)
    cst = make_consts()
    maps = []
    for b in range(8):
        pT = np.ascontiguousarray(p[:, b].reshape(4, S, 2, 128).transpose(0, 3, 2, 1)).reshape(4 * 128, 2 * S)
        maps.append({"x": np.ascontiguousarray(x[b]), "pT": pT, "wts": wts, "vecs": vecs, "wst": wst, "cst": cst})
    return maps


def kernel(**inputs):
    nc, es = build()
    maps = make_in_maps(inputs)
    res = run_bass_kernel_spmd(nc, maps, core_ids=list(range(8)))
    return np.stack([np.asarray(r["y"], dtype=np.float32) for r in res.results], axis=0)
```

```python
import numpy as np
OPT = "sp"
from contextlib import ExitStack
import concourse.bass as bass
import concourse.mybir as mybir
from concourse.bass_utils import run_bass_kernel_spmd

F32 = mybir.dt.float32
BF16 = mybir.dt.bfloat16
AF = mybir.ActivationFunctionType
ALU = mybir.AluOpType

S = 2048
D = 1024
HALF = 1024
EPS = 1e-6
NSLOT = 4
FAST_RECIP = True
SEM_CHUNK = 30000


class Tok:
    __slots__ = ("eng", "sem", "val")

    def __init__(self, eng):
        self.eng = eng
        self.sem = None
        self.val = None


class Reg:
    __slots__ = ("w", "rs", "name")

    def __init__(self, name=""):
        self.w = None
        self.rs = {}
        self.name = name


class Sched:
    def __init__(self, nc, es):
        self.nc = nc
        self.es = es
        self.E = {"pe": nc.tensor, "act": nc.scalar, "dve": nc.vector, "pool": nc.gpsimd, "sp": nc.sync}
        self.sems = {}
        self.cnt = {}
        self.pending = {k: [] for k in self.E}
        self.seen = {k: {} for k in self.E}
        self.nsem = 0
        for k in self.E:
            self._newsem(k)
        self.dsem = {}

    def _newsem(self, k):
        self.nsem += 1
        self.sems[k] = self.nc.alloc_semaphore(name=f"s_{k}_{self.nsem}")
        self.cnt[k] = 0

    def _wait(self, eng, toks):
        best = {}
        for t in toks:
            if t is None:
                continue
            assert t.sem is not None, f"unresolved token from {t.eng} needed by {eng}"
            key = t.sem
            if key not in best or best[key].val < t.val:
                best[key] = t
        for key, t in best.items():
            if self.seen[eng].get(key, 0) >= t.val:
                continue
            self.E[eng].wait_ge(t.sem, t.val)
            self.seen[eng][key] = t.val

    def _deps(self, eng, reads, writes):
        deps = []
        for r in reads:
            if r.w is not None:
                if r.w.eng == eng and eng == "pe":
                    continue
                deps.append(r.w)
        for r in writes:
            if r.w is not None and not (r.w.eng == eng and eng == "pe"):
                deps.append(r.w)
            for e2, t in r.rs.items():
                if not (e2 == eng and eng == "pe"):
                    deps.append(t)
        return deps

    def _mark(self, tok, reads, writes):
        for r in reads:
            r.rs[tok.eng] = tok
        for r in writes:
            r.w = tok
            r.rs = {}

    def op(self, eng, fn, reads=(), writes=(), signal=True):
        self._wait(eng, self._deps(eng, reads, writes))
        inst = fn(self.E[eng])
        tok = Tok(eng)
        if signal:
            if self.cnt[eng] >= SEM_CHUNK:
                self._newsem(eng)
            self.cnt[eng] += 1
            inst.then_inc(self.sems[eng], 1)
            tok.sem = self.sems[eng]
            tok.val = self.cnt[eng]
            for p in self.pending[eng]:
                p.sem = tok.sem
                p.val = tok.val
            self.pending[eng] = []
        else:
            self.pending[eng].append(tok)
        self._mark(tok, reads, writes)
        return tok

    def dma(self, q, out, in_, semname, reads=(), writes=()):
        self._wait(q, self._deps("dma:" + semname, reads, writes))
        if semname not in self.dsem:
            self.dsem[semname] = [self.nc.alloc_semaphore(name="d_" + semname), 0]
        ent = self.dsem[semname]
        ent[1] += 16
        self.E[q].dma_start(out=out, in_=in_).then_inc(ent[0], 16)
        tok = Tok("dma:" + semname)
        tok.sem = ent[0]
        tok.val = ent[1]
        self._mark(tok, reads, writes)
        return tok

    def wait_all(self, eng, regs):
        toks = []
        for r in regs:
            if r.w is not None:
                toks.append(r.w)
        self._wait(eng, toks)


def _kc(w, k0, nk, c0, ncol):
    return w[k0 * 128:(k0 + nk) * 128, c0:c0 + ncol].reshape(nk, 128, ncol).transpose(1, 0, 2)


def pack_weights(a_w_in, a_w_out, w_kv, b_w_in, b_w_out, ple_w, ple_gate_w):
    units = []
    idx = {}

    def add(a):
        units.append(np.ascontiguousarray(a, dtype=np.float32).reshape(128, 2048))

    def ple(l):
        idx["PW", l] = len(units)
        for kh in range(2):
            add(_kc(ple_gate_w[l], kh * 4, 4, 0, 512))
        add(np.stack([_kc(ple_w[l], 0, 2, dc * 512, 512) for dc in range(2)], axis=1))
        for kh in range(2):
            add(_kc(ple_gate_w[l], kh * 4, 4, 512, 512))

    for l in range(2):
        w = a_w_in[l]
        idx["WV", l] = len(units)
        for c in range(4):
            for kh in range(2):
                add(_kc(w, kh * 4, 4, 2048 + c * 512, 512))
        idx["WUG", l] = len(units)
        for j in range(16):
            add(np.concatenate([_kc(w, 0, 8, j * 128, 128), _kc(w, 0, 8, 4096 + j * 128, 128)], axis=2))
        idx["WO", l] = len(units)
        for dc in range(2):
            for kq in range(4):
                add(_kc(a_w_out[l], kq * 4, 4, dc * 512, 512))
        ple(l)
    idx["WK"] = len(units)
    for hp in range(4):
        add(np.stack([_kc(w_kv, 0, 8, (hp * 2 + e) * 128, 128) for e in range(2)], axis=1))
    idx["WVV"] = len(units)
    for vc in range(2):
        for kh in range(2):
            add(_kc(w_kv, kh * 4, 4, 1024 + vc * 512, 512))
    for j in range(2):
        l = 2 + j
        idx["QG", l] = len(units)
        for hd in range(8):
            add(np.concatenate([_kc(b_w_in[j], 0, 8, hd * 128, 128), _kc(b_w_in[j], 0, 8, 1024 + hd * 128, 128)], axis=2))
        idx["BO", l] = len(units)
        for dc in range(2):
            for kh in range(2):
                add(_kc(b_w_out[j], kh * 4, 4, dc * 512, 512))
        ple(l)
    return np.stack(units, axis=0).reshape(len(units) * 128, 2048), idx


VOFF = {}
_o = 0
for _n, _sz in [("ng0", 1024), ("ng1", 1024), ("ng2", 1024), ("ng3", 1024), ("kvg", 1024), ("fing", 1024),
                ("lng0", 2048), ("lng1", 2048), ("lnb0", 2048), ("lnb1", 2048), ("bs0", 1024), ("bs1", 1024)]:
    VOFF[_n] = _o
    _o += _sz
NV = _o


def pack_vecs(norm_g, kv_norm_g, final_g, a_ln_g, a_ln_b, a_b_s):
    v = np.concatenate([norm_g[0], norm_g[1], norm_g[2], norm_g[3], kv_norm_g, final_g,
                        a_ln_g[0], a_ln_g[1], a_ln_b[0], a_ln_b[1],
                        a_b_s[0].reshape(-1), a_b_s[1].reshape(-1)]).astype(np.float32)
    return np.ascontiguousarray(np.broadcast_to(v[None, :], (128, NV)))


def make_consts():
    c = np.zeros((128, 512), np.float32)
    i = np.arange(128)
    c[:, 0:128] = np.eye(128)
    c[:, 128:256] = -1.0 * (i[:, None] >= i[None, :])
    c[:, 256:384] = -1.0
    c[:, 384:512] = -30000.0 * (i[:, None] >= i[None, :])
    return c


NUNITS = 2 * (8 + 16 + 8 + 5) + 8 + 2 * (8 + 4 + 5)


def build(layers=(0, 1, 2, 3), final=True, dbg=False):
    nc = bass.Bass("TRN2", target_bir_lowering=False)
    es = ExitStack()
    sc = Sched(nc, es)
    _, widx = pack_weights(*[np.zeros(s, np.float32) for s in
                             [(2, 1024, 6144), (2, 2048, 1024), (1024, 2048), (2, 1024, 2048),
                              (2, 1024, 1024), (4, 256, 1024), (4, 1024, 1024)]])

    x_d = nc.dram_tensor("x", [S, D], F32, kind="ExternalInput").ap()
    pT_d = nc.dram_tensor("pT", [4 * 128, 2 * S], F32, kind="ExternalInput").ap()
    wts_d = nc.dram_tensor("wts", [NUNITS * 128, 2048], F32, kind="ExternalInput").ap()
    vecs_d = nc.dram_tensor("vecs", [128, NV], F32, kind="ExternalInput").ap()
    wst_d = nc.dram_tensor("wst", [2 * 128, 1024], F32, kind="ExternalInput").ap()
    cst_d = nc.dram_tensor("cst", [128, 512], F32, kind="ExternalInput").ap()
    y_d = nc.dram_tensor("y", [S, D], F32, kind="ExternalOutput").ap()
    dbg_d = {}
    rDBG = Reg("dbg")

    def dump(name, ap, regs):
        if not dbg:
            return
        shp = list(ap.shape)
        dbg_d[name] = nc.dram_tensor("dbg_" + name, shp, F32, kind="ExternalOutput").ap()
        sc.dma("pool", dbg_d[name], ap, "dbg", reads=regs, writes=[rDBG])

    def sb(name, shape, dt):
        return es.enter_context(nc.sbuf_tensor(name, shape, dt))

    H = sb("H", [128, 16 * 1024], F32)
    HT = sb("HT", [128, 8 * HALF], BF16)
    OGT = sb("OGT", [128, 8 * HALF], BF16)
    X = sb("X", [128, 32768], BF16)
    RING = sb("RING", [128, NSLOT * 2048], BF16)
    TAA = sb("TAA", [128, 1024], F32)
    GB = TAA
    TA = [TAA[:, 0:512], TAA[:, 512:1024]]
    HN = [sb(f"HN{i}", [128, 1024], BF16) for i in range(2)]
    CST = sb("CST", [128, 512], BF16)
    PT = sb("PT", [128, 2 * HALF], BF16)
    SS = sb("SS", [128, 16], F32)
    NHALF = sb("NHALF", [128, 16], F32)
    RS = sb("RS", [128, 16], F32)
    TBB = sb("TBB", [128, 2048], BF16)
    TB = [TBB[:, i * 512:(i + 1) * 512] for i in range(4)]
    PS = [es.enter_context(nc.psum_tensor(f"ps{i}", [128, 512], F32)) for i in range(8)]
    NQ = 512
    LSUM = sb("LSUM", [128, NQ], F32)
    LSUMB = [sb(f"LSUMB{i}", [128, NQ], BF16) for i in range(2)]
    EE = sb("EE", [128, NQ], F32)
    LP = [sb(f"LP{i}", [128, NQ], BF16) for i in range(3)]
    AT = [sb(f"AT{i}", [128, NQ], BF16) for i in range(2)]
    QT = [PT[:, 0:HALF], PT[:, HALF:2 * HALF]]
    SG = [TBB[:, 0:HALF], TBB[:, HALF:2 * HALF]]
    OGF = OGT[:, :].bitcast(F32)
    LNG = OGF[:, 0:2048]
    LNB = OGF[:, 2048:4096]
    XB = X[:, 16384:32768]
    VHAT2 = [XB[:, 0:2048], XB[:, 8192:10240]]
    WST = XB[:, 2048:3072]
    BSH = XB[:, 3072:4096]
    BSL = XB[:, 4096:5120]
    BSHL = XB[0:64, 5120:6144]
    ONES = XB[0:64, 6144:6272]
    BSR = XB[0:64, 3072:5120]
    BSF = XB[:, 8192:10240].bitcast(F32)
    SML = sb("SML", [128, 64], F32)
    ST8 = sb("ST8", [128, 192], F32)
    ST = SML[:, 0:24]
    MV8 = SML[:, 24:40]
    RSV8 = SML[:, 40:48]
    NMR8 = SML[:, 48:56]

    H3 = H[:, :].rearrange("p (i d) -> p i d", d=1024)
    HT3 = HT[:, :].rearrange("p (k t) -> p k t", t=HALF)
    OGT3 = OGT[:, :].rearrange("p (k t) -> p k t", t=HALF)
    PT3 = PT[:, :].rearrange("p (k t) -> p k t", t=HALF)
    IDENT = CST[:, 0:128]
    NEGTRI = CST[:, 128:256]
    NEGONES = CST[:, 256:384]
    NEGMASK = CST[:, 384:512]

    rH = [Reg(f"H{i}") for i in range(16)]
    rHT = [Reg(f"HT{i}") for i in range(8)]
    rOGT = [Reg(f"OGT{h}") for h in range(8)]
    rHN = [Reg(), Reg()]
    rCST = Reg("CST")
    rSS = Reg("SS")
    rRS = Reg("RS")
    rTA = [Reg(), Reg()]
    rTB = [Reg() for _ in range(4)]
    rGBL = rTA
    rQT = [Reg(), Reg()]
    rPTL = rQT
    rSG = [[rTB[0], rTB[1]], [rTB[2], rTB[3]]]
    rLS = Reg(); rLSB = [Reg(), Reg()]; rEE = Reg()
    rLP = [Reg(), Reg(), Reg()]; rAT = [Reg(), Reg()]
    rLNG = Reg(); rLNB = Reg(); rWST = Reg(); rBS = Reg(); rVH2 = [Reg(), rBS]; rBSHL = Reg(); rBSR = Reg()
    rST = Reg(); rMV = Reg(); rSV = Reg()
    rST8 = [Reg() for _ in range(8)]
    rX = [Reg(f"X{i}") for i in range(8)]
    rPS = [Reg(f"ps{i}") for i in range(8)]
    rY = [Reg(f"Y{i}") for i in range(16)]

    cnt = {"ta": 0, "tb": 0, "hn": 0, "bank": 0, "lp": 0, "at": 0, "ptmp": 0, "ta4": 0}

    def nxt(key, n):
        v = cnt[key] % n
        cnt[key] += 1
        return v

    bank_set = [list(range(8))]

    def bank():
        bs = bank_set[0]
        return bs[nxt("bank", len(bs))]

    seq = []
    seqA = []
    XBr = X[:, 16384:32768]
    slot_ap = [RING[:, i * 2048:(i + 1) * 2048] for i in range(NSLOT)] + \
              [XBr[:, 10240 + i * 2048:10240 + (i + 1) * 2048] for i in range(3)]
    NS_A = NSLOT + 3
    rSLOT = [Reg(f"slot{i}") for i in range(NS_A)]
    slot_occ = [-1] * NS_A
    unit_slot = {}
    ring_state = {"issued": 0, "pos": 0, "done": 0, "last": -1}

    def ring_try_issue(q):
        pool = NS_A if seqA[q] else NSLOT
        for d_ in range(1, pool + 1):
            s = (ring_state["last"] + d_) % pool
            if slot_occ[s] < ring_state["done"]:
                break
        else:
            return False
        u = seq[q]
        sc.dma("pool", slot_ap[s], wts_d[u * 128:(u + 1) * 128, :], f"ring{s}", writes=[rSLOT[s]])
        slot_occ[s] = q
        unit_slot[q] = s
        ring_state["last"] = s
        ring_state["issued"] += 1
        return True

    def ring_begin(keep=0):
        ring_state["done"] = ring_state["pos"] - keep

    def ring_next(u):
        q = ring_state["pos"]
        assert seq[q] == u, (q, seq[q], u)
        while ring_state["issued"] <= q:
            ok = ring_try_issue(ring_state["issued"])
            assert ok, "ring: no free slot for a unit that is needed now"
        ring_state["pos"] += 1
        return unit_slot[q]

    def ring_prefetch(*_):
        while ring_state["issued"] < len(seq) and ring_state["issued"] < ring_state["pos"] + NS_A:
            if not ring_try_issue(ring_state["issued"]):
                break

    def plan():
        def ext(r, isa):
            seq.extend(r)
            seqA.extend([isa] * len(r))
        for l in layers:
            for hf in range(2):
                if l < 2:
                    ext(range(widx["WV", l], widx["WV", l] + 8), l < 2)
                    ext(range(widx["WUG", l], widx["WUG", l] + 16), l < 2)
                    ext(range(widx["WO", l], widx["WO", l] + 8), l < 2)
                else:
                    if l == 2 or (l == 3 and 2 not in layers):
                        pass
                    ext(range(widx["QG", l], widx["QG", l] + 8), l < 2)
                    ext(range(widx["BO", l], widx["BO", l] + 4), l < 2)
                ext(range(widx["PW", l], widx["PW", l] + 5), l < 2)
            if l == 1 and (2 in layers or 3 in layers):
                for hf in range(2):
                    ext(range(widx["WK"], widx["WK"] + 4), False)
                    ext(range(widx["WVV"], widx["WVV"] + 4), False)

    def slot3(slot, k):
        return slot_ap[slot].rearrange("p (k c) -> p k c", k=k)

    rNH = Reg("nhalf")
    sc.op("pool", lambda e: e.memset(NHALF[:, :], -0.5), writes=[rNH])

    def rsqrt_inplace(ap, ncol, regs_rw):
        sc.op("pool", lambda e: e.tensor_tensor(out=ap, in0=ap, in1=NHALF[:, 0:ncol], op=ALU.pow),
              reads=regs_rw + [rNH], writes=regs_rw)

    sc.dma("pool", CST[:, :], cst_d[:, :], "cst", writes=[rCST])
    for i in range(16):
        sc.dma("sp", H3[:, i, :], x_d[i * 128:(i + 1) * 128, :], f"x{i}", writes=[rH[i]])

    JUNK = EE[:, :].bitcast(BF16)
    stats_ready = [False, False]

    def norm_stats(hf):
        if stats_ready[hf]:
            return
        for ii in range(8):
            i = hf * 8 + ii
            sc.op("act", lambda e: e.activation(out=JUNK, in_=H3[:, i, :], func=AF.Square, accum_out=SS[:, i:i + 1]),
                  reads=[rH[i]], writes=[rEE, rSS])
        sl = slice(hf * 8, hf * 8 + 8)
        sc.op("dve", lambda e: e.tensor_scalar(out=RS[:, sl], in0=SS[:, sl], scalar1=1.0 / 1024, scalar2=EPS,
                                                op0=ALU.mult, op1=ALU.add), reads=[rSS], writes=[rRS])
        rsqrt_inplace(RS[:, sl], 8, [rRS])
        stats_ready[hf] = True

    def phase_norm(goff, hf):
        if gb_loaded["goff"] == goff:
            gb_loaded["goff"] = None
        else:
            sc.dma("sp", GB[:, :], vecs_d[:, goff:goff + 1024], "gb", writes=rGBL)
        norm_stats(hf)
        for ii in range(8):
            i = hf * 8 + ii
            hb = ii % 2
            sc.op("dve", lambda e: e.scalar_tensor_tensor(out=HN[hb][:, :], in0=H3[:, i, :], scalar=RS[:, i:i + 1],
                                                           in1=GB[:, :], op0=ALU.mult, op1=ALU.mult),
                  reads=[rH[i], rRS] + rGBL, writes=[rHN[hb]])
            transpose_tile(HN[hb], rHN[hb], 8, None, [rHT[ii]], HT3[:, :, ii * 128:(ii + 1) * 128])

    def ple_prep_tile(hf, ii):
        ple_prep_lo(hf, ii)
        ple_prep_hi(hf, ii)

    def ple_prep_lo(hf, ii):
        i = hf * 8 + ii
        hb = ii % 2
        sc.op("dve", lambda e: e.tensor_copy(out=HN[hb][:, 0:512], in_=H3[:, i, 0:512]), reads=[rH[i]], writes=[rHN[hb]])

    def ple_prep_hi(hf, ii):
        i = hf * 8 + ii
        hb = ii % 2
        sc.op("dve", lambda e: e.tensor_copy(out=HN[hb][:, 512:1024], in_=H3[:, i, 512:1024]), reads=[rH[i]], writes=[rHN[hb]])
        transpose_tile(HN[hb], rHN[hb], 8, None, [rHT[ii]], HT3[:, :, ii * 128:(ii + 1) * 128])

    def ple_load_pt(l, hf):
        sc.dma("pool", PT3[:, :, :], pT_d[l * 128:(l + 1) * 128, :].rearrange("p (k t) -> p k t", t=S)[:, :, hf * HALF:(hf + 1) * HALF],
               "pt", writes=rPTL)

    def transpose_tile(src, rsrc, nk, dst_fn, wregs, dst_all):
        b = bank()
        pv = PS[b][:, :].bitcast(BF16).rearrange("p (k t) -> p k t", t=128)
        for k in range(nk):
            sc.op("pe", lambda e: e.transpose(out=pv[:, k, :], in_=src[:, k * 128:(k + 1) * 128], identity=IDENT),
                  reads=[rsrc, rCST], writes=[rPS[b]], signal=(k == nk - 1))
        sc.op("act", lambda e: e.copy(out=dst_all, in_=pv[:, 0:nk, :]), reads=[rPS[b]], writes=wregs)

    def phase_A(l, hf):
        if True:
            X3 = X[:, 0:16384].rearrange("p (i f) -> p i f", f=2048)
            WST3 = WST.rearrange("p (g t) -> p g t", t=128)
            BSHL3 = BSHL.rearrange("p (g t) -> p g t", t=128)

            if hf == 0:
                sc.dma("sp", LNG, vecs_d[:, VOFF[f"lng{l}"]:VOFF[f"lng{l}"] + 2048], "lng", writes=[rLNG])
                sc.dma("sp", LNB, vecs_d[:, VOFF[f"lnb{l}"]:VOFF[f"lnb{l}"] + 2048], "lnb", writes=[rLNB])
                sc.dma("pool", WST, wst_d[l * 128:(l + 1) * 128, :], "wst", writes=[rWST])
                sc.op("dve", lambda e: e.memset(WST3[64:128, :, 0:64], 0.0), writes=[rWST])
                sc.dma("sp", BSF, vecs_d[:, VOFF[f"bs{l}"]:VOFF[f"bs{l}"] + 1024], "bsf", writes=[rBS])
                sc.op("dve", lambda e: e.tensor_copy(out=BSH, in_=BSF), reads=[rBS], writes=[rBS])
                sc.op("dve", lambda e: e.tensor_tensor(out=BSF, in0=BSF, in1=BSH, op=ALU.subtract),
                      reads=[rBS], writes=[rBS])
                sc.op("dve", lambda e: e.tensor_copy(out=BSL, in_=BSF), reads=[rBS], writes=[rBS])
                sc.op("dve", lambda e: e.memset(BSHL, 0.0), writes=[rBSHL])
                sc.op("dve", lambda e: e.memset(ONES, 0.0), writes=[rBSHL])
                sc.op("dve", lambda e: e.tensor_copy(out=BSHL[0:1, :], in_=BSH[0:1, :]), reads=[rBS], writes=[rBSHL])
                sc.op("dve", lambda e: e.tensor_copy(out=BSHL[32:33, :], in_=BSL[32:33, :]), reads=[rBS], writes=[rBSHL])
                sc.op("dve", lambda e: e.memset(ONES[0:1, :], 1.0), writes=[rBSHL])
                sc.op("dve", lambda e: e.memset(ONES[32:33, :], 1.0), writes=[rBSHL])
                BSR4 = BSR.rearrange("p (g d t) -> p g d t", d=2, t=128)
                for d_ in range(2):
                    sc.op("dve", lambda e: e.tensor_copy(out=BSR4[:, :, d_, :], in_=BSHL3), reads=[rBSHL], writes=[rBS, rBSR])

            phase_norm(VOFF[f"ng{l}"], hf)
            if "s" in OPT:
                norm_stats(1 - hf)
            if l == 0 and hf == 0:
                dump("HT", HT[:, :], rHT)

            MV83 = MV8.rearrange("p (i t) -> p i t", t=2)

            def stats_batch(t0, t1):
                sl_ = slice(t0, t1)
                sc.op("dve", lambda e: e.tensor_scalar(out=RSV8[:, sl_], in0=MV83[:, sl_, 1], scalar1=EPS, scalar2=None, op0=ALU.add),
                      reads=[rMV], writes=[rSV])
                rsqrt_inplace(RSV8[:, sl_], t1 - t0, [rSV])
                sc.op("dve", lambda e: e.scalar_tensor_tensor(out=NMR8[:, sl_], in0=MV83[:, sl_, 0], scalar=-1.0, in1=RSV8[:, sl_],
                                                               op0=ALU.mult, op1=ALU.mult), reads=[rMV, rSV], writes=[rSV])

            TA4 = [TA[0], TA[1], EE[:, :], LSUM[:, :]]
            rTA4 = [rTA[0], rTA[1], rEE, rLS]

            def ps_produce(ii):
                VHAT = VHAT2[ii % 2]
                rVH = rVH2[ii % 2]
                for q in range(4):
                    ta = nxt("ta4", 4)
                    T_, rT_ = TA4[ta], rTA4[ta]
                    qs = slice(q * 512, (q + 1) * 512)
                    sc.op("act", lambda e: e.activation(out=T_, in_=X3[:, ii, qs], func=AF.Identity,
                                                         bias=NMR8[:, ii:ii + 1], scale=RSV8[:, ii:ii + 1]),
                          reads=[rX[ii], rSV], writes=[rT_])
                    sc.op("dve", lambda e: e.tensor_tensor(out=T_, in0=T_, in1=LNG[:, qs], op=ALU.mult),
                          reads=[rT_, rLNG], writes=[rT_])
                    sc.op("dve", lambda e: e.tensor_tensor(out=VHAT[:, qs], in0=T_, in1=LNB[:, qs], op=ALU.add),
                          reads=[rT_, rLNB], writes=[rVH])

            def ps_consume(ii):
                VHAT = VHAT2[ii % 2]
                rVH = rVH2[ii % 2]
                Xs = X3[:, ii, :].rearrange("p (j t) -> p j t", t=128)
                for jb in range(4):
                    b = bank()
                    sc.op("pe", lambda e: e.matmul(PS[b][:, :], lhsT=ONES[0:33, :], rhs=BSR[0:33, jb * 512:(jb + 1) * 512],
                                                    start=True, stop=False, skip_group_check=True),
                          reads=[rBSHL, rBSR], writes=[rPS[b]], signal=False)
                    for jj in range(4):
                        j = jb * 4 + jj
                        g = j // 2
                        sc.op("pe", lambda e: e.matmul(PS[b][:, jj * 128:(jj + 1) * 128], lhsT=VHAT[:, j * 128:(j + 1) * 128],
                                                        rhs=WST3[:, g, :], start=False, stop=(jj == 3), skip_group_check=True),
                              reads=[rVH, rWST], writes=[rPS[b]], signal=(jj == 3))
                    sc.op("act", lambda e: e.copy(out=Xs[:, jb * 4:(jb + 1) * 4, :],
                                                   in_=PS[b][:, :].rearrange("p (j t) -> p j t", t=128)),
                          reads=[rPS[b]], writes=[rX[ii]])

            for c in range(4):
                ring_begin()
                sA = ring_next(widx["WV", l] + c * 2)
                sB = ring_next(widx["WV", l] + c * 2 + 1)
                ring_prefetch(2)
                for ii in range(8):
                    b = bank()
                    for k in range(8):
                        s_ = sA if k < 4 else sB
                        sc.op("pe", lambda e: e.matmul(PS[b][:, :], lhsT=HT3[:, k, ii * 128:(ii + 1) * 128],
                                                        rhs=slot3(s_, 4)[:, k % 4, :], start=(k == 0), stop=(k == 7)),
                              reads=[rHT[ii], rSLOT[s_]], writes=[rPS[b]], signal=(k == 7))
                    sc.op("act", lambda e: e.activation(out=X3[:, ii, c * 512:(c + 1) * 512], in_=PS[b][:, :], func=AF.Gelu),
                          reads=[rPS[b]], writes=[rX[ii]])
                    sc.op("dve", lambda e: e.bn_stats(out=ST8[:, ii * 24 + c * 6:ii * 24 + (c + 1) * 6],
                                                       in_=X3[:, ii, c * 512:(c + 1) * 512]),
                          reads=[rX[ii]], writes=[rST8[ii]])
                    if c == 3:
                        sc.op("dve", lambda e: e.bn_aggr(out=MV8[:, ii * 2:(ii + 1) * 2], in_=ST8[:, ii * 24:(ii + 1) * 24]),
                              reads=[rST8[ii]], writes=[rMV])
                        if ii == 3:
                            stats_batch(0, 4)
                            ps_produce(0)
                            ps_produce(1)
                        if ii == 7:
                            stats_batch(4, 8)
            if l == 0 and hf == 0:
                dump("GV", X[:, 0:16384], rX)
            for ii in range(8):
                ps_consume(ii)
                if ii + 2 < 8:
                    ps_produce(ii + 2)

            if l == 0 and hf == 0:
                dump("SVT", X[:, 0:16384], rX)
            for j in range(16):
                ring_begin()
                s_ = ring_next(widx["WUG", l] + j)
                ring_prefetch(1)
                w3 = slot3(s_, 8)
                for st in range(2):
                    bu = bank()
                    for k in range(8):
                        sc.op("pe", lambda e: e.matmul(PS[bu][:, :], lhsT=w3[:, k, 0:128], rhs=HT3[:, k, st * 512:(st + 1) * 512],
                                                        start=(k == 0), stop=(k == 7)),
                              reads=[rSLOT[s_]] + rHT[st * 4:(st + 1) * 4], writes=[rPS[bu]], signal=(k == 7))
                    bg = bank()
                    for k in range(8):
                        sc.op("pe", lambda e: e.matmul(PS[bg][:, :], lhsT=w3[:, k, 128:256], rhs=HT3[:, k, st * 512:(st + 1) * 512],
                                                        start=(k == 0), stop=(k == 7)),
                              reads=[rSLOT[s_]] + rHT[st * 4:(st + 1) * 4], writes=[rPS[bg]], signal=(k == 7))
                    tb = nxt("tb", 4)
                    sc.op("act", lambda e: e.activation(out=TB[tb][:, :], in_=PS[bu][:, :], func=AF.Gelu),
                          reads=[rPS[bu]], writes=[rTB[tb]])
                    ta2 = nxt("ta", 2)
                    tb2 = nxt("tb", 4)
                    sc.op("act", lambda e: e.activation(out=TA[ta2][:, :], in_=PS[bg][:, :], func=AF.Tanh, scale=0.5),
                          reads=[rPS[bg]], writes=[rTA[ta2]])
                    sc.op("dve", lambda e: e.scalar_tensor_tensor(out=TB[tb2][:, :], in0=TA[ta2][:, :], scalar=1.0, in1=PS[bg][:, :],
                                                                   op0=ALU.add, op1=ALU.mult),
                          reads=[rTA[ta2], rPS[bg]], writes=[rTB[tb2]])
                    sc.op("dve", lambda e: e.tensor_tensor(out=TB[tb][:, :], in0=TB[tb][:, :], in1=TB[tb2][:, :], op=ALU.mult),
                          reads=[rTB[tb], rTB[tb2]], writes=[rTB[tb]])
                    xv = X3[:, st * 4:(st + 1) * 4, j * 128:(j + 1) * 128]
                    sc.op("dve", lambda e: e.scalar_tensor_tensor(out=xv, in0=TB[tb][:, :].rearrange("p (i t) -> p i t", t=128),
                                                                   scalar=0.5, in1=xv, op0=ALU.mult, op1=ALU.mult),
                          reads=[rTB[tb]] + rX[st * 4:(st + 1) * 4], writes=rX[st * 4:(st + 1) * 4])

            if l == 0 and hf == 0:
                dump("GAT", X[:, 0:16384], rX)
            ple_load_pt(l, hf)
            for dc in range(2):
                ring_begin()
                ss_ = [ring_next(widx["WO", l] + dc * 4 + kq) for kq in range(4)]
                ring_prefetch(4)
                for ii in range(8):
                    i = hf * 8 + ii
                    if dc == 1 and "p" in OPT:
                        ple_prep_lo(hf, ii)
                    b = bank()
                    for k in range(16):
                        s_ = ss_[k // 4]
                        sc.op("pe", lambda e: e.matmul(PS[b][:, :], lhsT=X3[:, ii, k * 128:(k + 1) * 128],
                                                        rhs=slot3(s_, 4)[:, k % 4, :], start=(k == 0), stop=(k == 15)),
                              reads=[rX[ii], rSLOT[s_]], writes=[rPS[b]], signal=(k == 15))
                    hv = H3[:, i, dc * 512:(dc + 1) * 512]
                    sc.op("dve", lambda e: e.tensor_tensor(out=hv, in0=hv, in1=PS[b][:, :], op=ALU.add),
                          reads=[rPS[b], rH[i]], writes=[rH[i]])
                    if dc == 1 and "p" in OPT:
                        if ii >= 1:
                            ple_prep_hi(hf, ii - 1)
                        if ii == 7:
                            ple_prep_hi(hf, 7)
        if l == 0 and hf == 0:
            dump("H1", H[:, 0:8192], rH[0:8])
        phase_ple(l, hf)

    gb_loaded = {"goff": None}
    final_done = [False, False]
    rFS = [Reg(f"fs{i}") for i in range(16)]

    def final_tile(i):
        hb = i % 2
        sc.op("act", lambda e: e.activation(out=HN[hb][:, :], in_=H3[:, i, :], func=AF.Square, accum_out=SS[:, i:i + 1]),
              reads=[rH[i]], writes=[rHN[hb], rFS[i]])
        sc.op("dve", lambda e: e.tensor_scalar(out=RS[:, i:i + 1], in0=SS[:, i:i + 1], scalar1=1.0 / 1024, scalar2=EPS,
                                                op0=ALU.mult, op1=ALU.add), reads=[rFS[i]], writes=[rFS[i]])
        rsqrt_inplace(RS[:, i:i + 1], 1, [rFS[i]])
        sc.op("dve", lambda e: e.scalar_tensor_tensor(out=H3[:, i, :], in0=H3[:, i, :], scalar=RS[:, i:i + 1], in1=GB[:, :],
                                                       op0=ALU.mult, op1=ALU.mult), reads=[rH[i], rFS[i]] + rGBL, writes=[rH[i]])
        sc.dma("sp", y_d[i * 128:(i + 1) * 128, :], H3[:, i, :], "y", reads=[rH[i]], writes=[rY[i]])

    def phase_ple(l, hf):
        nn = next_norm_of.get((l, hf))
        fin_here = final and l == 3 and hf == 1
        if nn is not None:
            sc.dma("sp", GB[:, :], vecs_d[:, nn[0]:nn[0] + 1024], "gb", writes=rGBL)
            gb_loaded["goff"] = nn[0]
        elif fin_here:
            sc.dma("sp", GB[:, :], vecs_d[:, VOFF["fing"]:VOFF["fing"] + 1024], "gb", writes=rGBL)
        if "p" not in OPT:
            for ii in range(8):
                ple_prep_tile(hf, ii)
        base = widx["PW", l]
        sw = None
        wp = None
        for dc in range(2):
            if dc == 0:
                ring_begin()
                sg = [ring_next(base + 0), ring_next(base + 1)]
                sw = ring_next(base + 2)
                wp = slot_ap[sw].rearrange("p (d k c) -> p d k c", d=2, k=2)
            else:
                ring_begin(keep=1)
                sg = [ring_next(base + 3), ring_next(base + 4)]
            ring_prefetch(3)
            for ii in range(8):
                i = hf * 8 + ii
                bg = bank()
                for k in range(8):
                    s_ = sg[k // 4]
                    sc.op("pe", lambda e: e.matmul(PS[bg][:, :], lhsT=HT3[:, k, ii * 128:(ii + 1) * 128],
                                                    rhs=slot3(s_, 4)[:, k % 4, :], start=(k == 0), stop=(k == 7)),
                          reads=[rHT[ii], rSLOT[s_]], writes=[rPS[bg]], signal=(k == 7))
                bp = bank()
                for k in range(2):
                    sc.op("pe", lambda e: e.matmul(PS[bp][:, :], lhsT=PT3[:, k, ii * 128:(ii + 1) * 128],
                                                    rhs=wp[:, dc, k, :], start=(k == 0), stop=(k == 1)),
                          reads=rPTL + [rSLOT[sw]], writes=[rPS[bp]], signal=(k == 1))
                tp_ = nxt("ptmp", 2)
                T_ = (EE, LSUM)[tp_]
                rT_ = (rEE, rLS)[tp_]
                sc.op("act", lambda e: e.activation(out=T_[:, :], in_=PS[bg][:, :], func=AF.Tanh, scale=0.5),
                      reads=[rPS[bg]], writes=[rT_])
                sc.op("dve", lambda e: e.scalar_tensor_tensor(out=T_[:, :], in0=T_[:, :], scalar=1.0, in1=PS[bp][:, :],
                                                               op0=ALU.add, op1=ALU.mult),
                      reads=[rT_, rPS[bp]], writes=[rT_])
                hv = H3[:, i, dc * 512:(dc + 1) * 512]
                sc.op("dve", lambda e: e.scalar_tensor_tensor(out=hv, in0=T_[:, :], scalar=0.5, in1=hv, op0=ALU.mult, op1=ALU.add),
                      reads=[rT_, rH[i]], writes=[rH[i]])
                if fin_here and dc == 1:
                    final_tile(i)
        stats_ready[hf] = False
        if fin_here:
            final_done[hf] = True

    KT3 = X[:, 0:16384].rearrange("p (h t) -> p h t", t=S)
    V3 = X[:, 16384:32768].rearrange("p (i f) -> p i f", f=1024)
    rKT = [Reg(f"KT{h}") for h in range(8)]
    rV = [Reg(f"V{i}") for i in range(16)]

    def phase_kv(hf):
        phase_norm(VOFF["kvg"], hf)
        if hf == 0:
            norm_stats(1)
            gb_loaded["goff"] = VOFF["kvg"]
        elif 2 in layers:
            sc.dma("sp", GB[:, :], vecs_d[:, VOFF["ng2"]:VOFF["ng2"] + 1024], "gb", writes=rGBL)
            gb_loaded["goff"] = VOFF["ng2"]
        for hp in range(4):
            ring_begin()
            s_ = ring_next(widx["WK"] + hp)
            ring_prefetch(1)
            w4 = slot_ap[s_].rearrange("p (e k c) -> p e k c", e=2, k=8)
            for e_ in range(2):
                hd = hp * 2 + e_
                for st in range(2):
                    b = bank()
                    for k in range(8):
                        sc.op("pe", lambda e: e.matmul(PS[b][:, :], lhsT=w4[:, e_, k, :], rhs=HT3[:, k, st * 512:(st + 1) * 512],
                                                        start=(k == 0), stop=(k == 7)),
                              reads=[rSLOT[s_]] + rHT[st * 4:(st + 1) * 4], writes=[rPS[b]], signal=(k == 7))
                    c0 = hf * HALF + st * 512
                    sc.op("act", lambda e: e.copy(out=KT3[:, hd, c0:c0 + 512], in_=PS[b][:, :]), reads=[rPS[b]], writes=[rKT[hd]])
        for vc in range(2):
            ring_begin()
            sv_ = [ring_next(widx["WVV"] + vc * 2 + kh) for kh in range(2)]
            ring_prefetch(2)
            for ii in range(8):
                i = hf * 8 + ii
                b = bank()
                for k in range(8):
                    s_ = sv_[k // 4]
                    sc.op("pe", lambda e: e.matmul(PS[b][:, :], lhsT=HT3[:, k, ii * 128:(ii + 1) * 128],
                                                    rhs=slot3(s_, 4)[:, k % 4, :], start=(k == 0), stop=(k == 7)),
                          reads=[rHT[ii], rSLOT[s_]], writes=[rPS[b]], signal=(k == 7))
                sc.op("dve", lambda e: e.tensor_copy(out=V3[:, i, vc * 512:(vc + 1) * 512], in_=PS[b][:, :]),
                      reads=[rPS[b]], writes=[rV[i]] + rSLOT[NSLOT:])

    def phase_B(l, hf):
        if True:
            SCALE = 1.0 / np.sqrt(128.0)

            phase_norm(VOFF[f"ng{l}"], hf)
            if "s" in OPT:
                norm_stats(1 - hf)
            if final and l == 3 and hf == 1:
                final_half(0)
            bank_set[0] = [0, 1, 2, 3, 6]

            def proj_items(hd, per):
                st8 = {}
                qb = hd % 2

                def start():
                    ring_begin()
                    st8["s"] = ring_next(widx["QG", l] + hd)
                    ring_prefetch()

                def group(st, isg):
                    g8 = {}

                    def mm(k0, k1):
                        def f():
                            if "s" not in st8:
                                start()
                            s_ = st8["s"]
                            w3 = slot3(s_, 8)
                            if "b" not in g8:
                                g8["b"] = bank()
                            b = g8["b"]
                            co = 128 if isg else 0
                            for k in range(k0, k1):
                                sc.op("pe", lambda e: e.matmul(PS[b][:, :], lhsT=w3[:, k, co:co + 128],
                                                                rhs=HT3[:, k, st * 512:(st + 1) * 512], start=(k == 0), stop=(k == 7)),
                                      reads=[rSLOT[s_]] + rHT[st * 4:(st + 1) * 4], writes=[rPS[b]], signal=(k == 7))
                            if k1 == 8:
                                if not isg:
                                    sc.op("dve", lambda e: e.tensor_scalar(out=QT[qb][:, st * 512:(st + 1) * 512], in0=PS[b][:, :],
                                                                            scalar1=float(SCALE), scalar2=None, op0=ALU.mult),
                                          reads=[rPS[b]], writes=[rQT[qb]])
                                else:
                                    ta = nxt("ta", 2)
                                    sc.op("act", lambda e: e.activation(out=TA[ta][:, :], in_=PS[b][:, :], func=AF.Exp, scale=-1.0),
                                          reads=[rPS[b]], writes=[rTA[ta]])
                                    sc.op("dve", lambda e: e.tensor_scalar(out=TA[ta][:, :], in0=TA[ta][:, :], scalar1=1.0, scalar2=None,
                                                                            op0=ALU.add), reads=[rTA[ta]], writes=[rTA[ta]])
                                    g8["ta"] = ta
                                    sc.op("dve", lambda e: e.tensor_copy(out=SG[qb][:, st * 512:(st + 1) * 512], in_=PS[b][:, :]),
                                          reads=[rPS[b]], writes=rSG[qb])
                        return f
                    its = [("mm", mm(k0, min(8, k0 + per)), None) for k0 in range(0, 8, per)]
                    if isg:
                        def rc(q):
                            def f():
                                ta = g8["ta"]
                                sl_ = slice(q * 128, (q + 1) * 128)
                                sc.op("dve", lambda e: e.reciprocal(out=TA[ta][:, sl_], in_=TA[ta][:, sl_]),
                                      reads=[rTA[ta]], writes=[rTA[ta]])
                            return f

                        def fin():
                            ta = g8["ta"]
                            sgv = SG[qb][:, st * 512:(st + 1) * 512]
                            sc.op("dve", lambda e: e.tensor_tensor(out=sgv, in0=sgv, in1=TA[ta][:, :], op=ALU.mult),
                                  reads=[rTA[ta]] + rSG[qb], writes=rSG[qb])
                        rdy = lambda: "ta" in g8
                        its += [("ch", rc(q), rdy) for q in range(4)] + [("ch", fin, rdy)]
                    return its
                items = []
                for st in range(2):
                    items += group(st, False)
                for st in range(2):
                    items += group(st, True)
                return items

            steps = []
            for hd in range(8):
                for c2 in range(2):
                    c = hf * 2 + c2
                    nkb = 4 * c + 4
                    prev = None
                    for kb in range(nkb - 1, -1, -1):
                        s = dict(hd=hd, c2=c2, c=c, kb=kb, first=(kb == nkb - 1), last=(kb == 0), idx=nkb - 1 - kb,
                                 prev=prev, nxt_last=(kb == 1))
                        steps.append(s)
                        prev = s
            n = len(steps)
            st_ = {"lsb": 0}

            def geom(s):
                c0 = max(0, s["kb"] - 4 * s["c"]) * 128
                return c0, slice(c0, NQ), slice(s["c2"] * 512 + c0, s["c2"] * 512 + NQ)

            def emit_Z(s):
                c0, cs, qs = geom(s)
                hd, kb, qb = s["hd"], s["kb"], s["hd"] % 2
                zb = bank()
                s["zb"] = zb
                Z = PS[zb]
                diag = kb >= 4 * s["c"]
                sc.op("pe", lambda e: e.matmul(Z[:, cs], lhsT=KT3[:, hd, kb * 128:(kb + 1) * 128], rhs=QT[qb][:, qs],
                                                start=True, stop=not diag, skip_group_check=True),
                      reads=[rKT[hd], rQT[qb]], writes=[rPS[zb]], signal=not diag)
                if diag:
                    sc.op("pe", lambda e: e.matmul(Z[:, c0:c0 + 128], lhsT=IDENT, rhs=NEGMASK, start=False, stop=True,
                                                    skip_group_check=True),
                          reads=[rCST], writes=[rPS[zb]], signal=True)

            def emit_ELP(s):
                c0, cs, qs = geom(s)
                zb = s["zb"]
                lb = nxt("lp", 3)
                s["lb"] = lb
                sc.op("act", lambda e: e.activation(out=PS[7][:, cs], in_=PS[zb][:, cs], func=AF.Exp),
                      reads=[rPS[zb]], writes=[rPS[7]])
                sc.op("act", lambda e: e.activation(out=LP[lb][:, cs], in_=PS[7][:, cs], func=AF.Ln, bias=1.0),
                      reads=[rPS[7]], writes=[rLP[lb]])

            def emit_TriOnes(s):
                c0, cs, qs = geom(s)
                zb, lb = s["zb"], s["lb"]
                Z = PS[zb]
                sc.op("pe", lambda e: e.matmul(Z[:, cs], lhsT=NEGTRI, rhs=LP[lb][:, cs], start=False, stop=s["first"],
                                                skip_group_check=True),
                      reads=[rCST, rLP[lb]], writes=[rPS[zb]], signal=s["first"])
                if not s["first"]:
                    k_ = s["prev"]["lsb_out"]
                    sc.op("pe", lambda e: e.matmul(Z[:, cs], lhsT=NEGONES, rhs=LSUMB[k_][:, cs], start=False, stop=True,
                                                    skip_group_check=True),
                          reads=[rCST, rLSB[k_]], writes=[rPS[zb]], signal=True)

            def emit_AT(s):
                c0, cs, qs = geom(s)
                zb = s["zb"]
                ab = nxt("at", 2)
                s["ab"] = ab
                sc.op("act", lambda e: e.activation(out=AT[ab][:, cs], in_=PS[zb][:, cs], func=AF.Exp),
                      reads=[rPS[zb]], writes=[rAT[ab]])

            def emit_LSUM(s):
                if s["last"]:
                    return
                c0, cs, qs = geom(s)
                lb = s["lb"]
                if s["first"]:
                    sc.op("dve", lambda e: e.memset(LSUM[:, :], 0.0), writes=[rLS])
                sc.op("dve", lambda e: e.tensor_tensor(out=LSUM[:, cs], in0=LSUM[:, cs], in1=LP[lb][:, cs], op=ALU.add),
                      reads=[rLS, rLP[lb]], writes=[rLS])
                k_ = 1 - st_["lsb"]
                sc.op("dve", lambda e: e.tensor_copy(out=LSUMB[k_][:, :], in_=LSUM[:, :]), reads=[rLS], writes=[rLSB[k_]])
                st_["lsb"] = k_
                s["lsb_out"] = k_

            def emit_AV(s):
                c0, cs, qs = geom(s)
                hd, kb, c2, qb = s["hd"], s["kb"], s["c2"], s["hd"] % 2
                ob = 4 + c2
                ab = s["ab"]
                sc.op("pe", lambda e: e.matmul(PS[ob][:, cs], lhsT=V3[:, kb, hd * 128:(hd + 1) * 128], rhs=AT[ab][:, cs],
                                                start=s["first"], stop=s["last"], skip_group_check=True),
                      reads=[rV[kb], rAT[ab]], writes=[rPS[ob]], signal=s["last"])
                if s["last"]:
                    sc.op("dve", lambda e: e.tensor_tensor(out=OGT3[:, hd, c2 * 512:(c2 + 1) * 512], in0=PS[ob][:, :],
                                                            in1=SG[qb][:, c2 * 512:(c2 + 1) * 512], op=ALU.mult),
                          reads=[rPS[ob]] + rSG[qb], writes=[rOGT[hd]])

            bg_mm = []
            bg_ch = []

            def bg_add(items):
                for kind, f, rdy in items:
                    (bg_mm if kind == "mm" else bg_ch).append((f, rdy))

            def bg_flush():
                while bg_mm:
                    bg_mm.pop(0)[0]()
                while bg_ch:
                    bg_ch.pop(0)[0]()

            def bg_step():
                if bg_mm:
                    bg_mm.pop(0)[0]()
                if bg_ch and (bg_ch[0][1] is None or bg_ch[0][1]()):
                    bg_ch.pop(0)[0]()

            bg_add(proj_items(0, 8))
            bg_flush()
            emit_Z(steps[0])
            emit_ELP(steps[0])
            emit_LSUM(steps[0])
            for i in range(n):
                s = steps[i]
                if s["first"] and s["c2"] == 0 and s["hd"] + 1 < 8:
                    bg_add(proj_items(s["hd"] + 1, 2 if hf == 1 else 4))
                if i + 1 < n:
                    s1 = steps[i + 1]
                    if s1["first"] and s1["c2"] == 0:
                        bg_flush()
                    emit_Z(s1)
                emit_TriOnes(s)
                if i + 1 < n:
                    emit_ELP(steps[i + 1])
                emit_AT(s)
                if i + 1 < n:
                    emit_LSUM(steps[i + 1])
                if i >= 1:
                    emit_AV(steps[i - 1])
                bg_step()
            emit_AV(steps[n - 1])
            bank_set[0] = list(range(8))
            ple_load_pt(l, hf)
            for dc in range(2):
                ring_begin()
                so = [ring_next(widx["BO", l] + dc * 2 + kh) for kh in range(2)]
                ring_prefetch(2)
                for ii in range(8):
                    i = hf * 8 + ii
                    if dc == 1 and "p" in OPT:
                        ple_prep_lo(hf, ii)
                    b = bank()
                    for k in range(8):
                        s_ = so[k // 4]
                        sc.op("pe", lambda e: e.matmul(PS[b][:, :], lhsT=OGT3[:, k, ii * 128:(ii + 1) * 128],
                                                        rhs=slot3(s_, 4)[:, k % 4, :], start=(k == 0), stop=(k == 7)),
                              reads=[rOGT[k], rSLOT[s_]], writes=[rPS[b]], signal=(k == 7))
                    hv = H3[:, i, dc * 512:(dc + 1) * 512]
                    sc.op("dve", lambda e: e.tensor_tensor(out=hv, in0=hv, in1=PS[b][:, :], op=ALU.add),
                          reads=[rPS[b], rH[i]], writes=[rH[i]])
                    if dc == 1 and "p" in OPT:
                        if ii >= 1:
                            ple_prep_hi(hf, ii - 1)
                        if ii == 7:
                            ple_prep_hi(hf, 7)
        phase_ple(l, hf)

    def final_half(hf):
        norm_stats(hf)
        sc.dma("sp", GB[:, :], vecs_d[:, VOFF["fing"]:VOFF["fing"] + 1024], "gb", writes=rGBL)
        for ii in range(8):
            i = hf * 8 + ii
            sc.op("dve", lambda e: e.scalar_tensor_tensor(out=H3[:, i, :], in0=H3[:, i, :], scalar=RS[:, i:i + 1], in1=GB[:, :],
                                                           op0=ALU.mult, op1=ALU.mult), reads=[rH[i], rRS] + rGBL, writes=[rH[i]])
            sc.dma("sp", y_d[i * 128:(i + 1) * 128, :], H3[:, i, :], "y", reads=[rH[i]], writes=[rY[i]])
        final_done[hf] = True

    def phase_final():
        for hf in range(2):
            if not final_done[hf]:
                final_half(hf)

    def store_raw():
        for i in range(16):
            sc.dma("sp", y_d[i * 128:(i + 1) * 128, :], H3[:, i, :], "y", reads=[rH[i]], writes=[rY[i]])

    next_norm_of = {}
    for l in layers:
        next_norm_of[(l, 0)] = (VOFF[f"ng{l}"], 1)
        if l == 1 and (2 in layers or 3 in layers):
            next_norm_of[(l, 1)] = (VOFF["kvg"], 0)
        elif (l + 1) in layers and l != 1:
            next_norm_of[(l, 1)] = (VOFF[f"ng{l + 1}"], 0)
    plan()
    for l in layers:
        for hf in range(2):
            if l < 2:
                phase_A(l, hf)
            else:
                phase_B(l, hf)
        if l == 1 and (2 in layers or 3 in layers):
            for hf in range(2):
                phase_kv(hf)
    if final:
        phase_final()
    else:
        store_raw()
    sc.wait_all("sp", rY)
    sc.wait_all("pool", [rDBG])
    assert ring_state["pos"] == len(seq), (ring_state, len(seq))
    return nc, es


def make_in_maps(inputs):
    f = lambda a: np.asarray(a, dtype=np.float32)
    x = f(inputs["x"]); p = f(inputs["p"])
    wts, _ = pack_weights(f(inputs["a_w_in"]), f(inputs["a_w_out"]), f(inputs["w_kv"]), f(inputs["b_w_in"]),
                          f(inputs["b_w_out"]), f(inputs["ple_w"]), f(inputs["ple_gate_w"]))
    vecs = pack_vecs(f(inputs["norm_g"]), f(inputs["kv_norm_g"]), f(inputs["final_g"]), f(inputs["a_ln_g"]),
                     f(inputs["a_ln_b"]), f(inputs["a_b_s"]))
    wst = np.ascontiguousarray(f(inputs["a_w_s"]).transpose(0, 3, 1, 2)).reshape(2 * 128, 1024)
    cst = make_consts()
    maps = []
    for b in range(8):
        pT = np.ascontiguousarray(p[:, b].reshape(4, S, 2, 128).transpose(0, 3, 2, 1)).reshape(4 * 128, 2 * S)
        maps.append({"x": np.ascontiguousarray(x[b]), "pT": pT, "wts": wts, "vecs": vecs, "wst": wst, "cst": cst})
    return maps


def kernel(**inputs):
    nc, es = build()
    maps = make_in_maps(inputs)
    res = run_bass_kernel_spmd(nc, maps, core_ids=list(range(8)))
    return np.stack([np.asarray(r["y"], dtype=np.float32) for r in res.results], axis=0)
```
